# Optimizing a Trainium2 kernel written in Bass

```python
import math
import jax, jax.numpy as jnp
from jax import lax
import numpy as np

D_MODEL = 2048
BATCH = 2
SEQ = 8192
DEPTH = 2

N_MEM = 256
EXPAND = 2
D_MIX = EXPAND * D_MODEL
N_GROUPS = 4
D_GROUP = D_MIX // N_GROUPS
CONV_WIDTH = 31
SHORT_CONV_WIDTH = 3
HYENA_ORDER = 2
HYENA_DIRS = 2
HYENA_BANDS = 16
HYENA_EMB = 1 + 2 * HYENA_BANDS
HYENA_FFN = 64
HYENA_FAST_DECAY = -math.log(1e-2) / 0.3
HYENA_SLOW_DECAY = -math.log(1e-2) / 1.5
MEM_HEADS = 4
MEM_HEAD_DIM = D_GROUP // MEM_HEADS
N_IN = 2 * D_GROUP + D_GROUP + (HYENA_ORDER + 1) * D_GROUP + D_GROUP + D_MIX
EPS = 1e-6

kernel_name = "hybrid_conv_fourier_hyena_memattn_encoder"


def rms_norm(x, g):
    xf = x.astype(jnp.float32)
    y = xf * lax.rsqrt(jnp.mean(xf * xf, axis=-1, keepdims=True) + EPS)
    return (y * g.astype(jnp.float32)).astype(x.dtype)


def layer_norm(x, g, b):
    xf = x.astype(jnp.float32)
    mu = jnp.mean(xf, axis=-1, keepdims=True)
    var = jnp.mean(jnp.square(xf - mu), axis=-1, keepdims=True)
    y = (xf - mu) * lax.rsqrt(var + EPS)
    return (y * g.astype(jnp.float32) + b.astype(jnp.float32)).astype(x.dtype)


def depthwise_conv(u, k, b):
    y = lax.conv_general_dilated(
        u, k[:, None, :].astype(u.dtype), window_strides=(1,), padding="SAME",
        dimension_numbers=("NWC", "WIO", "NWC"), feature_group_count=u.shape[-1])
    return y + b.astype(u.dtype)


def hyena_filters(positions, fw1, fb1, freq1, fw2, fb2, freq2, fw3, decay):
    f32 = jnp.float32
    L = positions.shape[0]
    pos = positions.astype(f32)
    t = pos / L
    bands = jnp.linspace(1e-4, HYENA_BANDS - 1, HYENA_BANDS, dtype=f32)
    w = 2.0 * jnp.pi * pos / L
    feats = jnp.concatenate([t[:, None], jnp.cos(w[:, None] * bands),
                             jnp.sin(w[:, None] * bands)], axis=-1)
    h = jnp.sin(freq1.astype(f32) * (feats @ fw1.astype(f32) + fb1.astype(f32)))
    h = jnp.sin(freq2.astype(f32) * (h @ fw2.astype(f32) + fb2.astype(f32)))
    h = (h @ fw3.astype(f32)).reshape(L, HYENA_ORDER, HYENA_DIRS, -1)
    h = h * jnp.exp(-t[:, None, None, None] * jnp.abs(decay.astype(f32)))
    fwd, bwd = h[:, :, 0], h[:, :, 1]
    k = jnp.concatenate([fwd, jnp.zeros_like(fwd[:1]), jnp.flip(bwd[1:], axis=0)], axis=0)
    k = k * lax.rsqrt(jnp.sum(k * k, axis=0, keepdims=True) + EPS)
    return jnp.fft.rfft(k, axis=0)


def long_conv(z, kf, skip):
    L = z.shape[1]
    zf = jnp.fft.rfft(z.astype(jnp.float32), n=2 * L, axis=1)
    y = jnp.fft.irfft(zf * kf[None], n=2 * L, axis=1)[:, :L]
    return (y + skip.astype(jnp.float32) * z.astype(jnp.float32)).astype(z.dtype)


def hybrid_layer(x, mem_n, positions, pre_g, post_g, w_in, conv_dw_w, conv_dw_b, conv_ln_g,
                 conv_ln_b, conv_pw_w, conv_pw_b, fnet_w, fnet_b, hy_short_w, hy_short_b,
                 hy_fw1, hy_fb1, hy_freq1, hy_fw2, hy_fb2, hy_freq2, hy_fw3, hy_decay, hy_skip,
                 mem_wk, mem_wv, group_g, w_out):
    B, L, _ = x.shape
    G = D_GROUP
    h = rms_norm(x, pre_g)
    proj = h @ w_in.astype(x.dtype)
    a, f, hy, q, gate = jnp.split(proj, [2 * G, 3 * G, 6 * G, 7 * G], axis=-1)

    a_val, a_gate = jnp.split(a, 2, axis=-1)
    u = a_val * jax.nn.sigmoid(a_gate)
    u = depthwise_conv(u, conv_dw_w, conv_dw_b)
    u = jax.nn.silu(layer_norm(u, conv_ln_g, conv_ln_b))
    y_a = u @ conv_pw_w.astype(x.dtype) + conv_pw_b.astype(x.dtype)

    ff = jnp.fft.fft2(f.astype(jnp.float32), axes=(1, 2), norm="ortho").real.astype(x.dtype)
    y_b = ff @ fnet_w.astype(x.dtype) + fnet_b.astype(x.dtype)

    hy = depthwise_conv(hy, hy_short_w, hy_short_b)
    v, x1, x2 = jnp.split(hy, 3, axis=-1)
    kf = hyena_filters(positions, hy_fw1, hy_fb1, hy_freq1, hy_fw2, hy_fb2, hy_freq2,
                       hy_fw3, hy_decay)
    z = v
    for o, g_o in enumerate((x1, x2)):
        z = g_o * long_conv(z, kf[:, o], hy_skip[o])
    y_c = z

    qh = q.reshape(B, L, MEM_HEADS, MEM_HEAD_DIM)
    kh = (mem_n @ mem_wk.astype(x.dtype)).reshape(B, -1, MEM_HEADS, MEM_HEAD_DIM)
    vh = (mem_n @ mem_wv.astype(x.dtype)).reshape(B, -1, MEM_HEADS, MEM_HEAD_DIM)
    s = jnp.einsum("bqhd,bkhd->bhqk", qh, kh).astype(jnp.float32) * (MEM_HEAD_DIM ** -0.5)
    p = jax.nn.softmax(s, axis=-1).astype(x.dtype)
    y_m = jnp.einsum("bhqk,bkhd->bqhd", p, vh).reshape(B, L, G)

    y = jnp.concatenate([y_a, y_b, y_c, y_m], axis=-1).reshape(B, L, N_GROUPS, G)
    y = rms_norm(y, group_g.reshape(N_GROUPS, G)).reshape(B, L, D_MIX)
    y = y * jax.nn.silu(gate)
    out = y @ w_out.astype(x.dtype)
    return x + rms_norm(out, post_g)


def setup_inputs(seed: int = 0) -> dict:
    key = jax.random.key(seed)
    ks = iter(jax.random.split(key, 40))
    f32 = jnp.float32

    def nrm(shape, scale):
        return scale * jax.random.normal(next(ks), shape, f32)

    def gain(shape):
        return 1.0 + nrm(shape, 0.05)

    G = D_GROUP
    decay_base = jnp.linspace(HYENA_SLOW_DECAY, HYENA_FAST_DECAY, G, dtype=f32)
    return {
        "x": nrm((BATCH, SEQ, D_MODEL), 1.0),
        "mem": nrm((BATCH, N_MEM, D_MODEL), 1.0),
        "positions": jnp.arange(SEQ, dtype=jnp.int32),
        "mem_norm_g": gain((D_MODEL,)),
        "pre_norm_g": gain((DEPTH, D_MODEL)),
        "post_norm_g": gain((DEPTH, D_MODEL)),
        "w_in": nrm((DEPTH, D_MODEL, N_IN), D_MODEL ** -0.5),
        "conv_dw_w": nrm((DEPTH, CONV_WIDTH, G), CONV_WIDTH ** -0.5),
        "conv_dw_b": nrm((DEPTH, G), 0.02),
        "conv_ln_g": gain((DEPTH, G)),
        "conv_ln_b": nrm((DEPTH, G), 0.02),
        "conv_pw_w": nrm((DEPTH, G, G), G ** -0.5),
        "conv_pw_b": nrm((DEPTH, G), 0.02),
        "fnet_w": nrm((DEPTH, G, G), G ** -0.5),
        "fnet_b": nrm((DEPTH, G), 0.02),
        "hy_short_w": nrm((DEPTH, SHORT_CONV_WIDTH, 3 * G), SHORT_CONV_WIDTH ** -0.5),
        "hy_short_b": nrm((DEPTH, 3 * G), 0.02),
        "hy_fw1": nrm((DEPTH, HYENA_EMB, HYENA_FFN), HYENA_EMB ** -0.5),
        "hy_fb1": nrm((DEPTH, HYENA_FFN), 0.1),
        "hy_freq1": gain((DEPTH, HYENA_FFN)),
        "hy_fw2": nrm((DEPTH, HYENA_FFN, HYENA_FFN), HYENA_FFN ** -0.5),
        "hy_fb2": nrm((DEPTH, HYENA_FFN), 0.1),
        "hy_freq2": gain((DEPTH, HYENA_FFN)),
        "hy_fw3": nrm((DEPTH, HYENA_FFN, HYENA_ORDER * HYENA_DIRS * G), HYENA_FFN ** -0.5),
        "hy_decay": decay_base * (1.0 + nrm((DEPTH, HYENA_ORDER, HYENA_DIRS, G), 0.1)),
        "hy_skip": nrm((DEPTH, HYENA_ORDER, G), 0.5),
        "mem_wk": nrm((DEPTH, D_MODEL, G), D_MODEL ** -0.5),
        "mem_wv": nrm((DEPTH, D_MODEL, G), D_MODEL ** -0.5),
        "group_norm_g": gain((DEPTH, D_MIX)),
        "w_out": nrm((DEPTH, D_MIX, D_MODEL), D_MIX ** -0.5),
    }


def reference(x, mem, positions, mem_norm_g, pre_norm_g, post_norm_g, w_in, conv_dw_w,
              conv_dw_b, conv_ln_g, conv_ln_b, conv_pw_w, conv_pw_b, fnet_w, fnet_b,
              hy_short_w, hy_short_b, hy_fw1, hy_fb1, hy_freq1, hy_fw2, hy_fb2, hy_freq2,
              hy_fw3, hy_decay, hy_skip, mem_wk, mem_wv, group_norm_g, w_out):
    mem_n = rms_norm(mem, mem_norm_g).astype(x.dtype)
    for l in range(DEPTH):
        x = hybrid_layer(x, mem_n, positions, pre_norm_g[l], post_norm_g[l], w_in[l],
                         conv_dw_w[l], conv_dw_b[l], conv_ln_g[l], conv_ln_b[l], conv_pw_w[l],
                         conv_pw_b[l], fnet_w[l], fnet_b[l], hy_short_w[l], hy_short_b[l],
                         hy_fw1[l], hy_fb1[l], hy_freq1[l], hy_fw2[l], hy_fb2[l], hy_freq2[l],
                         hy_fw3[l], hy_decay[l], hy_skip[l], mem_wk[l], mem_wv[l],
                         group_norm_g[l], w_out[l])
    return x
```

```python
import math
import ml_dtypes
import numpy as np
import concourse.bass as bass
import concourse.mybir as mybir
from concourse.bass_utils import run_bass_kernel_spmd

F32 = mybir.dt.float32
BF16 = mybir.dt.bfloat16
I32 = mybir.dt.int32
AF = mybir.ActivationFunctionType
ALU = mybir.AluOpType
AX = mybir.AxisListType


class Buf:
    __slots__ = ("name", "w", "r")

    def __init__(self, name=""):
        self.name = name
        self.w = None
        self.r = {}


class Sched:
    SEM_LIMIT = 30000

    def __init__(self, nc, n_dma_sems=40, same_engine_sync=True):
        self.nc = nc
        self.engs = {"pe": nc.tensor, "act": nc.scalar, "dve": nc.vector,
                     "pool": nc.gpsimd, "sp": nc.sync}
        self.same_engine_sync = same_engine_sync
        self.sems = {}
        self.cur = {}
        self.nsem = 0
        for e in self.engs:
            self._new_eng_sem(e)
        self.dma_sems = []
        for i in range(n_dma_sems):
            k = ("dma", i)
            self.sems[k] = nc.alloc_semaphore(f"dq{i}")
            self.dma_sems.append([k, 0])
        self.dma_rr = 0
        self.waited = {e: {} for e in self.engs}
        self.n_inst = {e: 0 for e in self.engs}
        self.n_wait = {e: 0 for e in self.engs}

    def _new_eng_sem(self, e):
        k = (e, self.nsem)
        self.nsem += 1
        self.sems[k] = self.nc.alloc_semaphore(f"s_{e}_{k[1]}")
        self.cur[e] = [k, 0]

    def _wait(self, e, tok):
        if tok is None:
            return
        k, v = tok
        if not self.same_engine_sync and k[0] == e:
            return
        if e == "pe" and k[0] == "pe":
            return
        if self.waited[e].get(k, 0) >= v:
            return
        self.engs[e].wait_ge(self.sems[k], v)
        self.waited[e][k] = v
        self.n_wait[e] += 1

    def _deps(self, e, reads, writes):
        for b in reads:
            self._wait(e, b.w)
        for b in writes:
            self._wait(e, b.w)
            for k, v in b.r.items():
                self._wait(e, (k, v))

    def _mark(self, tok, reads, writes):
        k, v = tok
        for b in reads:
            if b.r.get(k, 0) < v:
                b.r[k] = v
        for b in writes:
            b.w = tok
            b.r = {}

    def op(self, e, fn, reads=(), writes=()):
        self._deps(e, reads, writes)
        ins = fn(self.engs[e])
        c = self.cur[e]
        c[1] += 1
        ins.then_inc(self.sems[c[0]], 1)
        tok = (c[0], c[1])
        self._mark(tok, reads, writes)
        self.n_inst[e] += 1
        if c[1] >= self.SEM_LIMIT:
            self._new_eng_sem(e)
        return tok

    def dma(self, q, out, in_, reads=(), writes=(), **kw):
        self._deps(q, reads, writes)
        slot = self.dma_sems[self.dma_rr]
        self.dma_rr = (self.dma_rr + 1) % len(self.dma_sems)
        k, uses = slot
        if uses > 0:
            self._wait(q, (k, 16 * uses))
        ins = self.engs[q].dma_start(out=out, in_=in_, **kw)
        slot[1] = uses + 1
        ins.then_inc(self.sems[k], 16)
        tok = (k, 16 * (uses + 1))
        self._mark(tok, reads, writes)
        self.n_inst[q] += 1
        return tok

    def finish(self, bufs, e="sp"):
        for b in bufs:
            self._wait(e, b.w)

D = 2048
NIN = 11264
G = 1024
TB = 1024
H = 16
TBH = TB + 2 * H
NB = 2
EPS = 1e-6
C_AVAL, C_AGATE, C_F, C_HY, C_Q, C_GATE = 0, 1024, 2048, 3072, 6144, 7168


class Pool_:
    def __init__(self, nc, name, n, shape, dtype, psum=False):
        self.t = []
        for i in range(n):
            if psum:
                h = nc.alloc_psum_tensor(f"{name}{i}", shape, dtype)
            else:
                h = nc.alloc_sbuf_tensor(f"{name}{i}", shape, dtype)
            self.t.append((h, Buf(f"{name}{i}")))
        self.i = 0

    def get(self):
        r = self.t[self.i]
        self.i = (self.i + 1) % len(self.t)
        return r


def build_L1(nc):
    S = Sched(nc)
    dt_in = lambda name, shape, dt=F32: nc.dram_tensor(name, shape, dt, kind="ExternalInput").ap()
    dt_out = lambda name, shape, dt=BF16: nc.dram_tensor(name, shape, dt, kind="ExternalOutput").ap()
    xp = dt_in("xp", [NB, TBH, D])
    w_in = dt_in("w_in", [D, NIN])
    pre_g_bc = dt_in("pre_g_bc", [128, D])
    cw = dt_in("conv_w", [128, 8, 31])
    cvec = dt_in("conv_vec", [128, 5, 8])
    pw = dt_in("conv_pw_w", [G, G])
    hyw = dt_in("hy_w", [128, 24, 4])
    memx = dt_in("mem", [256, D])
    mem_g_bc = dt_in("mem_g_bc", [128, D])
    wk = dt_in("mem_wk", [D, G])
    wv = dt_in("mem_wv", [D, G])
    gD = dt_in("gD", [128, 8])
    dftG = dt_in("dftG", [G, 2 * G])
    o_yA = dt_out("yAg", [G, NB * TB])
    o_yM = dt_out("yMg", [G, NB * TB])
    o_sgB = dt_out("sgB", [G, NB * TB])
    o_sgC = dt_out("sgC", [G, NB * TB])
    o_Z = dt_out("Z", [2 * G, NB * TB])
    o_hy = dt_out("hyc", [3 * G, NB * TB])
    outs_b = []

    sb = lambda name, shape, dt=F32: nc.alloc_sbuf_tensor(name, shape, dt)
    ident = sb("ident", [128, 128], BF16); b_ident = Buf()
    ones_f = sb("ones_f", [128, 128], F32); b_ones_f = Buf()
    ones_b = sb("ones_b", [128, 128], BF16); b_ones_b = Buf()
    S.op("pool", lambda e: e.memset(ident[:], 1.0), writes=[b_ident])
    S.op("pool", lambda e: e.affine_select(ident[:], ident[:], pattern=[[-1, 128]], compare_op=ALU.is_equal,
                                           fill=0.0, base=0, channel_multiplier=1), reads=[b_ident], writes=[b_ident])
    S.op("pool", lambda e: e.memset(ones_f[:], 1.0), writes=[b_ones_f])
    S.op("pool", lambda e: e.memset(ones_b[:], 1.0), writes=[b_ones_b])
    g_bc = sb("g_bc", [128, D]); b_gbc = Buf()
    cw_s = sb("cw_s", [128, 8, 31]); cvec_s = sb("cvec_s", [128, 5, 8]); hyw_s = sb("hyw_s", [128, 24, 4])
    gD_s = sb("gD_s", [128, 8])
    b_par = Buf()
    S.dma("sp", cw_s[:], cw, writes=[b_par])
    S.dma("sp", cvec_s[:], cvec, writes=[b_par])
    S.dma("sp", hyw_s[:], hyw, writes=[b_par])
    S.dma("sp", gD_s[:], gD, writes=[b_par])

    psum = Pool_(nc, "ps", 8, [128, 512], F32, psum=True)
    wst = Pool_(nc, "wst", 2, [128, 16, 128], F32)
    wbf = Pool_(nc, "wbf", 3, [128, 16, 128], BF16)
    xs_p = Pool_(nc, "xs", 2, [128, D], F32)
    hb_p = Pool_(nc, "hb", 2, [128, D], BF16)
    st_p = Pool_(nc, "st", 4, [128, 4], F32)
    hT = sb("hT", [128, 16, TBH], BF16); b_hT = Buf()

    pending = []
    tick = [0]

    def flush(delay):
        while pending and pending[0][0] + delay <= tick[0]:
            _, dst, ob, bob = pending.pop(0)
            b = Buf()
            outs_b.append(b)
            S.dma("sp", dst, ob[:], reads=[bob], writes=[b])

    def out_dma(dst, ob, bob):
        pending.append((tick[0], dst, ob, bob))

    def stream_w(dram, c0, nk):
        tick[0] += 1
        flush(2)
        st, bst = wst.get()
        S.dma("sp", st[:, 0:nk, :], dram.rearrange("(k p) c -> p k c", p=128)[:, :, c0:c0 + 128], writes=[bst])
        wb, bwb = wbf.get()
        S.op("pool", lambda e: e.tensor_copy(wb[:, 0:nk, :], st[:, 0:nk, :]), reads=[bst], writes=[bwb])
        return wb, bwb

    def rms_rows(xs, bxs, rows, gtile, bg, out_bf, bout):
        stt, bstt = st_p.get()
        S.op("dve", lambda e: e.memset(stt[:], 0.0), writes=[bstt])
        S.op("act", lambda e: e.activation(out_bf[:rows], xs[:rows], AF.Square, accum_out=stt[:rows, 0:1]),
             reads=[bxs, bstt], writes=[bout, bstt])
        S.op("dve", lambda e: e.tensor_scalar(stt[:rows, 1:2], stt[:rows, 0:1], 1.0 / D, EPS, ALU.mult, ALU.add),
             reads=[bstt], writes=[bstt])
        S.op("dve", lambda e: e.reciprocal(stt[:rows, 2:3], stt[:rows, 1:2]), reads=[bstt], writes=[bstt])
        S.op("act", lambda e: e.activation(stt[:rows, 2:3], stt[:rows, 2:3], AF.Sqrt), reads=[bstt], writes=[bstt])
        S.op("dve", lambda e: e.scalar_tensor_tensor(out_bf[:rows], xs[:rows], stt[:rows, 2:3], gtile[:rows],
                                                     ALU.mult, ALU.mult), reads=[bxs, bstt, bg], writes=[bout])

    def transpose_into(src_bf, bsrc, rows, dstT, bdst, col0):
        for half in range(2):
            pt, bpt = psum.get()
            ptb = pt[:].bitcast(BF16)
            for kk in range(8):
                k = half * 8 + kk
                S.op("pe", lambda e, k=k, kk=kk: e.transpose(ptb[:, kk * 128:kk * 128 + rows],
                                                             src_bf[:rows, k * 128:(k + 1) * 128], ident[:rows, :rows]),
                     reads=[bsrc, b_ident], writes=[bpt])
            S.op("act", lambda e: e.copy(dstT[:, half * 8:half * 8 + 8, col0:col0 + rows],
                                         ptb.rearrange("p (k t) -> p k t", k=8)[:, :, 0:rows]),
                 reads=[bpt], writes=[bdst])

    cvb = sb("cvb", [128, 8, TB], BF16); b_cv = [Buf() for _ in range(8)]
    memT = cvb[:, 0:4, :].rearrange("p a (b m) -> p (a b) m", m=256); b_memT = Buf()
    mg_bc, b_mg = g_bc, b_gbc
    S.dma("sp", mg_bc[:], mem_g_bc, writes=[b_mg])
    for i in range(2):
        xs, bxs = xs_p.get()
        S.dma("sp", xs[:], memx[i * 128:(i + 1) * 128, :], writes=[bxs])
        hb, bhb = hb_p.get()
        rms_rows(xs, bxs, 128, mg_bc, b_mg, hb, bhb)
        transpose_into(hb, bhb, 128, memT, b_memT, i * 128)
    kT = sb("kT", [128, 8, 256], BF16); b_kT = Buf()
    vS = sb("vS", [128, 2, G], BF16); b_vS = Buf()
    for j in range(8):
        wb, bwb = stream_w(wk, j * 128, 16)
        pt, bpt = psum.get()
        for k in range(16):
            S.op("pe", lambda e, k=k: e.matmul(pt[:, 0:256], lhsT=wb[:, k, :], rhs=memT[:, k, :], start=(k == 0), stop=(k == 15)),
                 reads=[bwb, b_memT], writes=[bpt])
        S.op("act", lambda e: e.copy(kT[:, j, :], pt[:, 0:256]), reads=[bpt], writes=[b_kT])
    for j in range(8):
        wb, bwb = stream_w(wv, j * 128, 16)
        pt, bpt = psum.get()
        for m in range(2):
            for k in range(16):
                S.op("pe", lambda e, k=k, m=m: e.matmul(pt[:, m * 128:(m + 1) * 128], lhsT=memT[:, k, m * 128:(m + 1) * 128],
                                                        rhs=wb[:, k, :], start=(k == 0), stop=(k == 15)),
                     reads=[bwb, b_memT], writes=[bpt])
        S.op("act", lambda e: e.copy(vS[:, :, j * 128:(j + 1) * 128], pt[:, 0:256].rearrange("p (m c) -> p m c", m=2)),
             reads=[bpt], writes=[b_vS])

    S.dma("sp", g_bc[:], pre_g_bc, reads=[], writes=[b_gbc])
    yAb = sb("yAb", [128, 8, TB], BF16); b_yA = [Buf() for _ in range(8)]
    fT = sb("fT", [128, 8, TB], BF16); b_fT = Buf()
    s1 = sb("s1", [128, TB]); b_s1 = Buf()
    s2 = sb("s2", [128, TB]); b_s2 = Buf()
    u_p = Pool_(nc, "u", 2, [128, TBH], F32)
    sig_p = Pool_(nc, "sig", 1, [128, TBH], F32)
    acc_p = Pool_(nc, "acc", 2, [128, TB], F32)
    sq_p = Pool_(nc, "sq", 2, [128, TB], F32)
    ob_p = Pool_(nc, "ob", 4, [128, TB], BF16)
    eT_p = Pool_(nc, "eT", 2, [128, 2, 512], BF16)
    rden_p = Pool_(nc, "rden", 2, [128, 512], F32)

    def inproj(col0, halo):
        wb, bwb = stream_w(w_in, col0, 16)
        res = []
        if halo:
            chunks = [(i * 352, 352) for i in range(3)]
        else:
            chunks = [(H + i * 512, 512) for i in range(2)]
        for (t0, n) in chunks:
            pt, bpt = psum.get()
            for k in range(16):
                S.op("pe", lambda e, k=k, t0=t0, n=n, pt=pt: e.matmul(pt[:, 0:n], lhsT=wb[:, k, :], rhs=hT[:, k, t0:t0 + n],
                                                                       start=(k == 0), stop=(k == 15)),
                     reads=[bwb, b_hT], writes=[bpt])
            res.append((pt, bpt, t0, n))
        return res

    def colsum_acc(src, bsrc, acc, bacc, first):
        for hh in range(2):
            pt, bpt = psum.get()
            S.op("pe", lambda e: e.matmul(pt[:], lhsT=ones_f[:], rhs=src[:, hh * 512:(hh + 1) * 512], start=True, stop=True),
                 reads=[b_ones_f, bsrc], writes=[bpt])
            if first:
                S.op("dve", lambda e: e.tensor_copy(acc[:, hh * 512:(hh + 1) * 512], pt[:]), reads=[bpt], writes=[bacc])
            else:
                S.op("dve", lambda e: e.tensor_tensor(acc[:, hh * 512:(hh + 1) * 512], acc[:, hh * 512:(hh + 1) * 512], pt[:], ALU.add),
                     reads=[bpt, bacc], writes=[bacc])

    def rstd_from(acc, bacc, n):
        S.op("dve", lambda e: e.tensor_scalar(acc[:], acc[:], 1.0 / n, EPS, ALU.mult, ALU.add), reads=[bacc], writes=[bacc])
        S.op("dve", lambda e: e.reciprocal(acc[:], acc[:]), reads=[bacc], writes=[bacc])
        S.op("act", lambda e: e.activation(acc[:], acc[:], AF.Sqrt), reads=[bacc], writes=[bacc])

    def gated_out(ysrc, bys, j, gcol, rstd, brstd, gate_col0, odram, blk):
        res = inproj(gate_col0 + j * 128, False)
        ob, bob = ob_p.get()
        sg, bsg = sq_p.get()
        for (pt, bpt, t0, n) in res:
            o = t0 - H
            S.op("act", lambda e, pt=pt, o=o: e.activation(sg[:, o:o + 512], pt[:], AF.Silu), reads=[bpt], writes=[bsg])
        tmp, btmp = acc_p.get()
        S.op("dve", lambda e: e.scalar_tensor_tensor(tmp[:], ysrc, gcol, rstd[:], ALU.mult, ALU.mult),
             reads=[bys, brstd, b_par], writes=[btmp])
        S.op("dve", lambda e: e.tensor_tensor(ob[:], tmp[:], sg[:], ALU.mult), reads=[btmp, bsg], writes=[bob])
        out_dma(odram[j * 128:(j + 1) * 128, blk * TB:(blk + 1) * TB], ob, bob)

    for blk in range(NB):
        for i in range(9):
            rows = 128 if i < 8 else TBH - 8 * 128
            xs, bxs = xs_p.get()
            S.dma("sp", xs[:rows], xp[blk, i * 128:i * 128 + rows, :], writes=[bxs])
            hb, bhb = hb_p.get()
            rms_rows(xs, bxs, rows, g_bc, b_gbc, hb, bhb)
            transpose_into(hb, bhb, rows, hT, b_hT, i * 128)

        for j in range(24):
            res = inproj(C_HY + j * 128, True)
            ob, bob = ob_p.get()
            tmp, btmp = acc_p.get()
            u, bu = u_p.get()
            for (pt, bpt, t0, n) in res:
                S.op("act", lambda e, pt=pt, t0=t0, n=n: e.copy(u[:, t0:t0 + n], pt[:, 0:n]), reads=[bpt], writes=[bu])
            S.op("dve", lambda e: e.tensor_scalar(tmp[:], u[:, H - 1:H - 1 + TB], hyw_s[:, j, 0:1], hyw_s[:, j, 3:4], ALU.mult, ALU.add),
                 reads=[bu, b_par], writes=[btmp])
            S.op("dve", lambda e: e.scalar_tensor_tensor(tmp[:], u[:, H:H + TB], hyw_s[:, j, 1:2], tmp[:], ALU.mult, ALU.add),
                 reads=[bu, b_par, btmp], writes=[btmp])
            S.op("dve", lambda e: e.scalar_tensor_tensor(ob[:], u[:, H + 1:H + 1 + TB], hyw_s[:, j, 2:3], tmp[:], ALU.mult, ALU.add),
                 reads=[bu, b_par, btmp], writes=[bob])
            out_dma(o_hy[j * 128:(j + 1) * 128, blk * TB:(blk + 1) * TB], ob, bob)

        for gi, odram in ((1, o_sgB), (2, o_sgC)):
            for j in range(8):
                res = inproj(C_GATE + gi * G + j * 128, False)
                ob, bob = ob_p.get()
                for (pt, bpt, t0, n) in res:
                    o = t0 - H
                    S.op("act", lambda e, pt=pt, o=o: e.activation(ob[:, o:o + 512], pt[:], AF.Silu), reads=[bpt], writes=[bob])
                out_dma(odram[j * 128:(j + 1) * 128, blk * TB:(blk + 1) * TB], ob, bob)

        for j in range(8):
            res = inproj(C_F + j * 128, False)
            for (pt, bpt, t0, n) in res:
                o = t0 - H
                S.op("act", lambda e, pt=pt, o=o: e.copy(fT[:, j, o:o + 512], pt[:]), reads=[bpt], writes=[b_fT])
        for j in range(16):
            wb, bwb = stream_w(dftG, j * 128, 8)
            ob, bob = ob_p.get()
            for hh in range(2):
                pt, bpt = psum.get()
                for k in range(8):
                    S.op("pe", lambda e, k=k: e.matmul(pt[:], lhsT=wb[:, k, :], rhs=fT[:, k, hh * 512:(hh + 1) * 512],
                                                       start=(k == 0), stop=(k == 7)), reads=[bwb, b_fT], writes=[bpt])
                S.op("act", lambda e: e.copy(ob[:, hh * 512:(hh + 1) * 512], pt[:]), reads=[bpt], writes=[bob])
            out_dma(o_Z[j * 128:(j + 1) * 128, blk * TB:(blk + 1) * TB], ob, bob)

        for i in range(8):
            resg = inproj(C_AGATE + i * 128, True)
            sig, bsig = sig_p.get()
            for (pt, bpt, t0, n) in resg:
                S.op("act", lambda e, pt=pt, t0=t0, n=n: e.activation(sig[:, t0:t0 + n], pt[:, 0:n], AF.Sigmoid), reads=[bpt], writes=[bsig])
            resv = inproj(C_AVAL + i * 128, True)
            u, bu = u_p.get()
            for (pt, bpt, t0, n) in resv:
                S.op("dve", lambda e, pt=pt, t0=t0, n=n: e.tensor_tensor(u[:, t0:t0 + n], pt[:, 0:n], sig[:, t0:t0 + n], ALU.mult),
                     reads=[bpt, bsig], writes=[bu])
            acc, bacc = acc_p.get()
            S.op("dve", lambda e: e.tensor_scalar(acc[:], u[:, H - 15:H - 15 + TB], cw_s[:, i, 0:1], cvec_s[:, 0, i:i + 1], ALU.mult, ALU.add),
                 reads=[bu, b_par], writes=[bacc])
            for tap in range(1, 31):
                S.op("dve", lambda e, tap=tap: e.scalar_tensor_tensor(acc[:], u[:, H - 15 + tap:H - 15 + tap + TB], cw_s[:, i, tap:tap + 1], acc[:],
                                                                      ALU.mult, ALU.add), reads=[bu, b_par, bacc], writes=[bacc])
            sq, bsq = sq_p.get()
            S.op("act", lambda e: e.activation(sq[:], acc[:], AF.Square), reads=[bacc], writes=[bsq])
            S.op("act", lambda e: e.copy(cvb[:, i, :], acc[:]), reads=[bacc], writes=[b_cv[i], b_memT])
            colsum_acc(acc, bacc, s1, b_s1, i == 0)
            colsum_acc(sq, bsq, s2, b_s2, i == 0)
        S.op("dve", lambda e: e.tensor_scalar(s1[:], s1[:], 1.0 / G, None, ALU.mult), reads=[b_s1], writes=[b_s1])
        sq, bsq = sq_p.get()
        S.op("dve", lambda e: e.tensor_tensor(sq[:], s1[:], s1[:], ALU.mult), reads=[b_s1], writes=[bsq])
        S.op("dve", lambda e: e.scalar_tensor_tensor(s2[:], s2[:], 1.0 / G, sq[:], ALU.mult, ALU.subtract), reads=[b_s2, bsq], writes=[b_s2])
        S.op("dve", lambda e: e.tensor_scalar(s2[:], s2[:], EPS, None, ALU.add), reads=[b_s2], writes=[b_s2])
        S.op("dve", lambda e: e.reciprocal(s2[:], s2[:]), reads=[b_s2], writes=[b_s2])
        S.op("act", lambda e: e.activation(s2[:], s2[:], AF.Sqrt), reads=[b_s2], writes=[b_s2])
        for i in range(8):
            tmp, btmp = acc_p.get()
            S.op("dve", lambda e: e.tensor_tensor(tmp[:], cvb[:, i, :], s1[:], ALU.subtract), reads=[b_cv[i], b_s1], writes=[btmp])
            S.op("dve", lambda e: e.tensor_tensor(tmp[:], tmp[:], s2[:], ALU.mult), reads=[btmp, b_s2], writes=[btmp])
            S.op("act", lambda e: e.activation(cvb[:, i, :], tmp[:], AF.Silu, scale=cvec_s[:, 1, i:i + 1], bias=cvec_s[:, 2, i:i + 1]),
                 reads=[btmp, b_par], writes=[b_cv[i]])
        for j in range(8):
            wb, bwb = stream_w(pw, j * 128, 8)
            ya, bya = acc_p.get()
            for hh in range(2):
                pt, bpt = psum.get()
                for k in range(8):
                    S.op("pe", lambda e, k=k: e.matmul(pt[:], lhsT=wb[:, k, :], rhs=cvb[:, k, hh * 512:(hh + 1) * 512], start=(k == 0), stop=(k == 7)),
                         reads=[bwb, b_cv[k]], writes=[bpt])
                S.op("act", lambda e: e.activation(ya[:, hh * 512:(hh + 1) * 512], pt[:], AF.Identity, bias=cvec_s[:, 3, j:j + 1]),
                     reads=[bpt, b_par], writes=[bya])
            sq, bsq = sq_p.get()
            S.op("act", lambda e: e.activation(sq[:], ya[:], AF.Square), reads=[bya], writes=[bsq])
            S.op("dve", lambda e: e.tensor_copy(yAb[:, j, :], ya[:]), reads=[bya], writes=[b_yA[j]])
            colsum_acc(sq, bsq, s1, b_s1, j == 0)
        rstd_from(s1, b_s1, G)
        for j in range(8):
            gated_out(yAb[:, j, :], b_yA[j], j, cvec_s[:, 4, j:j + 1], s1, b_s1, C_GATE + 0 * G, o_yA, blk)

        for j in range(8):
            res = inproj(C_Q + j * 128, False)
            for (pt, bpt, t0, n) in res:
                o = t0 - H
                S.op("act", lambda e, pt=pt, o=o: e.copy(fT[:, j, o:o + 512], pt[:]), reads=[bpt], writes=[b_fT])
        first = True
        for hd in range(4):
            for hh in range(2):
                eT, beT = eT_p.get()
                for m in range(2):
                    pt, bpt = psum.get()
                    for dc in range(2):
                        S.op("pe", lambda e, dc=dc, m=m: e.matmul(pt[:], lhsT=kT[:, hd * 2 + dc, m * 128:(m + 1) * 128],
                                                                  rhs=fT[:, hd * 2 + dc, hh * 512:(hh + 1) * 512], start=(dc == 0), stop=(dc == 1)),
                             reads=[b_kT, b_fT], writes=[bpt])
                    S.op("act", lambda e, m=m: e.activation(eT[:, m, :], pt[:], AF.Exp, scale=1.0 / 16.0), reads=[bpt], writes=[beT])
                pd, bpd = psum.get()
                for m in range(2):
                    S.op("pe", lambda e, m=m: e.matmul(pd[:], lhsT=ones_b[:], rhs=eT[:, m, :], start=(m == 0), stop=(m == 1)),
                         reads=[b_ones_b, beT], writes=[bpd])
                rd, brd = rden_p.get()
                S.op("dve", lambda e: e.reciprocal(rd[:], pd[:]), reads=[bpd], writes=[brd])
                for cc in range(2):
                    j = hd * 2 + cc
                    pt, bpt = psum.get()
                    for m in range(2):
                        S.op("pe", lambda e, m=m: e.matmul(pt[:], lhsT=vS[:, m, j * 128:(j + 1) * 128], rhs=eT[:, m, :], start=(m == 0), stop=(m == 1)),
                             reads=[b_vS, beT], writes=[bpt])
                    S.op("dve", lambda e, j=j: e.tensor_tensor(yAb[:, j, hh * 512:(hh + 1) * 512], pt[:], rd[:], ALU.mult),
                         reads=[bpt, brd], writes=[b_yA[j]])
        for j in range(8):
            sq, bsq = sq_p.get()
            S.op("act", lambda e: e.activation(sq[:], yAb[:, j, :], AF.Square), reads=[b_yA[j]], writes=[bsq])
            colsum_acc(sq, bsq, s1, b_s1, j == 0)
        rstd_from(s1, b_s1, G)
        for j in range(8):
            gated_out(yAb[:, j, :], b_yA[j], j, gD_s[:, j:j + 1], s1, b_s1, C_GATE + 3 * G, o_yM, blk)

    flush(0)
    S.finish(outs_b, "sp")
    return S
import math

L = 8192
NCH = 256
CB = 8
NBATCH = NCH // CB
N16 = 16384

CO = {}
_o = 0
for _n, _w in (("FA", 256), ("FAhi", 256), ("C", 128), ("S", 128), ("nS", 128), ("IA1", 256), ("IA2", 256),
               ("IBc", 64), ("IBs", 64), ("FN1", 128), ("FN2", 128)):
    CO[_n] = (_o, _w)
    _o += _w
CBF_W = _o
CF = {"T16r": (0, 128), "T16i": (128, 128), "T16ci": (256, 128), "T8r": (384, 64), "T8i": (448, 64)}
CF_W = 512


def build_L2(nc):
    S = Sched(nc)
    dt_in = lambda name, shape, dt=F32: nc.dram_tensor(name, shape, dt, kind="ExternalInput").ap()
    dt_out = lambda name, shape, dt=BF16: nc.dram_tensor(name, shape, dt, kind="ExternalOutput").ap()
    zr_d = dt_in("zr", [NCH, L], BF16); zi_d = dt_in("zi", [NCH, L], BF16)
    hv_d = dt_in("hv", [NCH, L], BF16); hx1_d = dt_in("hx1", [NCH, L], BF16); hx2_d = dt_in("hx2", [NCH, L], BF16)
    pos_rep = dt_in("pos_rep", [2, 32, L], I32)
    pos_t = dt_in("pos_t", [2, 64, 128], I32)
    bsc = dt_in("bsc", [32, 2])
    fw1a = dt_in("fw1a", [1, 64]); fw1b = dt_in("fw1b", [32, 64])
    mvec = dt_in("mvec", [64, 4])
    fw2 = dt_in("fw2", [64, 64])
    fw3c = dt_in("fw3c", [64, 2, NBATCH, 2, CB])
    dec_rep = dt_in("dec_rep", [64, 2, NBATCH, 2, CB])
    skip_rep = dt_in("skip_rep", [64, 2, NCH])
    cbf_d = dt_in("cbf", [128, CBF_W], BF16)
    cf_d = dt_in("cf", [128, CF_W])
    o_ff = dt_out("ff", [NCH, L])
    o_yc = dt_out("yc", [NCH, L])
    outs_b = []

    sb = lambda name, shape, dt=F32: nc.alloc_sbuf_tensor(name, shape, dt)
    cbf = sb("cbf_s", [128, CBF_W], BF16); cf = sb("cf_s", [128, CF_W]); b_c = Buf()
    S.dma("sp", cbf[:], cbf_d, writes=[b_c])
    S.dma("sp", cf[:], cf_d, writes=[b_c])
    cm = lambda n, rows=128: cbf[0:rows, CO[n][0]:CO[n][0] + CO[n][1]]
    cfm = lambda n: cf[:, CF[n][0]:CF[n][0] + CF[n][1]]
    ones_f = sb("ones_f", [64, 64]); b_ones = Buf()
    S.op("pool", lambda e: e.memset(ones_f[:], 1.0), writes=[b_ones])

    psum = Pool_(nc, "ps", 8, [128, 512], F32, psum=True)

    h2 = [sb("h2f", [64, L], BF16), sb("h2b", [64, L], BF16)]; b_h2 = [Buf(), Buf()]
    fw3s = sb("fw3s", [64, 2, NBATCH, 2 * CB], BF16); b_fw3 = Buf()
    dec = sb("dec", [64, 2, NBATCH, 2 * CB]); b_dec = Buf()
    skp = sb("skp", [64, 2, NCH]); b_skp = Buf()
    tpos = sb("tpos", [64, 2, 128]); b_tpos = Buf()
    S.dma("sp", skp[:], skip_rep, writes=[b_skp])
    S.dma("sp", dec[:], dec_rep.rearrange("p d b o c -> p d b (o c)"), writes=[b_dec])
    dneg = sb("dneg", [64, 2, NBATCH, 2 * CB]); b_dneg = Buf()
    S.op("dve", lambda e: e.tensor_scalar(dneg[:], dec[:], -1.0, None, ALU.mult), reads=[b_dec], writes=[b_dneg])
    S.op("dve", lambda e: e.tensor_tensor(dec[:], dec[:], dneg[:], ALU.max), reads=[b_dec, b_dneg], writes=[b_dec])
    par = sb("mlp_par", [64, 4 + 64 + 64 + 64 + 2]); b_par = Buf()
    fq = sb("fq", [64, 2]); b_fq = Buf()
    f3st = sb("f3st", [64, 2, NBATCH, 2 * CB]); b_f3st = Buf()
    tpi = sb("tpi", [64, 2, 128], I32); b_tpi = Buf()
    b_ft = Buf()
    negpi = sb("pi_c", [64, 1]); b_pi = Buf()
    fr_i = sb("fr_i", [64, 512], I32); fr_f = sb("fr_f", [64, 512]); b_fr = Buf()

    def frac_centered(ap, rows, bufs):
        n = ap.shape[1]
        for c0 in range(0, n, 512):
            a = ap[:, c0:c0 + 512]
            S.op("dve", lambda e: e.tensor_copy(fr_i[0:rows, :], a), reads=bufs, writes=[b_fr])
            S.op("dve", lambda e: e.tensor_copy(fr_f[0:rows, :], fr_i[0:rows, :]), reads=[b_fr], writes=[b_fr])
            S.op("dve", lambda e: e.tensor_tensor(a, a, fr_f[0:rows, :], ALU.subtract), reads=bufs + [b_fr], writes=bufs)
            S.op("dve", lambda e: e.tensor_scalar(fr_f[0:rows, :], a, 0.5, None, ALU.is_gt), reads=bufs, writes=[b_fr])
            S.op("dve", lambda e: e.tensor_tensor(a, a, fr_f[0:rows, :], ALU.subtract), reads=bufs + [b_fr], writes=bufs)

    with nc.sbuf_tensor("mlp_tmp", [64, 16384], F32) as mt:
        b_mt = Buf()
        posi = mt[0:32, 0:8192].bitcast(I32)
        posi_t = mt[32:33, 0:8192].bitcast(I32)
        posf = mt[0:32, 8192:16384]
        feat_t = mt[32:33, 0:8192]
        S.dma("sp", par[:, 0:4], mvec, writes=[b_par])
        S.dma("sp", par[:, 4:68], fw2, writes=[b_par])
        S.dma("sp", par[0:32, 68:132], fw1b, writes=[b_par])
        S.dma("sp", par[32:33, 132:196], fw1a, writes=[b_par])
        S.dma("sp", par[0:32, 196:198], bsc, writes=[b_par])
        S.op("dve", lambda e: e.tensor_scalar(fq[:, 0:1], par[:, 1:2], 1.0 / (2 * math.pi), None, ALU.mult), reads=[b_par], writes=[b_fq])
        S.op("dve", lambda e: e.tensor_scalar(fq[:, 1:2], par[:, 3:4], 1.0 / (2 * math.pi), None, ALU.mult), reads=[b_par], writes=[b_fq])
        S.dma("sp", f3st[:], fw3c.rearrange("j d b o c -> j d b (o c)"), writes=[b_f3st])
        S.op("pool", lambda e: e.tensor_copy(fw3s[:], f3st[:]), reads=[b_f3st], writes=[b_fw3])
        S.dma("sp", tpi[:], pos_t.rearrange("d p s -> p d s"), writes=[b_tpi])
        S.op("dve", lambda e: e.tensor_copy(tpos[:], tpi[:]), reads=[b_tpi], writes=[b_tpos])
        S.op("dve", lambda e: e.tensor_scalar(tpos[:], tpos[:], 1.0 / L, None, ALU.mult), reads=[b_tpos], writes=[b_tpos])
        S.op("pool", lambda e: e.memset(negpi[:], math.pi), writes=[b_pi])
        h1 = mt[0:64, 8192:16384]
        for d in range(2):
            S.dma("sp", posi, pos_rep[d], writes=[b_mt])
            S.dma("sp", posi_t, pos_rep[d, 0:1, :], writes=[b_mt])
            S.op("dve", lambda e: e.tensor_copy(posf, posi), reads=[b_mt], writes=[b_mt])
            S.op("dve", lambda e: e.tensor_copy(mt[32:33, 8192:16384], posi_t), reads=[b_mt], writes=[b_mt])
            S.op("dve", lambda e: e.tensor_scalar(feat_t, mt[32:33, 8192:16384], 1.0 / L, None, ALU.mult), reads=[b_mt], writes=[b_ft])
            S.op("dve", lambda e: e.tensor_scalar(posf, posf, par[0:32, 196:197], par[0:32, 197:198], ALU.mult, ALU.add), reads=[b_mt, b_par], writes=[b_mt])
            frac_centered(posf, 32, [b_mt])
            feats = mt[0:32, 0:8192]
            S.op("act", lambda e: e.activation(feats, posf, AF.Sin, scale=2 * math.pi), reads=[b_mt], writes=[b_mt])
            for ch in range(16):
                sl = slice(ch * 512, (ch + 1) * 512)
                pt, bpt = psum.get()
                S.op("pe", lambda e: e.matmul(pt[0:64, :], lhsT=par[32:33, 132:196], rhs=feat_t[:, sl], start=True, stop=False),
                     reads=[b_par, b_ft], writes=[bpt])
                S.op("pe", lambda e: e.matmul(pt[0:64, :], lhsT=par[0:32, 68:132], rhs=feats[:, sl], start=False, stop=True),
                     reads=[b_par, b_mt], writes=[bpt])
                hs = h1[:, sl]
                S.op("dve", lambda e: e.tensor_scalar(hs, pt[0:64, :], par[:, 0:1], fq[:, 0:1], ALU.add, ALU.mult), reads=[bpt, b_par, b_fq], writes=[b_mt])
                S.op("dve", lambda e: e.tensor_scalar(hs, hs, 8.0, None, ALU.add), reads=[b_mt], writes=[b_mt])
                frac_centered(hs, 64, [b_mt])
                S.op("act", lambda e: e.activation(hs, hs, AF.Sin, scale=2 * math.pi), reads=[b_mt], writes=[b_mt])
                pt2, bpt2 = psum.get()
                S.op("pe", lambda e: e.matmul(pt2[0:64, :], lhsT=par[:, 4:68], rhs=hs, start=True, stop=True), reads=[b_par, b_mt], writes=[bpt2])
                S.op("dve", lambda e: e.tensor_scalar(hs, pt2[0:64, :], par[:, 2:3], fq[:, 1:2], ALU.add, ALU.mult), reads=[bpt2, b_par, b_fq], writes=[b_mt])
                S.op("dve", lambda e: e.tensor_scalar(hs, hs, 8.0, None, ALU.add), reads=[b_mt], writes=[b_mt])
                frac_centered(hs, 64, [b_mt])
                S.op("act", lambda e: e.activation(h2[d][:, sl], hs, AF.Sin, scale=2 * math.pi), reads=[b_mt], writes=[b_h2[d]])
        b_mt_final = b_mt

    Kf = sb("Kf", [128, 2, 2, CB, 128]); b_Kf = [Buf(), Buf()]
    stag_p = Pool_(nc, "stag", 2, [128, CB, 256], F32)
    tmp_p = {e: Pool_(nc, "tmp" + e, 2, [128, CB * 128], F32) for e in ("dve", "pool")}
    cb_p = Pool_(nc, "cb", 6, [128, CB, 128], BF16)
    xin_p = Pool_(nc, "xin", 8, [64, CB, 128], BF16)
    kt = [sb("ktf", [64, 2 * CB, 128]), sb("ktb", [64, 2 * CB, 128])]; b_kt = [Buf(), Buf()]
    ktb16 = [sb("ktf16", [64, 2 * CB, 128], BF16), sb("ktb16", [64, 2 * CB, 128], BF16)]; b_ktb = [Buf(), Buf()]
    win = sb("win", [64, 2 * CB, 128]); b_win = Buf()
    nrm = sb("nrm", [64, 4, 2 * CB]); b_nrm = Buf()
    yst = sb("yst", [64, CB, 128]); b_yst = Buf()
    fo_p = Pool_(nc, "fo", 2, [128, 2 * CB, 64], BF16)
    for b in [b_Kf[0], b_Kf[1], b_kt[0], b_kt[1], b_ktb[0], b_ktb[1], b_win, b_nrm, b_yst] + \
            [t[1] for t in stag_p.t + tmp_p["dve"].t + tmp_p["pool"].t + cb_p.t + xin_p.t + fo_p.t]:
        b.w = b_mt_final.w
        b.r = dict(b_mt_final.r)

    def out_dma(dst, src, bsrc):
        b = Buf(); outs_b.append(b)
        S.dma("sp", dst, src, reads=[bsrc], writes=[b])

    def stage_data_stationary(chan_mms, nch, width, rows=128):
        st, bst = stag_p.get()
        per = 512 // width
        for c0 in range(0, nch, per):
            pt, bpt = psum.get()
            n = min(per, nch - c0)
            for i in range(n):
                mms = chan_mms(c0 + i)
                for q, (lhsT, rhs, rd) in enumerate(mms):
                    S.op("pe", lambda e, lhsT=lhsT, rhs=rhs, i=i, q=q: e.matmul(pt[0:rows, i * width:(i + 1) * width], lhsT=lhsT, rhs=rhs,
                                                                                 start=(q == 0), stop=(q == len(mms) - 1)),
                         reads=rd + [b_c], writes=[bpt])
            S.op("act", lambda e: e.copy(st[0:rows, c0:c0 + n, 0:width], pt[0:rows, 0:n * width].rearrange("p (c w) -> p c w", c=n)),
                 reads=[bpt], writes=[bst])
        return st, bst

    def cplx_mul(sr, si, tr, ti, rd, n1, nch, rows=128):
        dr, bdr = cb_p.get(); di, bdi = cb_p.get()
        drv = dr[0:rows, 0:nch, 0:n1]; div = di[0:rows, 0:nch, 0:n1]
        ta, bta = tmp_p["dve"].get(); tb, btb = tmp_p["dve"].get()
        tav = ta[0:rows, 0:nch * n1].rearrange("p (c k) -> p c k", c=nch); tbv = tb[0:rows, 0:nch * n1].rearrange("p (c k) -> p c k", c=nch)
        S.op("dve", lambda e: e.tensor_tensor(tav, sr, tr, ALU.mult), reads=rd, writes=[bta])
        S.op("dve", lambda e: e.tensor_tensor(tbv, si, ti, ALU.mult), reads=rd, writes=[btb])
        S.op("dve", lambda e: e.tensor_tensor(drv, tav, tbv, ALU.subtract), reads=[bta, btb], writes=[bdr])
        tc, btc = tmp_p["pool"].get(); td, btd = tmp_p["pool"].get()
        tcv = tc[0:rows, 0:nch * n1].rearrange("p (c k) -> p c k", c=nch); tdv = td[0:rows, 0:nch * n1].rearrange("p (c k) -> p c k", c=nch)
        S.op("pool", lambda e: e.tensor_tensor(tcv, sr, ti, ALU.mult), reads=rd, writes=[btc])
        S.op("pool", lambda e: e.tensor_tensor(tdv, si, tr, ALU.mult), reads=rd, writes=[btd])
        S.op("pool", lambda e: e.tensor_tensor(div, tcv, tdv, ALU.add), reads=[btc, btd], writes=[bdi])
        return (dr, bdr), (di, bdi)

    def bc(tab, nch, n1):
        return tab.unsqueeze(1).broadcast_to([128, nch, n1])

    def stageB_fwd(Ar, Ai, nch, n1, want_imag, evac):
        per = 512 // n1
        for g0 in range(0, nch, per):
            ng = min(per, nch - g0)
            rr = Ar[0][:, g0:g0 + ng, 0:n1]; ri = Ai[0][:, g0:g0 + ng, 0:n1]
            ptr, bptr = psum.get()
            o = ptr[:, 0:ng * n1].rearrange("p (c k) -> p c k", c=ng)
            S.op("pe", lambda e: e.matmul(o, lhsT=cm("C"), rhs=rr, start=True, stop=False), reads=[Ar[1], b_c], writes=[bptr])
            S.op("pe", lambda e: e.matmul(o, lhsT=cm("S"), rhs=ri, start=False, stop=True), reads=[Ai[1], b_c], writes=[bptr])
            pti = bpti = None
            if want_imag:
                pti, bpti = psum.get()
                o2 = pti[:, 0:ng * n1].rearrange("p (c k) -> p c k", c=ng)
                S.op("pe", lambda e: e.matmul(o2, lhsT=cm("C"), rhs=ri, start=True, stop=False), reads=[Ai[1], b_c], writes=[bpti])
                S.op("pe", lambda e: e.matmul(o2, lhsT=cm("nS"), rhs=rr, start=False, stop=True), reads=[Ar[1], b_c], writes=[bpti])
            evac(g0, ng, ptr, bptr, pti, bpti)

    def fwd_fft16k(chan_mms, nch, evac):
        st, bst = stage_data_stationary(chan_mms, nch, 256)
        Ar, Ai = cplx_mul(st[:, 0:nch, 0:128], st[:, 0:nch, 128:256], bc(cfm("T16r"), nch, 128), bc(cfm("T16i"), nch, 128), [bst, b_c], 128, nch)
        stageB_fwd(Ar, Ai, nch, 128, True, evac)

    def load_x(dram, b):
        t, bt = xin_p.get()
        S.dma("sp", t[:], dram[b * CB:(b + 1) * CB, :].rearrange("c (s1 s2) -> s1 c s2", s2=128), writes=[bt])
        return t, bt

    def long_conv(xin, bxin, o, gate, bgate, b):
        def evacB(g0, ng, ptr, bptr, pti, bpti):
            S.op("act", lambda e: e.copy(stB[:, g0:g0 + ng, 0:128], ptr[:, 0:ng * 128].rearrange("p (c k) -> p c k", c=ng)), reads=[bptr], writes=[bstB])
            S.op("act", lambda e: e.copy(stB[:, g0:g0 + ng, 128:256], pti[:, 0:ng * 128].rearrange("p (c k) -> p c k", c=ng)), reads=[bpti], writes=[bstB])
        stB, bstB = stag_p.get()
        fwd_fft16k(lambda c: [(xin[:, c, :], cm("FA", 64), [bxin])], CB, evacB)
        Pr, Pi = cplx_mul(stB[:, :, 0:128], stB[:, :, 128:256], Kf[:, o, 0, :, :], Kf[:, o, 1, :, :], [bstB, b_Kf[o]], 128, CB)
        st, bst = stage_data_stationary(lambda c: [(Pr[0][:, c, :], cm("IA1"), [Pr[1]]), (Pi[0][:, c, :], cm("IA2"), [Pi[1]])], CB, 256)
        Br, Bi = cplx_mul(st[:, :, 0:128], st[:, :, 128:256], bc(cfm("T16r"), CB, 128), bc(cfm("T16ci"), CB, 128), [bst, b_c], 128, CB)
        for g0 in range(0, CB, 4):
            pt, bpt = psum.get()
            o4 = pt[0:64, :].rearrange("p (c k) -> p c k", c=4)
            S.op("pe", lambda e: e.matmul(o4, lhsT=cm("IBc"), rhs=Br[0][:, g0:g0 + 4, :], start=True, stop=False), reads=[Br[1], b_c], writes=[bpt])
            S.op("pe", lambda e: e.matmul(o4, lhsT=cm("IBs"), rhs=Bi[0][:, g0:g0 + 4, :], start=False, stop=True), reads=[Bi[1], b_c], writes=[bpt])
            S.op("act", lambda e: e.copy(yst[:, g0:g0 + 4, :], o4), reads=[bpt], writes=[b_yst])
        z, bz = xin_p.get()
        tq, btq = tmp_p["dve"].get()
        tv = tq[0:64, :].rearrange("p (c k) -> p c k", c=CB)
        skb = skp[:, o, b * CB:(b + 1) * CB].unsqueeze(2).broadcast_to([64, CB, 128])
        S.op("dve", lambda e: e.tensor_tensor(tv, xin[:], skb, ALU.mult), reads=[bxin, b_skp], writes=[btq])
        S.op("dve", lambda e: e.tensor_tensor(tv, tv, yst[:], ALU.add), reads=[btq, b_yst], writes=[btq])
        S.op("dve", lambda e: e.tensor_tensor(z[:], tv, gate[:], ALU.mult), reads=[btq, bgate], writes=[bz])
        return z, bz

    for b in range(NBATCH):
        for d in range(2):
            S.op("dve", lambda e: e.tensor_tensor(win[:], tpos[:, d, :].unsqueeze(1).broadcast_to([64, 2 * CB, 128]),
                                                  dec[:, d, b, :].unsqueeze(2).broadcast_to([64, 2 * CB, 128]), ALU.mult),
                 reads=[b_tpos, b_dec], writes=[b_win])
            S.op("act", lambda e: e.activation(win[:], win[:], AF.Exp, scale=-1.0), reads=[b_win], writes=[b_win])
            for s20 in range(0, 128, 32):
                pt, bpt = psum.get()
                for q in range(32):
                    s2 = s20 + q
                    S.op("pe", lambda e, s2=s2, q=q: e.matmul(pt[0:64, q * 16:(q + 1) * 16], lhsT=h2[d][:, s2:L:128], rhs=fw3s[:, d, b, :],
                                                              start=True, stop=True), reads=[b_h2[d], b_fw3], writes=[bpt])
                S.op("dve", lambda e: e.tensor_tensor(kt[d][:, :, s20:s20 + 32].rearrange("p c s -> p s c"),
                                                      pt[0:64, :].rearrange("p (s c) -> p s c", c=16),
                                                      win[:, :, s20:s20 + 32].rearrange("p c s -> p s c"), ALU.mult),
                     reads=[bpt, b_win], writes=[b_kt[d]])
            if d == 1:
                S.op("dve", lambda e: e.memset(kt[1][0:1, :, 0:1], 0.0), reads=[], writes=[b_kt[1]])
            S.op("dve", lambda e: e.tensor_tensor(win[:], kt[d][:], kt[d][:], ALU.mult), reads=[b_kt[d]], writes=[b_win])
            S.op("dve", lambda e: e.tensor_reduce(nrm[:, d, :], win[:], axis=AX.X, op=ALU.add), reads=[b_win], writes=[b_nrm])
        S.op("dve", lambda e: e.tensor_tensor(nrm[:, 2, :], nrm[:, 0, :], nrm[:, 1, :], ALU.add), reads=[b_nrm], writes=[b_nrm])
        pt, bpt = psum.get()
        S.op("pe", lambda e: e.matmul(pt[0:64, 0:2 * CB], lhsT=ones_f[:], rhs=nrm[:, 2, :], start=True, stop=True), reads=[b_ones, b_nrm], writes=[bpt])
        S.op("dve", lambda e: e.tensor_scalar(nrm[:, 3, :], pt[0:64, 0:2 * CB], 1e-6, None, ALU.add), reads=[bpt], writes=[b_nrm])
        S.op("dve", lambda e: e.reciprocal(nrm[:, 3, :], nrm[:, 3, :]), reads=[b_nrm], writes=[b_nrm])
        S.op("act", lambda e: e.activation(nrm[:, 3, :], nrm[:, 3, :], AF.Sqrt), reads=[b_nrm], writes=[b_nrm])
        for d in range(2):
            S.op("dve", lambda e: e.tensor_tensor(ktb16[d][:], kt[d][:], nrm[:, 3, :].unsqueeze(2).broadcast_to([64, 2 * CB, 128]), ALU.mult),
                 reads=[b_kt[d], b_nrm], writes=[b_ktb[d]])
        for o in range(2):
            def evacK(g0, ng, ptr, bptr, pti, bpti, o=o):
                S.op("act", lambda e: e.copy(Kf[:, o, 0, g0:g0 + ng, :], ptr[:, 0:ng * 128].rearrange("p (c k) -> p c k", c=ng)), reads=[bptr], writes=[b_Kf[o]])
                S.op("act", lambda e: e.copy(Kf[:, o, 1, g0:g0 + ng, :], pti[:, 0:ng * 128].rearrange("p (c k) -> p c k", c=ng)), reads=[bpti], writes=[b_Kf[o]])
            fwd_fft16k(lambda c, o=o: [(ktb16[0][:, o * CB + c, :], cm("FA", 64), [b_ktb[0]]),
                                       (ktb16[1][:, o * CB + c, :], cm("FAhi", 64), [b_ktb[1]])], CB, evacK)
        v, bv = load_x(hv_d, b)
        x1, bx1 = load_x(hx1_d, b)
        x2, bx2 = load_x(hx2_d, b)
        z1, bz1 = long_conv(v, bv, 0, x1, bx1, b)
        z2, bz2 = long_conv(z1, bz1, 1, x2, bx2, b)
        out_dma(o_yc[b * CB:(b + 1) * CB, :].rearrange("c (s1 s2) -> s1 c s2", s2=128), z2[:], bz2)

        zr, bzr = load_x(zr_d, b)
        zi, bzi = load_x(zi_d, b)
        st, bst = stage_data_stationary(lambda c: [(zr[:, c, :], cm("FN1", 64), [bzr]), (zi[:, c, :], cm("FN2", 64), [bzi])], CB, 128)
        Ar, Ai = cplx_mul(st[:, :, 0:64], st[:, :, 64:128], bc(cfm("T8r"), CB, 64), bc(cfm("T8i"), CB, 64), [bst, b_c], 64, CB)
        fo, bfo = fo_p.get()

        def evacF(g0, ng, ptr, bptr, pti, bpti):
            S.op("act", lambda e: e.copy(fo[:, g0:g0 + ng, :], ptr[:, 0:ng * 64].rearrange("p (c k) -> p c k", c=ng)), reads=[bptr], writes=[bfo])
        stageB_fwd(Ar, Ai, CB, 64, False, evacF)
        out_dma(o_ff[b * CB:(b + 1) * CB, :].rearrange("c (k2 k1) -> k2 c k1", k1=64), fo[:, 0:CB, :], bfo)

    S.finish(outs_b, "sp")
    return S

D = 2048
G = 1024
DM = 4096
NT = 2048
TBK = 512
EPS = 1e-6


def build_L3(nc):
    S = Sched(nc)
    dt_in = lambda name, shape, dt=F32: nc.dram_tensor(name, shape, dt, kind="ExternalInput").ap()
    ffc = dt_in("ffc", [G, NT], BF16); ycc = dt_in("ycc", [G, NT], BF16)
    sgB = dt_in("sgB", [G, NT], BF16); sgC = dt_in("sgC", [G, NT], BF16)
    yAg = dt_in("yAg", [G, NT], BF16); yMg = dt_in("yMg", [G, NT], BF16)
    x_d = dt_in("x", [NT, D])
    fnet_w = dt_in("fnet_w", [G, G])
    vec = dt_in("vec3", [128, 3, 8])
    w_out = dt_in("w_out", [DM, D])
    post_g_bc = dt_in("post_g_bc", [128, D])
    xo = nc.dram_tensor("xo", [NT, D], F32, kind="ExternalOutput").ap()
    wsc = nc.dram_tensor("w_out_bf", [DM, D], BF16, kind="Internal").ap()
    outs_b = []

    sb = lambda name, shape, dt=F32: nc.alloc_sbuf_tensor(name, shape, dt)
    ones_f = sb("ones_f", [128, 128]); b_ones = Buf()
    S.op("pool", lambda e: e.memset(ones_f[:], 1.0), writes=[b_ones])
    vec_s = sb("vec_s", [128, 3, 8]); b_vec = Buf()
    S.dma("sp", vec_s[:], vec, writes=[b_vec])
    pg = sb("pg", [128, D]); b_pg = Buf()
    S.dma("sp", pg[:], post_g_bc, writes=[b_pg])
    psum = Pool_(nc, "ps", 8, [128, 512], F32, psum=True)
    fw_bf = sb("fw_bf", [128, 8, G], BF16); b_fw = Buf()

    b_wsc = [Buf() for _ in range(32)]
    with nc.sbuf_tensor("stg0", [128, D], F32) as stg0, nc.sbuf_tensor("stg1", [128, D], F32) as stg1, \
            nc.sbuf_tensor("cst0", [128, D], BF16) as cst0, nc.sbuf_tensor("cst1", [128, D], BF16) as cst1:
        stg = [(stg0, Buf()), (stg1, Buf())]; cst = [(cst0, Buf()), (cst1, Buf())]
        for k in range(8):
            st, bst = stg[k % 2]
            S.dma("sp", st[:, 0:G], fnet_w[k * 128:(k + 1) * 128, :], writes=[bst])
            S.op("pool", lambda e: e.tensor_copy(fw_bf[:, k, :], st[:, 0:G]), reads=[bst], writes=[b_fw])
        for k in range(32):
            st, bst = stg[k % 2]; ct, bct = cst[k % 2]
            S.dma("sp", st[:], w_out[k * 128:(k + 1) * 128, :], writes=[bst])
            S.op("pool", lambda e: e.tensor_copy(ct[:], st[:]), reads=[bst], writes=[bct])
            S.dma("sp", wsc[k * 128:(k + 1) * 128, :], ct[:], reads=[bct], writes=[b_wsc[k]])
        last_scr = [stg[0][1], stg[1][1], cst[0][1], cst[1][1]]

    ygall = sb("ygall", [128, 32, TBK], BF16); b_yg = [Buf() for _ in range(4)]
    wt_p = Pool_(nc, "wt", 4, [128, 8, 512], BF16)
    outraw = sb("outraw", [128, 4, D]); b_or = [Buf() for _ in range(4)]
    x_p = Pool_(nc, "xt", 2, [128, D], F32)
    in_p = Pool_(nc, "inb", 2, [128, 8, TBK], BF16)
    sg_p = Pool_(nc, "sgb", 2, [128, 8, TBK], BF16)
    yb = sb("yb", [128, 8, TBK]); b_yb = Buf()
    acc = sb("acc", [128, TBK]); b_acc = Buf()
    sq_p = Pool_(nc, "sq", 2, [128, TBK], F32)
    st_p = Pool_(nc, "st", 4, [128, 4], F32)
    junk = sb("junk", [128, D], BF16); b_junk = Buf()
    for b in b_yg + b_or + [b_yb, b_acc, b_junk] + [t[1] for t in wt_p.t + x_p.t + in_p.t + sg_p.t + sq_p.t + st_p.t]:
        for ls in last_scr:
            if ls.w is not None:
                b.r[ls.w[0]] = max(b.r.get(ls.w[0], 0), ls.w[1])
            for k_, v_ in ls.r.items():
                b.r[k_] = max(b.r.get(k_, 0), v_)

    def colsum(src_ap, bsrc, first):
        pt, bpt = psum.get()
        S.op("pe", lambda e: e.matmul(pt[:], lhsT=ones_f[:], rhs=src_ap, start=True, stop=True), reads=[b_ones] + bsrc, writes=[bpt])
        if first:
            S.op("dve", lambda e: e.tensor_copy(acc[:], pt[:]), reads=[bpt], writes=[b_acc])
        else:
            S.op("dve", lambda e: e.tensor_tensor(acc[:], acc[:], pt[:], ALU.add), reads=[bpt, b_acc], writes=[b_acc])

    def rstd_acc():
        S.op("dve", lambda e: e.tensor_scalar(acc[:], acc[:], 1.0 / G, EPS, ALU.mult, ALU.add), reads=[b_acc], writes=[b_acc])
        S.op("dve", lambda e: e.reciprocal(acc[:], acc[:]), reads=[b_acc], writes=[b_acc])
        S.op("act", lambda e: e.activation(acc[:], acc[:], AF.Sqrt), reads=[b_acc], writes=[b_acc])

    for blk in range(NT // TBK):
        ts = slice(blk * TBK, (blk + 1) * TBK)
        ld = lambda dram: dram[:, ts].rearrange("(k p) t -> p k t", p=128)
        S.dma("sp", ygall[:, 0:8, :], ld(yAg), writes=[b_yg[0]])
        S.dma("sp", ygall[:, 24:32, :], ld(yMg), writes=[b_yg[3]])
        ff, bff = in_p.get()
        S.dma("sp", ff[:], ld(ffc), writes=[bff])
        sg, bsg = sg_p.get()
        S.dma("sp", sg[:], ld(sgB), writes=[bsg])
        for j in range(8):
            pt, bpt = psum.get()
            for k in range(8):
                S.op("pe", lambda e, k=k: e.matmul(pt[:], lhsT=fw_bf[:, k, j * 128:(j + 1) * 128], rhs=ff[:, k, :], start=(k == 0), stop=(k == 7)),
                     reads=[b_fw, bff], writes=[bpt])
            S.op("act", lambda e: e.activation(yb[:, j, :], pt[:], AF.Identity, bias=vec_s[:, 0, j:j + 1]), reads=[bpt, b_vec], writes=[b_yb])
            sq, bsq = sq_p.get()
            S.op("act", lambda e: e.activation(sq[:], yb[:, j, :], AF.Square), reads=[b_yb], writes=[bsq])
            colsum(sq[:], [bsq], j == 0)
        rstd_acc()
        for j in range(8):
            sq, bsq = sq_p.get()
            S.op("dve", lambda e: e.scalar_tensor_tensor(sq[:], yb[:, j, :], vec_s[:, 1, j:j + 1], acc[:], ALU.mult, ALU.mult),
                 reads=[b_yb, b_vec, b_acc], writes=[bsq])
            S.op("dve", lambda e: e.tensor_tensor(ygall[:, 8 + j, :], sq[:], sg[:, j, :], ALU.mult), reads=[bsq, bsg], writes=[b_yg[1]])
        yc, byc = in_p.get()
        S.dma("sp", yc[:], ld(ycc), writes=[byc])
        sg, bsg = sg_p.get()
        S.dma("sp", sg[:], ld(sgC), writes=[bsg])
        for j in range(8):
            sq, bsq = sq_p.get()
            S.op("act", lambda e: e.activation(sq[:], yc[:, j, :], AF.Square), reads=[byc], writes=[bsq])
            colsum(sq[:], [bsq], j == 0)
        rstd_acc()
        for j in range(8):
            sq, bsq = sq_p.get()
            S.op("dve", lambda e: e.scalar_tensor_tensor(sq[:], yc[:, j, :], vec_s[:, 2, j:j + 1], acc[:], ALU.mult, ALU.mult),
                 reads=[byc, b_vec, b_acc], writes=[bsq])
            S.op("dve", lambda e: e.tensor_tensor(ygall[:, 16 + j, :], sq[:], sg[:, j, :], ALU.mult), reads=[bsq, bsg], writes=[b_yg[2]])
        for ng in range(4):
            pts = [psum.get() for _ in range(4)]
            for kq in range(4):
                wt, bwt = wt_p.get()
                S.dma("sp", wt[:], wsc[kq * 1024:(kq + 1) * 1024, ng * 512:(ng + 1) * 512].rearrange("(k p) n -> p k n", p=128),
                      reads=b_wsc[kq * 8:(kq + 1) * 8], writes=[bwt])
                for tt in range(4):
                    pt, bpt = pts[tt]
                    for k in range(8):
                        kk = kq * 8 + k
                        S.op("pe", lambda e, k=k, kk=kk, tt=tt, pt=pt: e.matmul(pt[:], lhsT=ygall[:, kk, tt * 128:(tt + 1) * 128], rhs=wt[:, k, :],
                                                                                 start=(kk == 0), stop=(kk == 31)),
                             reads=[b_yg[kq], bwt], writes=[bpt])
            for tt in range(4):
                pt, bpt = pts[tt]
                S.op("act", lambda e, tt=tt, pt=pt: e.copy(outraw[:, tt, ng * 512:(ng + 1) * 512], pt[:]), reads=[bpt], writes=[b_or[tt]])
        for tt in range(4):
            r0 = blk * TBK + tt * 128
            xt, bxt = x_p.get()
            S.dma("sp", xt[:], x_d[r0:r0 + 128, :], writes=[bxt])
            stt, bstt = st_p.get()
            S.op("dve", lambda e: e.memset(stt[:], 0.0), writes=[bstt])
            S.op("act", lambda e: e.activation(junk[:], outraw[:, tt, :], AF.Square, accum_out=stt[:, 0:1]), reads=[b_or[tt], bstt], writes=[b_junk, bstt])
            S.op("dve", lambda e: e.tensor_scalar(stt[:, 1:2], stt[:, 0:1], 1.0 / D, EPS, ALU.mult, ALU.add), reads=[bstt], writes=[bstt])
            S.op("dve", lambda e: e.reciprocal(stt[:, 2:3], stt[:, 1:2]), reads=[bstt], writes=[bstt])
            S.op("act", lambda e: e.activation(stt[:, 2:3], stt[:, 2:3], AF.Sqrt), reads=[bstt], writes=[bstt])
            S.op("dve", lambda e: e.scalar_tensor_tensor(outraw[:, tt, :], outraw[:, tt, :], stt[:, 2:3], pg[:], ALU.mult, ALU.mult),
                 reads=[b_or[tt], bstt, b_pg], writes=[b_or[tt]])
            S.op("dve", lambda e: e.tensor_tensor(xt[:], xt[:], outraw[:, tt, :], ALU.add), reads=[bxt, b_or[tt]], writes=[bxt])
            b = Buf(); outs_b.append(b)
            S.dma("sp", xo[r0:r0 + 128, :], xt[:], reads=[bxt], writes=[b])

    S.finish(outs_b, "sp")
    return S
import numpy as np
D=2048; G=1024; L=8192; TB=1024; H=16; NB=2

def chunked(v, nch):
    return np.ascontiguousarray(v.reshape(nch, 128).T)

def dftG_const():
    k = np.arange(G)
    ang = 2*np.pi*((k[:,None]*k[None,:]) % G)/G
    sc = 1.0/np.sqrt(float(L)*G)
    return np.concatenate([np.cos(ang)*sc, -np.sin(ang)*sc], axis=1).astype(np.float32)

def l1_inputs(inp, l, core, x_cur):
    b = core // 4; j = core % 4
    xb = x_cur[b]
    xpad = np.concatenate([np.zeros((H, D), np.float32), xb, np.zeros((H, D), np.float32)], 0)
    xp = np.stack([xpad[j*2048 + blk*TB : j*2048 + blk*TB + TB + 2*H] for blk in range(NB)], 0)
    cw = np.ascontiguousarray(inp['conv_dw_w'][l].T.reshape(8, 128, 31).transpose(1, 0, 2))
    gn = inp['group_norm_g'][l]
    cvec = np.stack([chunked(inp['conv_dw_b'][l], 8), chunked(inp['conv_ln_g'][l], 8), chunked(inp['conv_ln_b'][l], 8),
                     chunked(inp['conv_pw_b'][l], 8), chunked(gn[0:G], 8)], axis=1)
    hw = np.concatenate([inp['hy_short_w'][l], inp['hy_short_b'][l][None]], 0)
    hyw = np.ascontiguousarray(hw.T.reshape(24, 128, 4).transpose(1, 0, 2))
    return {
        "xp": np.ascontiguousarray(xp), "w_in": inp['w_in'][l],
        "pre_g_bc": np.ascontiguousarray(np.broadcast_to(inp['pre_norm_g'][l], (128, D))),
        "conv_w": cw, "conv_vec": np.ascontiguousarray(cvec), "conv_pw_w": inp['conv_pw_w'][l],
        "hy_w": hyw, "mem": inp['mem'][b],
        "mem_g_bc": np.ascontiguousarray(np.broadcast_to(inp['mem_norm_g'], (128, D))),
        "mem_wk": inp['mem_wk'][l], "mem_wv": inp['mem_wv'][l],
        "gD": chunked(gn[3*G:4*G], 8), "dftG": dftG_const(),
    }

BF = ml_dtypes.bfloat16
NCH=256; CB=8; NBATCH=NCH//CB

def l2_consts():
    j = np.arange(128)
    ang = 2*np.pi*((j[:,None]*j[None,:]) % 128)/128.0
    C = np.cos(ang); Sn = np.sin(ang)
    j64 = np.arange(64)
    ang64 = 2*np.pi*((j64[:,None]*j64[None,:]) % 64)/64.0
    C64 = np.cos(ang64); S64 = np.sin(ang64)
    cb = np.zeros((128, CBF_W), np.float64)
    def put(name, m):
        o, w = CO[name]; assert m.shape[1] == w, (name, m.shape); cb[:m.shape[0], o:o+w] = m
    FA = np.concatenate([C, -Sn], 1)
    put("FA", FA); put("FAhi", FA[64:128]); put("C", C); put("S", Sn); put("nS", -Sn)
    put("IA1", np.concatenate([C, Sn], 1)); put("IA2", np.concatenate([-Sn, C], 1))
    put("IBc", C[:, :64]/16384.0); put("IBs", -Sn[:, :64]/16384.0)
    put("FN1", np.concatenate([C64, -S64], 1)); put("FN2", np.concatenate([S64, C64], 1))
    cf = np.zeros((128, CF_W), np.float64)
    a16 = 2*np.pi*(j[:,None]*j[None,:])/16384.0
    a8 = 2*np.pi*(j[:,None]*j64[None,:])/8192.0
    def putf(name, m):
        o, w = CF[name]; cf[:, o:o+w] = m
    putf("T16r", np.cos(a16)); putf("T16i", -np.sin(a16)); putf("T16ci", np.sin(a16))
    putf("T8r", np.cos(a8)); putf("T8i", -np.sin(a8))
    return cb.astype(BF), cf.astype(np.float32)

def l2_inputs(inp, l, core, Zb, hyb):
    jq = core % 4
    cs = slice(jq*NCH, (jq+1)*NCH)
    pos = inp['positions'].astype(np.int32)
    posb = pos[(L - np.arange(L)) % L]
    pos_rep = np.stack([np.broadcast_to(pos, (32, L)), np.broadcast_to(posb, (32, L))], 0)
    pos_t = np.stack([pos.reshape(64, 128), posb.reshape(64, 128)], 0)
    bands = np.linspace(1e-4, 15, 16, dtype=np.float32)
    bsc = np.zeros((32, 2), np.float32)
    bsc[:, 0] = np.concatenate([bands, bands]) / np.float32(L)
    bsc[:16, 1] = 0.25
    fw1 = inp['hy_fw1'][l]
    mvec = np.stack([inp['hy_fb1'][l], inp['hy_freq1'][l], inp['hy_fb2'][l], inp['hy_freq2'][l]], 1)
    fw3 = inp['hy_fw3'][l].reshape(64, 2, 2, 1024)[:, :, :, cs]
    fw3c = fw3.reshape(64, 2, 2, NBATCH, CB).transpose(0, 2, 3, 1, 4)
    dec = inp['hy_decay'][l][:, :, cs]
    decc = dec.reshape(2, 2, NBATCH, CB).transpose(1, 2, 0, 3)
    cbf, cf = l2_consts()
    return {
        "zr": np.ascontiguousarray(Zb[cs]), "zi": np.ascontiguousarray(Zb[1024 + jq*NCH: 1024 + (jq+1)*NCH]),
        "hv": np.ascontiguousarray(hyb[cs]), "hx1": np.ascontiguousarray(hyb[1024 + jq*NCH:1024 + (jq+1)*NCH]),
        "hx2": np.ascontiguousarray(hyb[2048 + jq*NCH:2048 + (jq+1)*NCH]),
        "pos_rep": np.ascontiguousarray(pos_rep), "pos_t": np.ascontiguousarray(pos_t), "bsc": bsc,
        "fw1a": np.ascontiguousarray(fw1[0:1]), "fw1b": np.ascontiguousarray(fw1[1:33]), "mvec": np.ascontiguousarray(mvec),
        "fw2": inp['hy_fw2'][l], "fw3c": np.ascontiguousarray(fw3c),
        "dec_rep": np.ascontiguousarray(np.broadcast_to(decc, (64,) + decc.shape)),
        "skip_rep": np.ascontiguousarray(np.broadcast_to(inp['hy_skip'][l][:, cs], (64, 2, NCH))),
        "cbf": cbf, "cf": cf,
    }

def l3_inputs(inp, l, core, x_cur, ffb, ycb, l1o):
    b = core // 4; j = core % 4
    ts = slice(j*2048, (j+1)*2048)
    gn = inp['group_norm_g'][l]
    vec3 = np.stack([chunked(inp['fnet_b'][l], 8), chunked(gn[1024:2048], 8), chunked(gn[2048:3072], 8)], axis=1)
    return {
        "ffc": np.ascontiguousarray(ffb[:, ts]), "ycc": np.ascontiguousarray(ycb[:, ts]),
        "sgB": l1o["sgB"], "sgC": l1o["sgC"], "yAg": l1o["yAg"], "yMg": l1o["yMg"],
        "x": np.ascontiguousarray(x_cur[b][ts]), "fnet_w": inp['fnet_w'][l], "vec3": np.ascontiguousarray(vec3),
        "w_out": inp['w_out'][l],
        "post_g_bc": np.ascontiguousarray(np.broadcast_to(inp['post_norm_g'][l], (128, 2048))),
    }

_PROGS = {}


def _prog(name, builder):
    if name not in _PROGS:
        nc = bass.Bass("TRN2", target_bir_lowering=False)
        builder(nc)
        _PROGS[name] = nc
    return _PROGS[name]


def kernel(**inputs):
    inp = {k: np.asarray(v) for k, v in inputs.items()}
    x_cur = np.ascontiguousarray(inp['x'], dtype=np.float32)
    cores = list(range(8))
    nc1 = _prog("L1", build_L1); nc2 = _prog("L2", build_L2); nc3 = _prog("L3", build_L3)
    for l in range(2):
        r1 = run_bass_kernel_spmd(nc1, [l1_inputs(inp, l, c, x_cur) for c in cores], core_ids=cores).results
        Zb = [np.concatenate([np.asarray(r1[4 * b + j]["Z"]) for j in range(4)], axis=1) for b in range(2)]
        hyb = [np.concatenate([np.asarray(r1[4 * b + j]["hyc"]) for j in range(4)], axis=1) for b in range(2)]
        r2 = run_bass_kernel_spmd(nc2, [l2_inputs(inp, l, c, Zb[c // 4], hyb[c // 4]) for c in cores], core_ids=cores).results
        ffb = [np.concatenate([np.asarray(r2[4 * b + j]["ff"]) for j in range(4)], axis=0) for b in range(2)]
        ycb = [np.concatenate([np.asarray(r2[4 * b + j]["yc"]) for j in range(4)], axis=0) for b in range(2)]
        l1o = [{k: np.asarray(r1[c][k]) for k in ("sgB", "sgC", "yAg", "yMg")} for c in cores]
        r3 = run_bass_kernel_spmd(nc3, [l3_inputs(inp, l, c, x_cur, ffb[c // 4], ycb[c // 4], l1o[c]) for c in cores],
                                  core_ids=cores).results
        x_cur = np.stack([np.concatenate([np.asarray(r3[4 * b + j]["xo"]) for j in range(4)], axis=0) for b in range(2)], axis=0)
    return np.ascontiguousarray(x_cur, dtype=np.float32)
```

```python
import math
import ml_dtypes
import numpy as np
import concourse.bass as bass
import concourse.mybir as mybir
from concourse.bass_utils import run_bass_kernel_spmd

F32 = mybir.dt.float32
BF16 = mybir.dt.bfloat16
I32 = mybir.dt.int32
AF = mybir.ActivationFunctionType
ALU = mybir.AluOpType
AX = mybir.AxisListType


class Buf:
    __slots__ = ("name", "w", "r")

    def __init__(self, name=""):
        self.name = name
        self.w = None
        self.r = {}


class Sched:
    SEM_LIMIT = 30000

    def __init__(self, nc, n_dma_sems=40, same_engine_sync=True):
        self.nc = nc
        self.engs = {"pe": nc.tensor, "act": nc.scalar, "dve": nc.vector,
                     "pool": nc.gpsimd, "sp": nc.sync}
        self.same_engine_sync = same_engine_sync
        self.sems = {}
        self.cur = {}
        self.nsem = 0
        for e in self.engs:
            self._new_eng_sem(e)
        self.dma_sems = []
        for i in range(n_dma_sems):
            k = ("dma", i)
            self.sems[k] = nc.alloc_semaphore(f"dq{i}")
            self.dma_sems.append([k, 0])
        self.dma_rr = 0
        self.waited = {e: {} for e in self.engs}
        self.n_inst = {e: 0 for e in self.engs}
        self.n_wait = {e: 0 for e in self.engs}

    def _new_eng_sem(self, e):
        k = (e, self.nsem)
        self.nsem += 1
        self.sems[k] = self.nc.alloc_semaphore(f"s_{e}_{k[1]}")
        self.cur[e] = [k, 0]

    def _wait(self, e, tok):
        if tok is None:
            return
        k, v = tok
        if not self.same_engine_sync and k[0] == e:
            return
        if e == "pe" and k[0] == "pe":
            return
        if self.waited[e].get(k, 0) >= v:
            return
        self.engs[e].wait_ge(self.sems[k], v)
        self.waited[e][k] = v
        self.n_wait[e] += 1

    def _deps(self, e, reads, writes):
        for b in reads:
            self._wait(e, b.w)
        for b in writes:
            self._wait(e, b.w)
            for k, v in b.r.items():
                self._wait(e, (k, v))

    def _mark(self, tok, reads, writes):
        k, v = tok
        for b in reads:
            if b.r.get(k, 0) < v:
                b.r[k] = v
        for b in writes:
            b.w = tok
            b.r = {}

    def op(self, e, fn, reads=(), writes=()):
        self._deps(e, reads, writes)
        ins = fn(self.engs[e])
        c = self.cur[e]
        c[1] += 1
        ins.then_inc(self.sems[c[0]], 1)
        tok = (c[0], c[1])
        self._mark(tok, reads, writes)
        self.n_inst[e] += 1
        if c[1] >= self.SEM_LIMIT:
            self._new_eng_sem(e)
        return tok

    def dma(self, q, out, in_, reads=(), writes=(), **kw):
        self._deps(q, reads, writes)
        slot = self.dma_sems[self.dma_rr]
        self.dma_rr = (self.dma_rr + 1) % len(self.dma_sems)
        k, uses = slot
        if uses > 0:
            self._wait(q, (k, 16 * uses))
        ins = self.engs[q].dma_start(out=out, in_=in_, **kw)
        slot[1] = uses + 1
        ins.then_inc(self.sems[k], 16)
        tok = (k, 16 * (uses + 1))
        self._mark(tok, reads, writes)
        self.n_inst[q] += 1
        return tok

    def finish(self, bufs, e="sp"):
        for b in bufs:
            self._wait(e, b.w)

D = 2048
NIN = 11264
G = 1024
TB = 1024
H = 16
TBH = TB + 2 * H
NB = 2
EPS = 1e-6
C_AVAL, C_AGATE, C_F, C_HY, C_Q, C_GATE = 0, 1024, 2048, 3072, 6144, 7168


class Pool_:
    def __init__(self, nc, name, n, shape, dtype, psum=False):
        self.t = []
        for i in range(n):
            if psum:
                h = nc.alloc_psum_tensor(f"{name}{i}", shape, dtype)
            else:
                h = nc.alloc_sbuf_tensor(f"{name}{i}", shape, dtype)
            self.t.append((h, Buf(f"{name}{i}")))
        self.i = 0

    def get(self):
        r = self.t[self.i]
        self.i = (self.i + 1) % len(self.t)
        return r


def build_L1(nc):
    S = Sched(nc)
    dt_in = lambda name, shape, dt=F32: nc.dram_tensor(name, shape, dt, kind="ExternalInput").ap()
    dt_out = lambda name, shape, dt=BF16: nc.dram_tensor(name, shape, dt, kind="ExternalOutput").ap()
    xp = dt_in("xp", [NB, TBH, D])
    w_in = dt_in("w_in", [D, NIN])
    pre_g_bc = dt_in("pre_g_bc", [128, D])
    cw = dt_in("conv_w", [128, 8, 31])
    cvec = dt_in("conv_vec", [128, 5, 8])
    pw = dt_in("conv_pw_w", [G, G])
    hyw = dt_in("hy_w", [128, 24, 4])
    memx = dt_in("mem", [256, D])
    mem_g_bc = dt_in("mem_g_bc", [128, D])
    wk = dt_in("mem_wk", [D, G])
    wv = dt_in("mem_wv", [D, G])
    gD = dt_in("gD", [128, 8])
    dftG = dt_in("dftG", [G, 2 * G])
    o_yA = dt_out("yAg", [G, NB * TB])
    o_yM = dt_out("yMg", [G, NB * TB])
    o_sgB = dt_out("sgB", [G, NB * TB])
    o_sgC = dt_out("sgC", [G, NB * TB])
    o_Z = dt_out("Z", [2 * G, NB * TB])
    o_hy = dt_out("hyc", [3 * G, NB * TB])
    outs_b = []

    sb = lambda name, shape, dt=F32: nc.alloc_sbuf_tensor(name, shape, dt)
    ident = sb("ident", [128, 128], BF16); b_ident = Buf()
    ones_f = sb("ones_f", [128, 128], F32); b_ones_f = Buf()
    ones_b = sb("ones_b", [128, 128], BF16); b_ones_b = Buf()
    S.op("pool", lambda e: e.memset(ident[:], 1.0), writes=[b_ident])
    S.op("pool", lambda e: e.affine_select(ident[:], ident[:], pattern=[[-1, 128]], compare_op=ALU.is_equal,
                                           fill=0.0, base=0, channel_multiplier=1), reads=[b_ident], writes=[b_ident])
    S.op("pool", lambda e: e.memset(ones_f[:], 1.0), writes=[b_ones_f])
    S.op("pool", lambda e: e.memset(ones_b[:], 1.0), writes=[b_ones_b])
    g_bc = sb("g_bc", [128, D]); b_gbc = Buf()
    cw_s = sb("cw_s", [128, 8, 31]); cvec_s = sb("cvec_s", [128, 5, 8]); hyw_s = sb("hyw_s", [128, 24, 4])
    gD_s = sb("gD_s", [128, 8])
    b_par = Buf()
    S.dma("sp", cw_s[:], cw, writes=[b_par])
    S.dma("sp", cvec_s[:], cvec, writes=[b_par])
    S.dma("sp", hyw_s[:], hyw, writes=[b_par])
    S.dma("sp", gD_s[:], gD, writes=[b_par])

    psum = Pool_(nc, "ps", 8, [128, 512], F32, psum=True)
    wst = Pool_(nc, "wst", 2, [128, 16, 128], F32)
    wbf = Pool_(nc, "wbf", 3, [128, 16, 128], BF16)
    xs_p = Pool_(nc, "xs", 2, [128, D], F32)
    hb_p = Pool_(nc, "hb", 2, [128, D], BF16)
    st_p = Pool_(nc, "st", 4, [128, 4], F32)
    hT = sb("hT", [128, 16, TBH], BF16); b_hT = Buf()

    pending = []
    tick = [0]

    def flush(delay):
        while pending and pending[0][0] + delay <= tick[0]:
            _, dst, ob, bob = pending.pop(0)
            b = Buf()
            outs_b.append(b)
            S.dma("sp", dst, ob[:], reads=[bob], writes=[b])

    def out_dma(dst, ob, bob):
        pending.append((tick[0], dst, ob, bob))

    def stream_w(dram, c0, nk):
        tick[0] += 1
        flush(2)
        st, bst = wst.get()
        S.dma("sp", st[:, 0:nk, :], dram.rearrange("(k p) c -> p k c", p=128)[:, :, c0:c0 + 128], writes=[bst])
        wb, bwb = wbf.get()
        S.op("pool", lambda e: e.tensor_copy(wb[:, 0:nk, :], st[:, 0:nk, :]), reads=[bst], writes=[bwb])
        return wb, bwb

    def rms_rows(xs, bxs, rows, gtile, bg, out_bf, bout):
        stt, bstt = st_p.get()
        S.op("dve", lambda e: e.memset(stt[:], 0.0), writes=[bstt])
        S.op("act", lambda e: e.activation(out_bf[:rows], xs[:rows], AF.Square, accum_out=stt[:rows, 0:1]),
             reads=[bxs, bstt], writes=[bout, bstt])
        S.op("dve", lambda e: e.tensor_scalar(stt[:rows, 1:2], stt[:rows, 0:1], 1.0 / D, EPS, ALU.mult, ALU.add),
             reads=[bstt], writes=[bstt])
        S.op("dve", lambda e: e.reciprocal(stt[:rows, 2:3], stt[:rows, 1:2]), reads=[bstt], writes=[bstt])
        S.op("act", lambda e: e.activation(stt[:rows, 2:3], stt[:rows, 2:3], AF.Sqrt), reads=[bstt], writes=[bstt])
        S.op("dve", lambda e: e.scalar_tensor_tensor(out_bf[:rows], xs[:rows], stt[:rows, 2:3], gtile[:rows],
                                                     ALU.mult, ALU.mult), reads=[bxs, bstt, bg], writes=[bout])

    def transpose_into(src_bf, bsrc, rows, dstT, bdst, col0):
        for half in range(2):
            pt, bpt = psum.get()
            ptb = pt[:].bitcast(BF16)
            for kk in range(8):
                k = half * 8 + kk
                S.op("pe", lambda e, k=k, kk=kk: e.transpose(ptb[:, kk * 128:kk * 128 + rows],
                                                             src_bf[:rows, k * 128:(k + 1) * 128], ident[:rows, :rows]),
                     reads=[bsrc, b_ident], writes=[bpt])
            S.op("act", lambda e: e.copy(dstT[:, half * 8:half * 8 + 8, col0:col0 + rows],
                                         ptb.rearrange("p (k t) -> p k t", k=8)[:, :, 0:rows]),
                 reads=[bpt], writes=[bdst])

    cvb = sb("cvb", [128, 8, TB], BF16); b_cv = [Buf() for _ in range(8)]
    memT = cvb[:, 0:4, :].rearrange("p a (b m) -> p (a b) m", m=256); b_memT = Buf()
    mg_bc, b_mg = g_bc, b_gbc
    S.dma("sp", mg_bc[:], mem_g_bc, writes=[b_mg])
    for i in range(2):
        xs, bxs = xs_p.get()
        S.dma("sp", xs[:], memx[i * 128:(i + 1) * 128, :], writes=[bxs])
        hb, bhb = hb_p.get()
        rms_rows(xs, bxs, 128, mg_bc, b_mg, hb, bhb)
        transpose_into(hb, bhb, 128, memT, b_memT, i * 128)
    kT = sb("kT", [128, 8, 256], BF16); b_kT = Buf()
    vS = sb("vS", [128, 2, G], BF16); b_vS = Buf()
    for j in range(8):
        wb, bwb = stream_w(wk, j * 128, 16)
        pt, bpt = psum.get()
        for k in range(16):
            S.op("pe", lambda e, k=k: e.matmul(pt[:, 0:256], lhsT=wb[:, k, :], rhs=memT[:, k, :], start=(k == 0), stop=(k == 15)),
                 reads=[bwb, b_memT], writes=[bpt])
        S.op("act", lambda e: e.copy(kT[:, j, :], pt[:, 0:256]), reads=[bpt], writes=[b_kT])
    for j in range(8):
        wb, bwb = stream_w(wv, j * 128, 16)
        pt, bpt = psum.get()
        for m in range(2):
            for k in range(16):
                S.op("pe", lambda e, k=k, m=m: e.matmul(pt[:, m * 128:(m + 1) * 128], lhsT=memT[:, k, m * 128:(m + 1) * 128],
                                                        rhs=wb[:, k, :], start=(k == 0), stop=(k == 15)),
                     reads=[bwb, b_memT], writes=[bpt])
        S.op("act", lambda e: e.copy(vS[:, :, j * 128:(j + 1) * 128], pt[:, 0:256].rearrange("p (m c) -> p m c", m=2)),
             reads=[bpt], writes=[b_vS])

    S.dma("sp", g_bc[:], pre_g_bc, reads=[], writes=[b_gbc])
    yAb = sb("yAb", [128, 8, TB], BF16); b_yA = [Buf() for _ in range(8)]
    fT = sb("fT", [128, 8, TB], BF16); b_fT = Buf()
    s1 = sb("s1", [128, TB]); b_s1 = Buf()
    s2 = sb("s2", [128, TB]); b_s2 = Buf()
    u_p = Pool_(nc, "u", 2, [128, TBH], F32)
    sig_p = Pool_(nc, "sig", 1, [128, TBH], F32)
    acc_p = Pool_(nc, "acc", 2, [128, TB], F32)
    sq_p = Pool_(nc, "sq", 2, [128, TB], F32)
    ob_p = Pool_(nc, "ob", 4, [128, TB], BF16)
    eT_p = Pool_(nc, "eT", 2, [128, 2, 512], BF16)
    rden_p = Pool_(nc, "rden", 2, [128, 512], F32)

    def inproj(col0, halo):
        wb, bwb = stream_w(w_in, col0, 16)
        res = []
        if halo:
            chunks = [(i * 352, 352) for i in range(3)]
        else:
            chunks = [(H + i * 512, 512) for i in range(2)]
        for (t0, n) in chunks:
            pt, bpt = psum.get()
            for k in range(16):
                S.op("pe", lambda e, k=k, t0=t0, n=n, pt=pt: e.matmul(pt[:, 0:n], lhsT=wb[:, k, :], rhs=hT[:, k, t0:t0 + n],
                                                                       start=(k == 0), stop=(k == 15)),
                     reads=[bwb, b_hT], writes=[bpt])
            res.append((pt, bpt, t0, n))
        return res

    def colsum_acc(src, bsrc, acc, bacc, first):
        for hh in range(2):
            pt, bpt = psum.get()
            S.op("pe", lambda e: e.matmul(pt[:], lhsT=ones_f[:], rhs=src[:, hh * 512:(hh + 1) * 512], start=True, stop=True),
                 reads=[b_ones_f, bsrc], writes=[bpt])
            if first:
                S.op("dve", lambda e: e.tensor_copy(acc[:, hh * 512:(hh + 1) * 512], pt[:]), reads=[bpt], writes=[bacc])
            else:
                S.op("dve", lambda e: e.tensor_tensor(acc[:, hh * 512:(hh + 1) * 512], acc[:, hh * 512:(hh + 1) * 512], pt[:], ALU.add),
                     reads=[bpt, bacc], writes=[bacc])

    def rstd_from(acc, bacc, n):
        S.op("dve", lambda e: e.tensor_scalar(acc[:], acc[:], 1.0 / n, EPS, ALU.mult, ALU.add), reads=[bacc], writes=[bacc])
        S.op("dve", lambda e: e.reciprocal(acc[:], acc[:]), reads=[bacc], writes=[bacc])
        S.op("act", lambda e: e.activation(acc[:], acc[:], AF.Sqrt), reads=[bacc], writes=[bacc])

    def gated_out(ysrc, bys, j, gcol, rstd, brstd, gate_col0, odram, blk):
        res = inproj(gate_col0 + j * 128, False)
        ob, bob = ob_p.get()
        sg, bsg = sq_p.get()
        for (pt, bpt, t0, n) in res:
            o = t0 - H
            S.op("act", lambda e, pt=pt, o=o: e.activation(sg[:, o:o + 512], pt[:], AF.Silu), reads=[bpt], writes=[bsg])
        tmp, btmp = acc_p.get()
        S.op("dve", lambda e: e.scalar_tensor_tensor(tmp[:], ysrc, gcol, rstd[:], ALU.mult, ALU.mult),
             reads=[bys, brstd, b_par], writes=[btmp])
        S.op("dve", lambda e: e.tensor_tensor(ob[:], tmp[:], sg[:], ALU.mult), reads=[btmp, bsg], writes=[bob])
        out_dma(odram[j * 128:(j + 1) * 128, blk * TB:(blk + 1) * TB], ob, bob)

    for blk in range(NB):
        for i in range(9):
            rows = 128 if i < 8 else TBH - 8 * 128
            xs, bxs = xs_p.get()
            S.dma("sp", xs[:rows], xp[blk, i * 128:i * 128 + rows, :], writes=[bxs])
            hb, bhb = hb_p.get()
            rms_rows(xs, bxs, rows, g_bc, b_gbc, hb, bhb)
            transpose_into(hb, bhb, rows, hT, b_hT, i * 128)

        for j in range(24):
            res = inproj(C_HY + j * 128, True)
            ob, bob = ob_p.get()
            tmp, btmp = acc_p.get()
            u, bu = u_p.get()
            for (pt, bpt, t0, n) in res:
                S.op("act", lambda e, pt=pt, t0=t0, n=n: e.copy(u[:, t0:t0 + n], pt[:, 0:n]), reads=[bpt], writes=[bu])
            S.op("dve", lambda e: e.tensor_scalar(tmp[:], u[:, H - 1:H - 1 + TB], hyw_s[:, j, 0:1], hyw_s[:, j, 3:4], ALU.mult, ALU.add),
                 reads=[bu, b_par], writes=[btmp])
            S.op("dve", lambda e: e.scalar_tensor_tensor(tmp[:], u[:, H:H + TB], hyw_s[:, j, 1:2], tmp[:], ALU.mult, ALU.add),
                 reads=[bu, b_par, btmp], writes=[btmp])
            S.op("dve", lambda e: e.scalar_tensor_tensor(ob[:], u[:, H + 1:H + 1 + TB], hyw_s[:, j, 2:3], tmp[:], ALU.mult, ALU.add),
                 reads=[bu, b_par, btmp], writes=[bob])
            out_dma(o_hy[j * 128:(j + 1) * 128, blk * TB:(blk + 1) * TB], ob, bob)

        for gi, odram in ((1, o_sgB), (2, o_sgC)):
            for j in range(8):
                res = inproj(C_GATE + gi * G + j * 128, False)
                ob, bob = ob_p.get()
                for (pt, bpt, t0, n) in res:
                    o = t0 - H
                    S.op("act", lambda e, pt=pt, o=o: e.activation(ob[:, o:o + 512], pt[:], AF.Silu), reads=[bpt], writes=[bob])
                out_dma(odram[j * 128:(j + 1) * 128, blk * TB:(blk + 1) * TB], ob, bob)

        for j in range(8):
            res = inproj(C_F + j * 128, False)
            for (pt, bpt, t0, n) in res:
                o = t0 - H
                S.op("act", lambda e, pt=pt, o=o: e.copy(fT[:, j, o:o + 512], pt[:]), reads=[bpt], writes=[b_fT])
        for j in range(16):
            wb, bwb = stream_w(dftG, j * 128, 8)
            ob, bob = ob_p.get()
            for hh in range(2):
                pt, bpt = psum.get()
                for k in range(8):
                    S.op("pe", lambda e, k=k: e.matmul(pt[:], lhsT=wb[:, k, :], rhs=fT[:, k, hh * 512:(hh + 1) * 512],
                                                       start=(k == 0), stop=(k == 7)), reads=[bwb, b_fT], writes=[bpt])
                S.op("act", lambda e: e.copy(ob[:, hh * 512:(hh + 1) * 512], pt[:]), reads=[bpt], writes=[bob])
            out_dma(o_Z[j * 128:(j + 1) * 128, blk * TB:(blk + 1) * TB], ob, bob)

        for i in range(8):
            resg = inproj(C_AGATE + i * 128, True)
            sig, bsig = sig_p.get()
            for (pt, bpt, t0, n) in resg:
                S.op("act", lambda e, pt=pt, t0=t0, n=n: e.activation(sig[:, t0:t0 + n], pt[:, 0:n], AF.Sigmoid), reads=[bpt], writes=[bsig])
            resv = inproj(C_AVAL + i * 128, True)
            u, bu = u_p.get()
            for (pt, bpt, t0, n) in resv:
                S.op("dve", lambda e, pt=pt, t0=t0, n=n: e.tensor_tensor(u[:, t0:t0 + n], pt[:, 0:n], sig[:, t0:t0 + n], ALU.mult),
                     reads=[bpt, bsig], writes=[bu])
            acc, bacc = acc_p.get()
            S.op("dve", lambda e: e.tensor_scalar(acc[:], u[:, H - 15:H - 15 + TB], cw_s[:, i, 0:1], cvec_s[:, 0, i:i + 1], ALU.mult, ALU.add),
                 reads=[bu, b_par], writes=[bacc])
            for tap in range(1, 31):
                S.op("dve", lambda e, tap=tap: e.scalar_tensor_tensor(acc[:], u[:, H - 15 + tap:H - 15 + tap + TB], cw_s[:, i, tap:tap + 1], acc[:],
                                                                      ALU.mult, ALU.add), reads=[bu, b_par, bacc], writes=[bacc])
            sq, bsq = sq_p.get()
            S.op("act", lambda e: e.activation(sq[:], acc[:], AF.Square), reads=[bacc], writes=[bsq])
            S.op("act", lambda e: e.copy(cvb[:, i, :], acc[:]), reads=[bacc], writes=[b_cv[i], b_memT])
            colsum_acc(acc, bacc, s1, b_s1, i == 0)
            colsum_acc(sq, bsq, s2, b_s2, i == 0)
        S.op("dve", lambda e: e.tensor_scalar(s1[:], s1[:], 1.0 / G, None, ALU.mult), reads=[b_s1], writes=[b_s1])
        sq, bsq = sq_p.get()
        S.op("dve", lambda e: e.tensor_tensor(sq[:], s1[:], s1[:], ALU.mult), reads=[b_s1], writes=[bsq])
        S.op("dve", lambda e: e.scalar_tensor_tensor(s2[:], s2[:], 1.0 / G, sq[:], ALU.mult, ALU.subtract), reads=[b_s2, bsq], writes=[b_s2])
        S.op("dve", lambda e: e.tensor_scalar(s2[:], s2[:], EPS, None, ALU.add), reads=[b_s2], writes=[b_s2])
        S.op("dve", lambda e: e.reciprocal(s2[:], s2[:]), reads=[b_s2], writes=[b_s2])
        S.op("act", lambda e: e.activation(s2[:], s2[:], AF.Sqrt), reads=[b_s2], writes=[b_s2])
        for i in range(8):
            tmp, btmp = acc_p.get()
            S.op("dve", lambda e: e.tensor_tensor(tmp[:], cvb[:, i, :], s1[:], ALU.subtract), reads=[b_cv[i], b_s1], writes=[btmp])
            S.op("dve", lambda e: e.tensor_tensor(tmp[:], tmp[:], s2[:], ALU.mult), reads=[btmp, b_s2], writes=[btmp])
            S.op("act", lambda e: e.activation(cvb[:, i, :], tmp[:], AF.Silu, scale=cvec_s[:, 1, i:i + 1], bias=cvec_s[:, 2, i:i + 1]),
                 reads=[btmp, b_par], writes=[b_cv[i]])
        for j in range(8):
            wb, bwb = stream_w(pw, j * 128, 8)
            ya, bya = acc_p.get()
            for hh in range(2):
                pt, bpt = psum.get()
                for k in range(8):
                    S.op("pe", lambda e, k=k: e.matmul(pt[:], lhsT=wb[:, k, :], rhs=cvb[:, k, hh * 512:(hh + 1) * 512], start=(k == 0), stop=(k == 7)),
                         reads=[bwb, b_cv[k]], writes=[bpt])
                S.op("act", lambda e: e.activation(ya[:, hh * 512:(hh + 1) * 512], pt[:], AF.Identity, bias=cvec_s[:, 3, j:j + 1]),
                     reads=[bpt, b_par], writes=[bya])
            sq, bsq = sq_p.get()
            S.op("act", lambda e: e.activation(sq[:], ya[:], AF.Square), reads=[bya], writes=[bsq])
            S.op("dve", lambda e: e.tensor_copy(yAb[:, j, :], ya[:]), reads=[bya], writes=[b_yA[j]])
            colsum_acc(sq, bsq, s1, b_s1, j == 0)
        rstd_from(s1, b_s1, G)
        for j in range(8):
            gated_out(yAb[:, j, :], b_yA[j], j, cvec_s[:, 4, j:j + 1], s1, b_s1, C_GATE + 0 * G, o_yA, blk)

        for j in range(8):
            res = inproj(C_Q + j * 128, False)
            for (pt, bpt, t0, n) in res:
                o = t0 - H
                S.op("act", lambda e, pt=pt, o=o: e.copy(fT[:, j, o:o + 512], pt[:]), reads=[bpt], writes=[b_fT])
        first = True
        for hd in range(4):
            for hh in range(2):
                eT, beT = eT_p.get()
                for m in range(2):
                    pt, bpt = psum.get()
                    for dc in range(2):
                        S.op("pe", lambda e, dc=dc, m=m: e.matmul(pt[:], lhsT=kT[:, hd * 2 + dc, m * 128:(m + 1) * 128],
                                                                  rhs=fT[:, hd * 2 + dc, hh * 512:(hh + 1) * 512], start=(dc == 0), stop=(dc == 1)),
                             reads=[b_kT, b_fT], writes=[bpt])
                    S.op("act", lambda e, m=m: e.activation(eT[:, m, :], pt[:], AF.Exp, scale=1.0 / 16.0), reads=[bpt], writes=[beT])
                pd, bpd = psum.get()
                for m in range(2):
                    S.op("pe", lambda e, m=m: e.matmul(pd[:], lhsT=ones_b[:], rhs=eT[:, m, :], start=(m == 0), stop=(m == 1)),
                         reads=[b_ones_b, beT], writes=[bpd])
                rd, brd = rden_p.get()
                S.op("dve", lambda e: e.reciprocal(rd[:], pd[:]), reads=[bpd], writes=[brd])
                for cc in range(2):
                    j = hd * 2 + cc
                    pt, bpt = psum.get()
                    for m in range(2):
                        S.op("pe", lambda e, m=m: e.matmul(pt[:], lhsT=vS[:, m, j * 128:(j + 1) * 128], rhs=eT[:, m, :], start=(m == 0), stop=(m == 1)),
                             reads=[b_vS, beT], writes=[bpt])
                    S.op("dve", lambda e, j=j: e.tensor_tensor(yAb[:, j, hh * 512:(hh + 1) * 512], pt[:], rd[:], ALU.mult),
                         reads=[bpt, brd], writes=[b_yA[j]])
        for j in range(8):
            sq, bsq = sq_p.get()
            S.op("act", lambda e: e.activation(sq[:], yAb[:, j, :], AF.Square), reads=[b_yA[j]], writes=[bsq])
            colsum_acc(sq, bsq, s1, b_s1, j == 0)
        rstd_from(s1, b_s1, G)
        for j in range(8):
            gated_out(yAb[:, j, :], b_yA[j], j, gD_s[:, j:j + 1], s1, b_s1, C_GATE + 3 * G, o_yM, blk)

    flush(0)
    S.finish(outs_b, "sp")
    return S
import math

L = 8192
NCH = 128
CB = 8
NBATCH = NCH // CB
N16 = 16384
import os
SES = bool(int(os.environ.get("SES", "1")))

CO = {}
_o = 0
for _n, _w in (("FA", 256), ("FAhi", 256), ("C", 128), ("S", 128), ("nS", 128), ("IA1", 256), ("IA2", 256),
               ("IBc", 64), ("IBs", 64), ("FN1", 128), ("FN2", 128), ("FAi", 256), ("IBsp", 64)):
    CO[_n] = (_o, _w)
    _o += _w
CBF_W = _o
CF = {"T16r": (0, 128), "T16i": (128, 128), "T16ci": (256, 128), "T8r": (384, 64), "T8i": (448, 64)}
CF_W = 512


def build_L2(nc):
    S = Sched(nc, same_engine_sync=SES)
    dt_in = lambda name, shape, dt=F32: nc.dram_tensor(name, shape, dt, kind="ExternalInput").ap()
    dt_out = lambda name, shape, dt=BF16: nc.dram_tensor(name, shape, dt, kind="ExternalOutput").ap()
    zr_d = dt_in("zr", [2 * NCH, L], BF16); zi_d = dt_in("zi", [2 * NCH, L], BF16)
    hv_d = dt_in("hv", [2 * NCH, L], BF16); hx1_d = dt_in("hx1", [2 * NCH, L], BF16); hx2_d = dt_in("hx2", [2 * NCH, L], BF16)
    pos_rep = dt_in("pos_rep", [2, 32, L], I32)
    pos_t = dt_in("pos_t", [2, 64, 128], I32)
    bsc = dt_in("bsc", [32, 2])
    fw1a = dt_in("fw1a", [1, 64]); fw1b = dt_in("fw1b", [32, 64])
    mvec = dt_in("mvec", [64, 4])
    fw2 = dt_in("fw2", [64, 64])
    fw3c = dt_in("fw3c", [64, 2, NBATCH, 2, CB])
    dec_rep = dt_in("dec_rep", [64, 2, NBATCH, 2, CB])
    skip_rep = dt_in("skip_rep", [64, 2, NCH])
    cbf_d = dt_in("cbf", [128, CBF_W], BF16)
    cf_d = dt_in("cf", [128, CF_W])
    o_ff = dt_out("ff", [2 * NCH, L])
    o_yc = dt_out("yc", [2 * NCH, L])
    outs_b = []

    sb = lambda name, shape, dt=F32: nc.alloc_sbuf_tensor(name, shape, dt)
    cbf = sb("cbf_s", [128, CBF_W], BF16); cf = sb("cf_s", [128, CF_W]); b_c = Buf()
    S.dma("sp", cbf[:], cbf_d, writes=[b_c])
    S.dma("sp", cf[:], cf_d, writes=[b_c])
    cm = lambda n, rows=128: cbf[0:rows, CO[n][0]:CO[n][0] + CO[n][1]]
    cfm = lambda n: cf[:, CF[n][0]:CF[n][0] + CF[n][1]]
    ones_f = sb("ones_f", [64, 64]); b_ones = Buf()
    S.op("pool", lambda e: e.memset(ones_f[:], 1.0), writes=[b_ones])

    psum = Pool_(nc, "ps", 8, [128, 512], F32, psum=True)

    h2 = [sb("h2f", [64, L], BF16), sb("h2b", [64, L], BF16)]; b_h2 = [Buf(), Buf()]
    fw3s = sb("fw3s", [64, 2, NBATCH, 2 * CB], BF16); b_fw3 = Buf()
    dec = sb("dec", [64, 2, NBATCH, 2 * CB]); b_dec = Buf()
    skp = sb("skp", [64, 2, NCH]); b_skp = Buf()
    tpos = sb("tpos", [64, 2, 128]); b_tpos = Buf()
    S.dma("sp", skp[:], skip_rep, writes=[b_skp])
    S.dma("sp", dec[:], dec_rep.rearrange("p d b o c -> p d b (o c)"), writes=[b_dec])
    par = sb("mlp_par", [64, 4 + 64 + 64 + 64 + 2]); b_par = Buf()
    fq = sb("fq", [64, 2]); b_fq = Buf()
    f3st = sb("f3st", [64, 2, NBATCH, 2 * CB]); b_f3st = Buf()
    tpi = sb("tpi", [64, 2, 128], I32); b_tpi = Buf()
    b_ft = Buf()
    MW = 2048

    with nc.sbuf_tensor("mlp_tmp", [64, 16384], F32) as mt, nc.sbuf_tensor("fr_i", [64, MW], I32) as fr_i, \
            nc.sbuf_tensor("fr_f", [64, MW], F32) as fr_f:
        b_mt = Buf(); b_fr = Buf()

        def frac_neg(a, rows, bufs):
            S.op("dve", lambda e: e.tensor_copy(fr_i[0:rows, :], a), reads=bufs, writes=[b_fr])
            S.op("dve", lambda e: e.tensor_copy(fr_f[0:rows, :], fr_i[0:rows, :]), reads=[b_fr], writes=[b_fr])
            S.op("dve", lambda e: e.tensor_tensor(a, a, fr_f[0:rows, :], ALU.subtract), reads=bufs + [b_fr], writes=bufs)
            S.op("dve", lambda e: e.scalar_tensor_tensor(a, a, 0.5, a, ALU.is_gt, ALU.subtract), reads=bufs, writes=bufs)

        posi = mt[0:32, 0:8192].bitcast(I32)
        posi_t = mt[32:33, 0:8192].bitcast(I32)
        posf = mt[0:32, 8192:16384]
        feat_t = mt[32:33, 0:8192]
        S.dma("sp", par[:, 0:4], mvec, writes=[b_par])
        S.dma("sp", par[:, 4:68], fw2, writes=[b_par])
        S.dma("sp", par[0:32, 68:132], fw1b, writes=[b_par])
        S.dma("sp", par[32:33, 132:196], fw1a, writes=[b_par])
        S.dma("sp", par[0:32, 196:198], bsc, writes=[b_par])
        S.op("dve", lambda e: e.tensor_scalar(fq[:, 0:1], par[:, 1:2], 1.0 / (2 * math.pi), None, ALU.mult), reads=[b_par], writes=[b_fq])
        S.op("dve", lambda e: e.tensor_scalar(fq[:, 1:2], par[:, 3:4], 1.0 / (2 * math.pi), None, ALU.mult), reads=[b_par], writes=[b_fq])
        S.dma("sp", f3st[:], fw3c.rearrange("j d b o c -> j d b (o c)"), writes=[b_f3st])
        S.op("pool", lambda e: e.tensor_copy(fw3s[:], f3st[:]), reads=[b_f3st], writes=[b_fw3])
        S.dma("sp", tpi[:], pos_t.rearrange("d p s -> p d s"), writes=[b_tpi])
        S.op("dve", lambda e: e.tensor_copy(tpos[:], tpi[:]), reads=[b_tpi], writes=[b_tpos])
        S.op("dve", lambda e: e.tensor_scalar(tpos[:], tpos[:], 1.0 / L, None, ALU.mult), reads=[b_tpos], writes=[b_tpos])
        h1 = mt[0:64, 8192:16384]
        for d in range(2):
            S.dma("sp", posi, pos_rep[d], writes=[b_mt])
            S.dma("sp", posi_t, pos_rep[d, 0:1, :], writes=[b_mt])
            S.op("dve", lambda e: e.tensor_copy(posf, posi), reads=[b_mt], writes=[b_mt])
            S.op("dve", lambda e: e.tensor_copy(mt[32:33, 8192:16384], posi_t), reads=[b_mt], writes=[b_mt])
            S.op("dve", lambda e: e.tensor_scalar(feat_t, mt[32:33, 8192:16384], 1.0 / L, None, ALU.mult), reads=[b_mt], writes=[b_ft])
            S.op("dve", lambda e: e.tensor_scalar(posf, posf, par[0:32, 196:197], par[0:32, 197:198], ALU.mult, ALU.add), reads=[b_mt, b_par], writes=[b_mt])
            feats = mt[0:32, 0:8192]
            for q in range(L // MW):
                ws = slice(q * MW, (q + 1) * MW)
                frac_neg(posf[:, ws], 32, [b_mt])
                S.op("act", lambda e: e.activation(feats[:, ws], posf[:, ws], AF.Sin, scale=-2 * math.pi), reads=[b_mt], writes=[b_mt])
            for q in range(L // MW):
                hs = h1[:, q * MW:(q + 1) * MW]
                for ch in range(MW // 512):
                    sl = slice(q * MW + ch * 512, q * MW + (ch + 1) * 512)
                    pt, bpt = psum.get()
                    S.op("pe", lambda e: e.matmul(pt[0:64, :], lhsT=par[32:33, 132:196], rhs=feat_t[:, sl], start=True, stop=False),
                         reads=[b_par, b_ft], writes=[bpt])
                    S.op("pe", lambda e: e.matmul(pt[0:64, :], lhsT=par[0:32, 68:132], rhs=feats[:, sl], start=False, stop=True),
                         reads=[b_par, b_mt], writes=[bpt])
                    S.op("dve", lambda e: e.tensor_scalar(h1[:, sl], pt[0:64, :], par[:, 0:1], fq[:, 0:1], ALU.add, ALU.mult), reads=[bpt, b_par, b_fq], writes=[b_mt])
                S.op("dve", lambda e: e.tensor_scalar(hs, hs, 8.0, None, ALU.add), reads=[b_mt], writes=[b_mt])
                frac_neg(hs, 64, [b_mt])
                S.op("act", lambda e: e.activation(hs, hs, AF.Sin, scale=-2 * math.pi), reads=[b_mt], writes=[b_mt])
                pts = []
                for ch in range(MW // 512):
                    sl = slice(q * MW + ch * 512, q * MW + (ch + 1) * 512)
                    pt2, bpt2 = psum.get()
                    S.op("pe", lambda e: e.matmul(pt2[0:64, :], lhsT=par[:, 4:68], rhs=h1[:, sl], start=True, stop=True), reads=[b_par, b_mt], writes=[bpt2])
                    pts.append((pt2, bpt2, sl))
                for (pt2, bpt2, sl) in pts:
                    S.op("dve", lambda e: e.tensor_scalar(h1[:, sl], pt2[0:64, :], par[:, 2:3], fq[:, 1:2], ALU.add, ALU.mult), reads=[bpt2, b_par, b_fq], writes=[b_mt])
                S.op("dve", lambda e: e.tensor_scalar(hs, hs, 8.0, None, ALU.add), reads=[b_mt], writes=[b_mt])
                frac_neg(hs, 64, [b_mt])
                S.op("act", lambda e: e.activation(h2[d][:, q * MW:(q + 1) * MW], hs, AF.Sin, scale=-2 * math.pi), reads=[b_mt], writes=[b_h2[d]])
        b_mt_final = Buf()
        b_mt_final.w = b_mt.w
        b_mt_final.r = dict(b_mt.r)
        for k_, v_ in list(b_fr.r.items()) + ([b_fr.w] if b_fr.w else []):
            b_mt_final.r[k_] = max(b_mt_final.r.get(k_, 0), v_)

    S.op("dve", lambda e: e.tensor_scalar(f3st[:], dec[:], -1.0, None, ALU.mult), reads=[b_dec], writes=[b_f3st])
    S.op("dve", lambda e: e.tensor_tensor(dec[:], dec[:], f3st[:], ALU.max), reads=[b_dec, b_f3st], writes=[b_dec])
    Kf = [sb("Kf0", [128, 2, 2, CB, 128]), sb("Kf1", [128, 2, 2, CB, 128])]
    b_Kf = [[Buf(), Buf()], [Buf(), Buf()]]
    stag_p = Pool_(nc, "stag", 3, [128, CB, 256], F32)
    tmp_p = {e: Pool_(nc, "tmp" + e, n_, [128, CB * 128], F32) for e, n_ in (("dve", 3), ("pool", 2))}
    cb_p = Pool_(nc, "cb", 8, [128, CB, 128], BF16)
    xin_p = Pool_(nc, "xin", 12, [64, CB, 128], BF16)
    kt = [sb("ktf", [64, 2 * CB, 128]), sb("ktb", [64, 2 * CB, 128])]; b_kt = [Buf(), Buf()]
    ktb16 = [sb("ktf16", [64, 2 * CB, 128], BF16), sb("ktb16", [64, 2 * CB, 128], BF16)]; b_ktb = [Buf(), Buf()]
    win = sb("win", [64, 2 * CB, 128]); b_win = Buf()
    nrm = sb("nrm", [64, 4, 2 * CB]); b_nrm = Buf()
    yst = [sb("yst_r", [64, CB, 128]), sb("yst_i", [64, CB, 128])]; b_yst = [Buf(), Buf()]
    fo_p = Pool_(nc, "fo", 2, [128, CB, 64], BF16)
    for b in b_Kf[0] + b_Kf[1] + [b_kt[0], b_kt[1], b_ktb[0], b_ktb[1], b_win, b_nrm] + b_yst + \
            [t[1] for t in stag_p.t + tmp_p["dve"].t + tmp_p["pool"].t + cb_p.t + xin_p.t + fo_p.t]:
        b.w = b_mt_final.w
        b.r = dict(b_mt_final.r)

    def out_dma(dst, src, bsrc):
        b = Buf(); outs_b.append(b)
        S.dma("sp", dst, src, reads=[bsrc], writes=[b])

    def stage_data_stationary(chan_mms, nch, width, rows=128):
        st, bst = stag_p.get()
        per = 512 // width
        for c0 in range(0, nch, per):
            pt, bpt = psum.get()
            n = min(per, nch - c0)
            for i in range(n):
                mms = chan_mms(c0 + i)
                for q, (lhsT, rhs, rd) in enumerate(mms):
                    S.op("pe", lambda e, lhsT=lhsT, rhs=rhs, i=i, q=q: e.matmul(pt[0:rows, i * width:(i + 1) * width], lhsT=lhsT, rhs=rhs,
                                                                                 start=(q == 0), stop=(q == len(mms) - 1)),
                         reads=rd + [b_c], writes=[bpt])
            S.op("act", lambda e: e.copy(st[0:rows, c0:c0 + n, 0:width], pt[0:rows, 0:n * width].rearrange("p (c w) -> p c w", c=n)),
                 reads=[bpt], writes=[bst])
        return st, bst

    def cplx_mul(sr, si, tr, ti, rd, n1, nch, rows=128):
        dr, bdr = cb_p.get(); di, bdi = cb_p.get()
        drv = dr[0:rows, 0:nch, 0:n1]; div = di[0:rows, 0:nch, 0:n1]
        ta, bta = tmp_p["dve"].get(); tb, btb = tmp_p["dve"].get()
        tav = ta[0:rows, 0:nch * n1].rearrange("p (c k) -> p c k", c=nch); tbv = tb[0:rows, 0:nch * n1].rearrange("p (c k) -> p c k", c=nch)
        S.op("dve", lambda e: e.tensor_tensor(tav, sr, tr, ALU.mult), reads=rd, writes=[bta])
        S.op("dve", lambda e: e.tensor_tensor(tbv, si, ti, ALU.mult), reads=rd, writes=[btb])
        S.op("dve", lambda e: e.tensor_tensor(drv, tav, tbv, ALU.subtract), reads=[bta, btb], writes=[bdr])
        tc, btc = tmp_p["pool"].get(); td, btd = tmp_p["pool"].get()
        tcv = tc[0:rows, 0:nch * n1].rearrange("p (c k) -> p c k", c=nch); tdv = td[0:rows, 0:nch * n1].rearrange("p (c k) -> p c k", c=nch)
        S.op("pool", lambda e: e.tensor_tensor(tcv, sr, ti, ALU.mult), reads=rd, writes=[btc])
        S.op("pool", lambda e: e.tensor_tensor(tdv, si, tr, ALU.mult), reads=rd, writes=[btd])
        S.op("pool", lambda e: e.tensor_tensor(div, tcv, tdv, ALU.add), reads=[btc, btd], writes=[bdi])
        return (dr, bdr), (di, bdi)

    def bc(tab, nch, n1):
        return tab.unsqueeze(1).broadcast_to([128, nch, n1])

    def stageB_fwd(Ar, Ai, nch, n1, want_imag, evac):
        per = 512 // n1
        for g0 in range(0, nch, per):
            ng = min(per, nch - g0)
            rr = Ar[0][:, g0:g0 + ng, 0:n1]; ri = Ai[0][:, g0:g0 + ng, 0:n1]
            ptr, bptr = psum.get()
            o = ptr[:, 0:ng * n1].rearrange("p (c k) -> p c k", c=ng)
            S.op("pe", lambda e: e.matmul(o, lhsT=cm("C"), rhs=rr, start=True, stop=False), reads=[Ar[1], b_c], writes=[bptr])
            S.op("pe", lambda e: e.matmul(o, lhsT=cm("S"), rhs=ri, start=False, stop=True), reads=[Ai[1], b_c], writes=[bptr])
            pti = bpti = None
            if want_imag:
                pti, bpti = psum.get()
                o2 = pti[:, 0:ng * n1].rearrange("p (c k) -> p c k", c=ng)
                S.op("pe", lambda e: e.matmul(o2, lhsT=cm("C"), rhs=ri, start=True, stop=False), reads=[Ai[1], b_c], writes=[bpti])
                S.op("pe", lambda e: e.matmul(o2, lhsT=cm("nS"), rhs=rr, start=False, stop=True), reads=[Ar[1], b_c], writes=[bpti])
            evac(g0, ng, ptr, bptr, pti, bpti)

    def fwd_fft16k(chan_mms, nch, evac):
        st, bst = stage_data_stationary(chan_mms, nch, 256)
        yield
        Ar, Ai = cplx_mul(st[:, 0:nch, 0:128], st[:, 0:nch, 128:256], bc(cfm("T16r"), nch, 128), bc(cfm("T16i"), nch, 128), [bst, b_c], 128, nch)
        yield
        stageB_fwd(Ar, Ai, nch, 128, True, evac)
        yield

    def load_x(dram, row0):
        t, bt = xin_p.get()
        S.dma("sp", t[:], dram[row0:row0 + CB, :].rearrange("c (s1 s2) -> s1 c s2", s2=128), writes=[bt])
        return t, bt

    def long_conv(xa, xb, o, ga, gb, b):
        Kt = Kf[b % 2]; bK = b_Kf[b % 2][o]
        stB, bstB = stag_p.get()

        def evacB(g0, ng, ptr, bptr, pti, bpti):
            S.op("act", lambda e: e.copy(stB[:, g0:g0 + ng, 0:128], ptr[:, 0:ng * 128].rearrange("p (c k) -> p c k", c=ng)), reads=[bptr], writes=[bstB])
            S.op("act", lambda e: e.copy(stB[:, g0:g0 + ng, 128:256], pti[:, 0:ng * 128].rearrange("p (c k) -> p c k", c=ng)), reads=[bpti], writes=[bstB])
        yield from fwd_fft16k(lambda c: [(xa[0][:, c, :], cm("FA", 64), [xa[1]]), (xb[0][:, c, :], cm("FAi", 64), [xb[1]])], CB, evacB)
        Pr, Pi = cplx_mul(stB[:, :, 0:128], stB[:, :, 128:256], Kt[:, o, 0, :, :], Kt[:, o, 1, :, :], [bstB, bK], 128, CB)
        yield
        st, bst = stage_data_stationary(lambda c: [(Pr[0][:, c, :], cm("IA1"), [Pr[1]]), (Pi[0][:, c, :], cm("IA2"), [Pi[1]])], CB, 256)
        yield
        Br, Bi = cplx_mul(st[:, :, 0:128], st[:, :, 128:256], bc(cfm("T16r"), CB, 128), bc(cfm("T16ci"), CB, 128), [bst, b_c], 128, CB)
        yield
        for g0 in range(0, CB, 4):
            pt, bpt = psum.get()
            o4 = pt[0:64, :].rearrange("p (c k) -> p c k", c=4)
            S.op("pe", lambda e: e.matmul(o4, lhsT=cm("IBc"), rhs=Br[0][:, g0:g0 + 4, :], start=True, stop=False), reads=[Br[1], b_c], writes=[bpt])
            S.op("pe", lambda e: e.matmul(o4, lhsT=cm("IBs"), rhs=Bi[0][:, g0:g0 + 4, :], start=False, stop=True), reads=[Bi[1], b_c], writes=[bpt])
            S.op("act", lambda e: e.copy(yst[0][:, g0:g0 + 4, :], o4), reads=[bpt], writes=[b_yst[0]])
            pt2, bpt2 = psum.get()
            o5 = pt2[0:64, :].rearrange("p (c k) -> p c k", c=4)
            S.op("pe", lambda e: e.matmul(o5, lhsT=cm("IBc"), rhs=Bi[0][:, g0:g0 + 4, :], start=True, stop=False), reads=[Bi[1], b_c], writes=[bpt2])
            S.op("pe", lambda e: e.matmul(o5, lhsT=cm("IBsp"), rhs=Br[0][:, g0:g0 + 4, :], start=False, stop=True), reads=[Br[1], b_c], writes=[bpt2])
            S.op("act", lambda e: e.copy(yst[1][:, g0:g0 + 4, :], o5), reads=[bpt2], writes=[b_yst[1]])
        yield
        res = []
        skb = skp[:, o, b * CB:(b + 1) * CB].unsqueeze(2).broadcast_to([64, CB, 128])
        for h, (xin, gate) in enumerate(((xa, ga), (xb, gb))):
            z, bz = xin_p.get()
            tq, btq = tmp_p["dve"].get()
            tv = tq[0:64, :].rearrange("p (c k) -> p c k", c=CB)
            S.op("dve", lambda e: e.tensor_tensor(tv, xin[0][:], skb, ALU.mult), reads=[xin[1], b_skp], writes=[btq])
            S.op("dve", lambda e: e.tensor_tensor(tv, tv, yst[h][:], ALU.add), reads=[btq, b_yst[h]], writes=[btq])
            S.op("dve", lambda e: e.tensor_tensor(z[:], tv, gate[0][:], ALU.mult), reads=[btq, gate[1]], writes=[bz])
            res.append((z, bz))
        yield
        return res

    def filter_chain(b):
        Kt = Kf[b % 2]
        for d in range(2):
            S.op("dve", lambda e: e.tensor_tensor(win[:], tpos[:, d, :].unsqueeze(1).broadcast_to([64, 2 * CB, 128]),
                                                  dec[:, d, b, :].unsqueeze(2).broadcast_to([64, 2 * CB, 128]), ALU.mult),
                 reads=[b_tpos, b_dec], writes=[b_win])
            S.op("act", lambda e: e.activation(win[:], win[:], AF.Exp, scale=-1.0), reads=[b_win], writes=[b_win])
            for s20 in range(0, 128, 32):
                pt, bpt = psum.get()
                for q in range(32):
                    s2 = s20 + q
                    S.op("pe", lambda e, s2=s2, q=q: e.matmul(pt[0:64, q * 16:(q + 1) * 16], lhsT=h2[d][:, s2:L:128], rhs=fw3s[:, d, b, :],
                                                              start=True, stop=True), reads=[b_h2[d], b_fw3], writes=[bpt])
                S.op("dve", lambda e: e.tensor_tensor(kt[d][:, :, s20:s20 + 32].rearrange("p c s -> p s c"),
                                                      pt[0:64, :].rearrange("p (s c) -> p s c", c=16),
                                                      win[:, :, s20:s20 + 32].rearrange("p c s -> p s c"), ALU.mult),
                     reads=[bpt, b_win], writes=[b_kt[d]])
                yield
            if d == 1:
                S.op("dve", lambda e: e.memset(kt[1][0:1, :, 0:1], 0.0), reads=[], writes=[b_kt[1]])
            S.op("dve", lambda e: e.tensor_tensor(win[:], kt[d][:], kt[d][:], ALU.mult), reads=[b_kt[d]], writes=[b_win])
            S.op("dve", lambda e: e.tensor_reduce(nrm[:, d, :], win[:], axis=AX.X, op=ALU.add), reads=[b_win], writes=[b_nrm])
            yield
        S.op("dve", lambda e: e.tensor_tensor(nrm[:, 2, :], nrm[:, 0, :], nrm[:, 1, :], ALU.add), reads=[b_nrm], writes=[b_nrm])
        pt, bpt = psum.get()
        S.op("pe", lambda e: e.matmul(pt[0:64, 0:2 * CB], lhsT=ones_f[:], rhs=nrm[:, 2, :], start=True, stop=True), reads=[b_ones, b_nrm], writes=[bpt])
        S.op("dve", lambda e: e.tensor_scalar(nrm[:, 3, :], pt[0:64, 0:2 * CB], 1e-6, None, ALU.add), reads=[bpt], writes=[b_nrm])
        S.op("dve", lambda e: e.reciprocal(nrm[:, 3, :], nrm[:, 3, :]), reads=[b_nrm], writes=[b_nrm])
        S.op("act", lambda e: e.activation(nrm[:, 3, :], nrm[:, 3, :], AF.Sqrt), reads=[b_nrm], writes=[b_nrm])
        yield
        for d in range(2):
            S.op("dve", lambda e: e.tensor_tensor(ktb16[d][:], kt[d][:], nrm[:, 3, :].unsqueeze(2).broadcast_to([64, 2 * CB, 128]), ALU.mult),
                 reads=[b_kt[d], b_nrm], writes=[b_ktb[d]])
        yield
        for o in range(2):
            def evacK(g0, ng, ptr, bptr, pti, bpti, o=o):
                S.op("act", lambda e: e.copy(Kt[:, o, 0, g0:g0 + ng, :], ptr[:, 0:ng * 128].rearrange("p (c k) -> p c k", c=ng)), reads=[bptr], writes=[b_Kf[b % 2][o]])
                S.op("act", lambda e: e.copy(Kt[:, o, 1, g0:g0 + ng, :], pti[:, 0:ng * 128].rearrange("p (c k) -> p c k", c=ng)), reads=[bpti], writes=[b_Kf[b % 2][o]])
            yield from fwd_fft16k(lambda c, o=o: [(ktb16[0][:, o * CB + c, :], cm("FA", 64), [b_ktb[0]]),
                                                  (ktb16[1][:, o * CB + c, :], cm("FAhi", 64), [b_ktb[1]])], CB, evacK)

    def conv_chain(b):
        r0 = [b * CB, NCH + b * CB]
        v = [load_x(hv_d, r) for r in r0]
        x1 = [load_x(hx1_d, r) for r in r0]
        x2 = [load_x(hx2_d, r) for r in r0]
        yield
        z1 = yield from long_conv(v[0], v[1], 0, x1[0], x1[1], b)
        z2 = yield from long_conv(z1[0], z1[1], 1, x2[0], x2[1], b)
        for h in range(2):
            out_dma(o_yc[r0[h]:r0[h] + CB, :].rearrange("c (s1 s2) -> s1 c s2", s2=128), z2[h][0][:], z2[h][1])
        yield

    def fnet_chain(b):
        for h in range(2):
            row = h * NCH + b * CB
            zr, bzr = load_x(zr_d, row)
            zi, bzi = load_x(zi_d, row)
            yield
            st, bst = stage_data_stationary(lambda c: [(zr[:, c, :], cm("FN1", 64), [bzr]), (zi[:, c, :], cm("FN2", 64), [bzi])], CB, 128)
            yield
            Ar, Ai = cplx_mul(st[:, :, 0:64], st[:, :, 64:128], bc(cfm("T8r"), CB, 64), bc(cfm("T8i"), CB, 64), [bst, b_c], 64, CB)
            yield
            fo, bfo = fo_p.get()

            def evacF(g0, ng, ptr, bptr, pti, bpti):
                S.op("act", lambda e: e.copy(fo[:, g0:g0 + ng, :], ptr[:, 0:ng * 64].rearrange("p (c k) -> p c k", c=ng)), reads=[bptr], writes=[bfo])
            stageB_fwd(Ar, Ai, CB, 64, False, evacF)
            out_dma(o_ff[row:row + CB, :].rearrange("c (k2 k1) -> k2 c k1", k1=64), fo[:], bfo)
            yield

    def run_all(gens):
        gens = [g for g in gens if g is not None]
        while gens:
            for g in list(gens):
                try:
                    next(g)
                except StopIteration:
                    gens.remove(g)

    run_all([filter_chain(0)])
    for b in range(NBATCH):
        run_all([conv_chain(b), filter_chain(b + 1) if b + 1 < NBATCH else None, fnet_chain(b)])

    S.finish(outs_b, "sp")
    return S

D = 2048
G = 1024
DM = 4096
NT = 2048
TBK = 512
EPS = 1e-6


def build_L3(nc):
    S = Sched(nc)
    dt_in = lambda name, shape, dt=F32: nc.dram_tensor(name, shape, dt, kind="ExternalInput").ap()
    ffc = dt_in("ffc", [G, NT], BF16); ycc = dt_in("ycc", [G, NT], BF16)
    sgB = dt_in("sgB", [G, NT], BF16); sgC = dt_in("sgC", [G, NT], BF16)
    yAg = dt_in("yAg", [G, NT], BF16); yMg = dt_in("yMg", [G, NT], BF16)
    x_d = dt_in("x", [NT, D])
    fnet_w = dt_in("fnet_w", [G, G])
    vec = dt_in("vec3", [128, 3, 8])
    w_out = dt_in("w_out", [DM, D])
    post_g_bc = dt_in("post_g_bc", [128, D])
    xo = nc.dram_tensor("xo", [NT, D], F32, kind="ExternalOutput").ap()
    wsc = nc.dram_tensor("w_out_bf", [DM, D], BF16, kind="Internal").ap()
    outs_b = []

    sb = lambda name, shape, dt=F32: nc.alloc_sbuf_tensor(name, shape, dt)
    ones_f = sb("ones_f", [128, 128]); b_ones = Buf()
    S.op("pool", lambda e: e.memset(ones_f[:], 1.0), writes=[b_ones])
    vec_s = sb("vec_s", [128, 3, 8]); b_vec = Buf()
    S.dma("sp", vec_s[:], vec, writes=[b_vec])
    pg = sb("pg", [128, D]); b_pg = Buf()
    S.dma("sp", pg[:], post_g_bc, writes=[b_pg])
    psum = Pool_(nc, "ps", 8, [128, 512], F32, psum=True)
    fw_bf = sb("fw_bf", [128, 8, G], BF16); b_fw = Buf()

    b_wsc = [Buf() for _ in range(32)]
    with nc.sbuf_tensor("stg0", [128, D], F32) as stg0, nc.sbuf_tensor("stg1", [128, D], F32) as stg1, \
            nc.sbuf_tensor("cst0", [128, D], BF16) as cst0, nc.sbuf_tensor("cst1", [128, D], BF16) as cst1:
        stg = [(stg0, Buf()), (stg1, Buf())]; cst = [(cst0, Buf()), (cst1, Buf())]
        for k in range(8):
            st, bst = stg[k % 2]
            S.dma("sp", st[:, 0:G], fnet_w[k * 128:(k + 1) * 128, :], writes=[bst])
            S.op("pool", lambda e: e.tensor_copy(fw_bf[:, k, :], st[:, 0:G]), reads=[bst], writes=[b_fw])
        for k in range(32):
            st, bst = stg[k % 2]; ct, bct = cst[k % 2]
            S.dma("sp", st[:], w_out[k * 128:(k + 1) * 128, :], writes=[bst])
            S.op("pool", lambda e: e.tensor_copy(ct[:], st[:]), reads=[bst], writes=[bct])
            S.dma("sp", wsc[k * 128:(k + 1) * 128, :], ct[:], reads=[bct], writes=[b_wsc[k]])
        last_scr = [stg[0][1], stg[1][1], cst[0][1], cst[1][1]]

    ygall = sb("ygall", [128, 32, TBK], BF16); b_yg = [Buf() for _ in range(4)]
    wt_p = Pool_(nc, "wt", 4, [128, 8, 512], BF16)
    outraw = sb("outraw", [128, 4, D]); b_or = [Buf() for _ in range(4)]
    x_p = Pool_(nc, "xt", 2, [128, D], F32)
    in_p = Pool_(nc, "inb", 2, [128, 8, TBK], BF16)
    sg_p = Pool_(nc, "sgb", 2, [128, 8, TBK], BF16)
    yb = sb("yb", [128, 8, TBK]); b_yb = Buf()
    acc = sb("acc", [128, TBK]); b_acc = Buf()
    sq_p = Pool_(nc, "sq", 2, [128, TBK], F32)
    st_p = Pool_(nc, "st", 4, [128, 4], F32)
    junk = sb("junk", [128, D], BF16); b_junk = Buf()
    for b in b_yg + b_or + [b_yb, b_acc, b_junk] + [t[1] for t in wt_p.t + x_p.t + in_p.t + sg_p.t + sq_p.t + st_p.t]:
        for ls in last_scr:
            if ls.w is not None:
                b.r[ls.w[0]] = max(b.r.get(ls.w[0], 0), ls.w[1])
            for k_, v_ in ls.r.items():
                b.r[k_] = max(b.r.get(k_, 0), v_)

    def colsum(src_ap, bsrc, first):
        pt, bpt = psum.get()
        S.op("pe", lambda e: e.matmul(pt[:], lhsT=ones_f[:], rhs=src_ap, start=True, stop=True), reads=[b_ones] + bsrc, writes=[bpt])
        if first:
            S.op("dve", lambda e: e.tensor_copy(acc[:], pt[:]), reads=[bpt], writes=[b_acc])
        else:
            S.op("dve", lambda e: e.tensor_tensor(acc[:], acc[:], pt[:], ALU.add), reads=[bpt, b_acc], writes=[b_acc])

    def rstd_acc():
        S.op("dve", lambda e: e.tensor_scalar(acc[:], acc[:], 1.0 / G, EPS, ALU.mult, ALU.add), reads=[b_acc], writes=[b_acc])
        S.op("dve", lambda e: e.reciprocal(acc[:], acc[:]), reads=[b_acc], writes=[b_acc])
        S.op("act", lambda e: e.activation(acc[:], acc[:], AF.Sqrt), reads=[b_acc], writes=[b_acc])

    for blk in range(NT // TBK):
        ts = slice(blk * TBK, (blk + 1) * TBK)
        ld = lambda dram: dram[:, ts].rearrange("(k p) t -> p k t", p=128)
        S.dma("sp", ygall[:, 0:8, :], ld(yAg), writes=[b_yg[0]])
        S.dma("sp", ygall[:, 24:32, :], ld(yMg), writes=[b_yg[3]])
        ff, bff = in_p.get()
        S.dma("sp", ff[:], ld(ffc), writes=[bff])
        sg, bsg = sg_p.get()
        S.dma("sp", sg[:], ld(sgB), writes=[bsg])
        for j in range(8):
            pt, bpt = psum.get()
            for k in range(8):
                S.op("pe", lambda e, k=k: e.matmul(pt[:], lhsT=fw_bf[:, k, j * 128:(j + 1) * 128], rhs=ff[:, k, :], start=(k == 0), stop=(k == 7)),
                     reads=[b_fw, bff], writes=[bpt])
            S.op("act", lambda e: e.activation(yb[:, j, :], pt[:], AF.Identity, bias=vec_s[:, 0, j:j + 1]), reads=[bpt, b_vec], writes=[b_yb])
            sq, bsq = sq_p.get()
            S.op("act", lambda e: e.activation(sq[:], yb[:, j, :], AF.Square), reads=[b_yb], writes=[bsq])
            colsum(sq[:], [bsq], j == 0)
        rstd_acc()
        for j in range(8):
            sq, bsq = sq_p.get()
            S.op("dve", lambda e: e.scalar_tensor_tensor(sq[:], yb[:, j, :], vec_s[:, 1, j:j + 1], acc[:], ALU.mult, ALU.mult),
                 reads=[b_yb, b_vec, b_acc], writes=[bsq])
            S.op("dve", lambda e: e.tensor_tensor(ygall[:, 8 + j, :], sq[:], sg[:, j, :], ALU.mult), reads=[bsq, bsg], writes=[b_yg[1]])
        yc, byc = in_p.get()
        S.dma("sp", yc[:], ld(ycc), writes=[byc])
        sg, bsg = sg_p.get()
        S.dma("sp", sg[:], ld(sgC), writes=[bsg])
        for j in range(8):
            sq, bsq = sq_p.get()
            S.op("act", lambda e: e.activation(sq[:], yc[:, j, :], AF.Square), reads=[byc], writes=[bsq])
            colsum(sq[:], [bsq], j == 0)
        rstd_acc()
        for j in range(8):
            sq, bsq = sq_p.get()
            S.op("dve", lambda e: e.scalar_tensor_tensor(sq[:], yc[:, j, :], vec_s[:, 2, j:j + 1], acc[:], ALU.mult, ALU.mult),
                 reads=[byc, b_vec, b_acc], writes=[bsq])
            S.op("dve", lambda e: e.tensor_tensor(ygall[:, 16 + j, :], sq[:], sg[:, j, :], ALU.mult), reads=[bsq, bsg], writes=[b_yg[2]])
        for ng in range(4):
            pts = [psum.get() for _ in range(4)]
            for kq in range(4):
                wt, bwt = wt_p.get()
                S.dma("sp", wt[:], wsc[kq * 1024:(kq + 1) * 1024, ng * 512:(ng + 1) * 512].rearrange("(k p) n -> p k n", p=128),
                      reads=b_wsc[kq * 8:(kq + 1) * 8], writes=[bwt])
                for tt in range(4):
                    pt, bpt = pts[tt]
                    for k in range(8):
                        kk = kq * 8 + k
                        S.op("pe", lambda e, k=k, kk=kk, tt=tt, pt=pt: e.matmul(pt[:], lhsT=ygall[:, kk, tt * 128:(tt + 1) * 128], rhs=wt[:, k, :],
                                                                                 start=(kk == 0), stop=(kk == 31)),
                             reads=[b_yg[kq], bwt], writes=[bpt])
            for tt in range(4):
                pt, bpt = pts[tt]
                S.op("act", lambda e, tt=tt, pt=pt: e.copy(outraw[:, tt, ng * 512:(ng + 1) * 512], pt[:]), reads=[bpt], writes=[b_or[tt]])
        for tt in range(4):
            r0 = blk * TBK + tt * 128
            xt, bxt = x_p.get()
            S.dma("sp", xt[:], x_d[r0:r0 + 128, :], writes=[bxt])
            stt, bstt = st_p.get()
            S.op("dve", lambda e: e.memset(stt[:], 0.0), writes=[bstt])
            S.op("act", lambda e: e.activation(junk[:], outraw[:, tt, :], AF.Square, accum_out=stt[:, 0:1]), reads=[b_or[tt], bstt], writes=[b_junk, bstt])
            S.op("dve", lambda e: e.tensor_scalar(stt[:, 1:2], stt[:, 0:1], 1.0 / D, EPS, ALU.mult, ALU.add), reads=[bstt], writes=[bstt])
            S.op("dve", lambda e: e.reciprocal(stt[:, 2:3], stt[:, 1:2]), reads=[bstt], writes=[bstt])
            S.op("act", lambda e: e.activation(stt[:, 2:3], stt[:, 2:3], AF.Sqrt), reads=[bstt], writes=[bstt])
            S.op("dve", lambda e: e.scalar_tensor_tensor(outraw[:, tt, :], outraw[:, tt, :], stt[:, 2:3], pg[:], ALU.mult, ALU.mult),
                 reads=[b_or[tt], bstt, b_pg], writes=[b_or[tt]])
            S.op("dve", lambda e: e.tensor_tensor(xt[:], xt[:], outraw[:, tt, :], ALU.add), reads=[bxt, b_or[tt]], writes=[bxt])
            b = Buf(); outs_b.append(b)
            S.dma("sp", xo[r0:r0 + 128, :], xt[:], reads=[bxt], writes=[b])

    S.finish(outs_b, "sp")
    return S
import numpy as np
D=2048; G=1024; L=8192; TB=1024; H=16; NB=2

def chunked(v, nch):
    return np.ascontiguousarray(v.reshape(nch, 128).T)

def dftG_const():
    k = np.arange(G)
    ang = 2*np.pi*((k[:,None]*k[None,:]) % G)/G
    sc = 1.0/np.sqrt(float(L)*G)
    return np.concatenate([np.cos(ang)*sc, -np.sin(ang)*sc], axis=1).astype(np.float32)

def l1_inputs(inp, l, core, x_cur):
    b = core // 4; j = core % 4
    xb = x_cur[b]
    xpad = np.concatenate([np.zeros((H, D), np.float32), xb, np.zeros((H, D), np.float32)], 0)
    xp = np.stack([xpad[j*2048 + blk*TB : j*2048 + blk*TB + TB + 2*H] for blk in range(NB)], 0)
    cw = np.ascontiguousarray(inp['conv_dw_w'][l].T.reshape(8, 128, 31).transpose(1, 0, 2))
    gn = inp['group_norm_g'][l]
    cvec = np.stack([chunked(inp['conv_dw_b'][l], 8), chunked(inp['conv_ln_g'][l], 8), chunked(inp['conv_ln_b'][l], 8),
                     chunked(inp['conv_pw_b'][l], 8), chunked(gn[0:G], 8)], axis=1)
    hw = np.concatenate([inp['hy_short_w'][l], inp['hy_short_b'][l][None]], 0)
    hyw = np.ascontiguousarray(hw.T.reshape(24, 128, 4).transpose(1, 0, 2))
    return {
        "xp": np.ascontiguousarray(xp), "w_in": inp['w_in'][l],
        "pre_g_bc": np.ascontiguousarray(np.broadcast_to(inp['pre_norm_g'][l], (128, D))),
        "conv_w": cw, "conv_vec": np.ascontiguousarray(cvec), "conv_pw_w": inp['conv_pw_w'][l],
        "hy_w": hyw, "mem": inp['mem'][b],
        "mem_g_bc": np.ascontiguousarray(np.broadcast_to(inp['mem_norm_g'], (128, D))),
        "mem_wk": inp['mem_wk'][l], "mem_wv": inp['mem_wv'][l],
        "gD": chunked(gn[3*G:4*G], 8), "dftG": dftG_const(),
    }

BF = ml_dtypes.bfloat16
NCH=128; CB=8; NBATCH=NCH//CB

def l2_consts():
    j = np.arange(128)
    ang = 2*np.pi*((j[:,None]*j[None,:]) % 128)/128.0
    C = np.cos(ang); Sn = np.sin(ang)
    j64 = np.arange(64)
    ang64 = 2*np.pi*((j64[:,None]*j64[None,:]) % 64)/64.0
    C64 = np.cos(ang64); S64 = np.sin(ang64)
    cb = np.zeros((128, CBF_W), np.float64)
    def put(name, m):
        o, w = CO[name]; assert m.shape[1] == w, (name, m.shape); cb[:m.shape[0], o:o+w] = m
    FA = np.concatenate([C, -Sn], 1)
    put("FA", FA); put("FAhi", FA[64:128]); put("C", C); put("S", Sn); put("nS", -Sn)
    put("IA1", np.concatenate([C, Sn], 1)); put("IA2", np.concatenate([-Sn, C], 1))
    put("IBc", C[:, :64]/16384.0); put("IBs", -Sn[:, :64]/16384.0)
    put("FN1", np.concatenate([C64, -S64], 1)); put("FN2", np.concatenate([S64, C64], 1))
    put("FAi", np.concatenate([Sn, C], 1)); put("IBsp", Sn[:, :64]/16384.0)
    cf = np.zeros((128, CF_W), np.float64)
    a16 = 2*np.pi*(j[:,None]*j[None,:])/16384.0
    a8 = 2*np.pi*(j[:,None]*j64[None,:])/8192.0
    def putf(name, m):
        o, w = CF[name]; cf[:, o:o+w] = m
    putf("T16r", np.cos(a16)); putf("T16i", -np.sin(a16)); putf("T16ci", np.sin(a16))
    putf("T8r", np.cos(a8)); putf("T8i", -np.sin(a8))
    return cb.astype(BF), cf.astype(np.float32)

def l2_inputs(inp, l, core, Zb, hyb):
    cs = slice(core*NCH, (core+1)*NCH)
    def rows(arrs, off):
        return np.ascontiguousarray(np.concatenate([a[off + core*NCH: off + (core+1)*NCH] for a in arrs], 0))
    pos = inp['positions'].astype(np.int32)
    posb = pos[(L - np.arange(L)) % L]
    pos_rep = np.stack([np.broadcast_to(pos, (32, L)), np.broadcast_to(posb, (32, L))], 0)
    pos_t = np.stack([pos.reshape(64, 128), posb.reshape(64, 128)], 0)
    bands = np.linspace(1e-4, 15, 16, dtype=np.float32)
    bsc = np.zeros((32, 2), np.float32)
    bsc[:, 0] = np.concatenate([bands, bands]) / np.float32(L)
    bsc[:16, 1] = 0.25
    fw1 = inp['hy_fw1'][l]
    mvec = np.stack([inp['hy_fb1'][l], inp['hy_freq1'][l], inp['hy_fb2'][l], inp['hy_freq2'][l]], 1)
    fw3 = inp['hy_fw3'][l].reshape(64, 2, 2, 1024)[:, :, :, cs]
    fw3c = fw3.reshape(64, 2, 2, NBATCH, CB).transpose(0, 2, 3, 1, 4)
    dec = inp['hy_decay'][l][:, :, cs]
    decc = dec.reshape(2, 2, NBATCH, CB).transpose(1, 2, 0, 3)
    cbf, cf = l2_consts()
    return {
        "zr": rows(Zb, 0), "zi": rows(Zb, 1024),
        "hv": rows(hyb, 0), "hx1": rows(hyb, 1024), "hx2": rows(hyb, 2048),
        "pos_rep": np.ascontiguousarray(pos_rep), "pos_t": np.ascontiguousarray(pos_t), "bsc": bsc,
        "fw1a": np.ascontiguousarray(fw1[0:1]), "fw1b": np.ascontiguousarray(fw1[1:33]), "mvec": np.ascontiguousarray(mvec),
        "fw2": inp['hy_fw2'][l], "fw3c": np.ascontiguousarray(fw3c),
        "dec_rep": np.ascontiguousarray(np.broadcast_to(decc, (64,) + decc.shape)),
        "skip_rep": np.ascontiguousarray(np.broadcast_to(inp['hy_skip'][l][:, cs], (64, 2, NCH))),
        "cbf": cbf, "cf": cf,
    }


def l3_inputs(inp, l, core, x_cur, ffb, ycb, l1o):
    b = core // 4; j = core % 4
    ts = slice(j*2048, (j+1)*2048)
    gn = inp['group_norm_g'][l]
    vec3 = np.stack([chunked(inp['fnet_b'][l], 8), chunked(gn[1024:2048], 8), chunked(gn[2048:3072], 8)], axis=1)
    return {
        "ffc": np.ascontiguousarray(ffb[:, ts]), "ycc": np.ascontiguousarray(ycb[:, ts]),
        "sgB": l1o["sgB"], "sgC": l1o["sgC"], "yAg": l1o["yAg"], "yMg": l1o["yMg"],
        "x": np.ascontiguousarray(x_cur[b][ts]), "fnet_w": inp['fnet_w'][l], "vec3": np.ascontiguousarray(vec3),
        "w_out": inp['w_out'][l],
        "post_g_bc": np.ascontiguousarray(np.broadcast_to(inp['post_norm_g'][l], (128, 2048))),
    }

_PROGS = {}


def _prog(name, builder):
    if name not in _PROGS:
        nc = bass.Bass("TRN2", target_bir_lowering=False)
        builder(nc)
        _PROGS[name] = nc
    return _PROGS[name]


def kernel(**inputs):
    inp = {k: np.asarray(v) for k, v in inputs.items()}
    x_cur = np.ascontiguousarray(inp['x'], dtype=np.float32)
    cores = list(range(8))
    nc1 = _prog("L1", build_L1); nc2 = _prog("L2", build_L2); nc3 = _prog("L3", build_L3)
    for l in range(2):
        r1 = run_bass_kernel_spmd(nc1, [l1_inputs(inp, l, c, x_cur) for c in cores], core_ids=cores).results
        Zb = [np.concatenate([np.asarray(r1[4 * b + j]["Z"]) for j in range(4)], axis=1) for b in range(2)]
        hyb = [np.concatenate([np.asarray(r1[4 * b + j]["hyc"]) for j in range(4)], axis=1) for b in range(2)]
        r2 = run_bass_kernel_spmd(nc2, [l2_inputs(inp, l, c, Zb, hyb) for c in cores], core_ids=cores).results
        ffb = [np.concatenate([np.asarray(r2[c]["ff"])[b * NCH:(b + 1) * NCH] for c in cores], axis=0) for b in range(2)]
        ycb = [np.concatenate([np.asarray(r2[c]["yc"])[b * NCH:(b + 1) * NCH] for c in cores], axis=0) for b in range(2)]
        l1o = [{k: np.asarray(r1[c][k]) for k in ("sgB", "sgC", "yAg", "yMg")} for c in cores]
        r3 = run_bass_kernel_spmd(nc3, [l3_inputs(inp, l, c, x_cur, ffb[c // 4], ycb[c // 4], l1o[c]) for c in cores],
                                  core_ids=cores).results
        x_cur = np.stack([np.concatenate([np.asarray(r3[4 * b + j]["xo"]) for j in range(4)], axis=0) for b in range(2)], axis=0)
    return np.ascontiguousarray(x_cur, dtype=np.float32)
```

```python
import math
import ml_dtypes
import numpy as np
import concourse.bass as bass
import concourse.mybir as mybir
from concourse.bass_utils import run_bass_kernel_spmd

F32 = mybir.dt.float32
BF16 = mybir.dt.bfloat16
I32 = mybir.dt.int32
AF = mybir.ActivationFunctionType
ALU = mybir.AluOpType
AX = mybir.AxisListType


class Buf:
    __slots__ = ("name", "w", "r")

    def __init__(self, name=""):
        self.name = name
        self.w = None
        self.r = {}


class Sched:
    SEM_LIMIT = 30000

    def __init__(self, nc, n_dma_sems=40, same_engine_sync=True):
        self.nc = nc
        self.engs = {"pe": nc.tensor, "act": nc.scalar, "dve": nc.vector,
                     "pool": nc.gpsimd, "sp": nc.sync}
        self.same_engine_sync = same_engine_sync
        self.sems = {}
        self.cur = {}
        self.nsem = 0
        for e in self.engs:
            self._new_eng_sem(e)
        self.dma_sems = []
        for i in range(n_dma_sems):
            k = ("dma", i)
            self.sems[k] = nc.alloc_semaphore(f"dq{i}")
            self.dma_sems.append([k, 0])
        self.dma_rr = 0
        self.waited = {e: {} for e in self.engs}
        self.n_inst = {e: 0 for e in self.engs}
        self.n_wait = {e: 0 for e in self.engs}

    def _new_eng_sem(self, e):
        k = (e, self.nsem)
        self.nsem += 1
        self.sems[k] = self.nc.alloc_semaphore(f"s_{e}_{k[1]}")
        self.cur[e] = [k, 0]

    def _wait(self, e, tok):
        if tok is None:
            return
        k, v = tok
        if not self.same_engine_sync and k[0] == e:
            return
        if e == "pe" and k[0] == "pe":
            return
        if self.waited[e].get(k, 0) >= v:
            return
        self.engs[e].wait_ge(self.sems[k], v)
        self.waited[e][k] = v
        self.n_wait[e] += 1

    def _deps(self, e, reads, writes):
        for b in reads:
            self._wait(e, b.w)
        for b in writes:
            self._wait(e, b.w)
            for k, v in b.r.items():
                self._wait(e, (k, v))

    def _mark(self, tok, reads, writes):
        k, v = tok
        for b in reads:
            if b.r.get(k, 0) < v:
                b.r[k] = v
        for b in writes:
            b.w = tok
            b.r = {}

    def op(self, e, fn, reads=(), writes=()):
        self._deps(e, reads, writes)
        ins = fn(self.engs[e])
        c = self.cur[e]
        c[1] += 1
        ins.then_inc(self.sems[c[0]], 1)
        tok = (c[0], c[1])
        self._mark(tok, reads, writes)
        self.n_inst[e] += 1
        if c[1] >= self.SEM_LIMIT:
            self._new_eng_sem(e)
        return tok

    def dma(self, q, out, in_, reads=(), writes=(), **kw):
        self._deps(q, reads, writes)
        slot = self.dma_sems[self.dma_rr]
        self.dma_rr = (self.dma_rr + 1) % len(self.dma_sems)
        k, uses = slot
        if uses > 0:
            self._wait(q, (k, 16 * uses))
        ins = self.engs[q].dma_start(out=out, in_=in_, **kw)
        slot[1] = uses + 1
        ins.then_inc(self.sems[k], 16)
        tok = (k, 16 * (uses + 1))
        self._mark(tok, reads, writes)
        self.n_inst[q] += 1
        return tok

    def finish(self, bufs, e="sp"):
        for b in bufs:
            self._wait(e, b.w)

D = 2048
NIN = 11264
G = 1024
TB = 1024
H = 16
TBH = TB + 2 * H
NB = 2
EPS = 1e-6
C_AVAL, C_AGATE, C_F, C_HY, C_Q, C_GATE = 0, 1024, 2048, 3072, 6144, 7168


class Pool_:
    def __init__(self, nc, name, n, shape, dtype, psum=False):
        self.t = []
        for i in range(n):
            if psum:
                h = nc.alloc_psum_tensor(f"{name}{i}", shape, dtype)
            else:
                h = nc.alloc_sbuf_tensor(f"{name}{i}", shape, dtype)
            self.t.append((h, Buf(f"{name}{i}")))
        self.i = 0

    def get(self):
        r = self.t[self.i]
        self.i = (self.i + 1) % len(self.t)
        return r


def build_L1(nc):
    S = Sched(nc)
    dt_in = lambda name, shape, dt=F32: nc.dram_tensor(name, shape, dt, kind="ExternalInput").ap()
    dt_out = lambda name, shape, dt=BF16: nc.dram_tensor(name, shape, dt, kind="ExternalOutput").ap()
    xp = dt_in("xp", [NB, TBH, D])
    w_in = dt_in("w_in", [D, NIN])
    pre_g_bc = dt_in("pre_g_bc", [128, D])
    cw = dt_in("conv_w", [128, 8, 31])
    cvec = dt_in("conv_vec", [128, 5, 8])
    pw = dt_in("conv_pw_w", [G, G])
    hyw = dt_in("hy_w", [128, 24, 4])
    memx = dt_in("mem", [256, D])
    mem_g_bc = dt_in("mem_g_bc", [128, D])
    wk = dt_in("mem_wk", [D, G])
    wv = dt_in("mem_wv", [D, G])
    gD = dt_in("gD", [128, 8])
    dftG = dt_in("dftG", [G, 2 * G])
    o_yA = dt_out("yAg", [G, NB * TB])
    o_yM = dt_out("yMg", [G, NB * TB])
    o_sgB = dt_out("sgB", [G, NB * TB])
    o_sgC = dt_out("sgC", [G, NB * TB])
    o_Z = dt_out("Z", [2 * G, NB * TB])
    o_hy = dt_out("hyc", [3 * G, NB * TB])
    outs_b = []

    sb = lambda name, shape, dt=F32: nc.alloc_sbuf_tensor(name, shape, dt)
    ident = sb("ident", [128, 128], BF16); b_ident = Buf()
    ones_f = sb("ones_f", [128, 128], F32); b_ones_f = Buf()
    ones_b = sb("ones_b", [128, 128], BF16); b_ones_b = Buf()
    S.op("pool", lambda e: e.memset(ident[:], 1.0), writes=[b_ident])
    S.op("pool", lambda e: e.affine_select(ident[:], ident[:], pattern=[[-1, 128]], compare_op=ALU.is_equal,
                                           fill=0.0, base=0, channel_multiplier=1), reads=[b_ident], writes=[b_ident])
    S.op("pool", lambda e: e.memset(ones_f[:], 1.0), writes=[b_ones_f])
    S.op("pool", lambda e: e.memset(ones_b[:], 1.0), writes=[b_ones_b])
    g_bc = sb("g_bc", [128, D]); b_gbc = Buf()
    cw_s = sb("cw_s", [128, 8, 31]); cvec_s = sb("cvec_s", [128, 5, 8]); hyw_s = sb("hyw_s", [128, 24, 4])
    gD_s = sb("gD_s", [128, 8])
    b_par = Buf()
    S.dma("sp", cw_s[:], cw, writes=[b_par])
    S.dma("sp", cvec_s[:], cvec, writes=[b_par])
    S.dma("sp", hyw_s[:], hyw, writes=[b_par])
    S.dma("sp", gD_s[:], gD, writes=[b_par])

    psum = Pool_(nc, "ps", 8, [128, 512], F32, psum=True)
    wbf = Pool_(nc, "wbf", 3, [128, 16, 256], BF16)
    wkeys = {}
    xs_p = Pool_(nc, "xs", 2, [128, D], F32)
    hb_p = Pool_(nc, "hb", 2, [128, D], BF16)
    st_p = Pool_(nc, "st", 4, [128, 4], F32)
    hT = sb("hT", [128, 16, TBH], BF16); b_hT = Buf()

    pending = []
    tick = [0]

    def flush(delay):
        while pending and pending[0][0] + delay <= tick[0]:
            _, dst, ob, bob = pending.pop(0)
            b = Buf()
            outs_b.append(b)
            S.dma("sp", dst, ob[:], reads=[bob], writes=[b])

    def out_dma(dst, ob, bob):
        pending.append((tick[0], dst, ob, bob))

    def stream_w(dram, c0, nk):
        tick[0] += 1
        flush(2)
        key = (dram.tensor.name, c0 // 256)
        if key in wkeys:
            wb, bwb = wbf.t[wkeys[key]]
        else:
            slot = wbf.i
            for k_ in [k_ for k_, v_ in wkeys.items() if v_ == slot]:
                del wkeys[k_]
            wb, bwb = wbf.get()
            wkeys[key] = slot
            cbase = (c0 // 256) * 256
            S.dma("pool", wb[:, 0:nk, :], dram.rearrange("(k p) c -> p k c", p=128)[:, :, cbase:cbase + 256], writes=[bwb])
        off = c0 % 256
        return wb[:, :, off:off + 128], bwb

    def rms_rows(xs, bxs, rows, gtile, bg, out_bf, bout):
        stt, bstt = st_p.get()
        S.op("dve", lambda e: e.memset(stt[:], 0.0), writes=[bstt])
        S.op("act", lambda e: e.activation(out_bf[:rows], xs[:rows], AF.Square, accum_out=stt[:rows, 0:1]),
             reads=[bxs, bstt], writes=[bout, bstt])
        S.op("dve", lambda e: e.tensor_scalar(stt[:rows, 1:2], stt[:rows, 0:1], 1.0 / D, EPS, ALU.mult, ALU.add),
             reads=[bstt], writes=[bstt])
        S.op("dve", lambda e: e.reciprocal(stt[:rows, 2:3], stt[:rows, 1:2]), reads=[bstt], writes=[bstt])
        S.op("act", lambda e: e.activation(stt[:rows, 2:3], stt[:rows, 2:3], AF.Sqrt), reads=[bstt], writes=[bstt])
        S.op("dve", lambda e: e.scalar_tensor_tensor(out_bf[:rows], xs[:rows], stt[:rows, 2:3], gtile[:rows],
                                                     ALU.mult, ALU.mult), reads=[bxs, bstt, bg], writes=[bout])

    def transpose_into(src_bf, bsrc, rows, dstT, bdst, col0):
        for half in range(2):
            pt, bpt = psum.get()
            ptb = pt[:].bitcast(BF16)
            for kk in range(8):
                k = half * 8 + kk
                S.op("pe", lambda e, k=k, kk=kk: e.transpose(ptb[:, kk * 128:kk * 128 + rows],
                                                             src_bf[:rows, k * 128:(k + 1) * 128], ident[:rows, :rows]),
                     reads=[bsrc, b_ident], writes=[bpt])
            S.op("act", lambda e: e.copy(dstT[:, half * 8:half * 8 + 8, col0:col0 + rows],
                                         ptb.rearrange("p (k t) -> p k t", k=8)[:, :, 0:rows]),
                 reads=[bpt], writes=[bdst])

    cvb = sb("cvb", [128, 8, TB], BF16); b_cv = [Buf() for _ in range(8)]
    memT = cvb[:, 0:4, :].rearrange("p a (b m) -> p (a b) m", m=256); b_memT = Buf()
    mg_bc, b_mg = g_bc, b_gbc
    S.dma("sp", mg_bc[:], mem_g_bc, writes=[b_mg])
    for i in range(2):
        xs, bxs = xs_p.get()
        S.dma("sp", xs[:], memx[i * 128:(i + 1) * 128, :], writes=[bxs])
        hb, bhb = hb_p.get()
        rms_rows(xs, bxs, 128, mg_bc, b_mg, hb, bhb)
        transpose_into(hb, bhb, 128, memT, b_memT, i * 128)
    kT = sb("kT", [128, 8, 256], BF16); b_kT = Buf()
    vS = sb("vS", [128, 2, G], BF16); b_vS = Buf()
    for j in range(8):
        wb, bwb = stream_w(wk, j * 128, 16)
        pt, bpt = psum.get()
        for k in range(16):
            S.op("pe", lambda e, k=k: e.matmul(pt[:, 0:256], lhsT=wb[:, k, :], rhs=memT[:, k, :], start=(k == 0), stop=(k == 15)),
                 reads=[bwb, b_memT], writes=[bpt])
        S.op("act", lambda e: e.copy(kT[:, j, :], pt[:, 0:256]), reads=[bpt], writes=[b_kT])
    for j in range(8):
        wb, bwb = stream_w(wv, j * 128, 16)
        pt, bpt = psum.get()
        for m in range(2):
            for k in range(16):
                S.op("pe", lambda e, k=k, m=m: e.matmul(pt[:, m * 128:(m + 1) * 128], lhsT=memT[:, k, m * 128:(m + 1) * 128],
                                                        rhs=wb[:, k, :], start=(k == 0), stop=(k == 15)),
                     reads=[bwb, b_memT], writes=[bpt])
        S.op("act", lambda e: e.copy(vS[:, :, j * 128:(j + 1) * 128], pt[:, 0:256].rearrange("p (m c) -> p m c", m=2)),
             reads=[bpt], writes=[b_vS])

    S.dma("sp", g_bc[:], pre_g_bc, reads=[], writes=[b_gbc])
    yAb = sb("yAb", [128, 8, TB], BF16); b_yA = [Buf() for _ in range(8)]
    fT = sb("fT", [128, 8, TB], BF16); b_fT = Buf()
    s1 = sb("s1", [128, TB]); b_s1 = Buf()
    s2 = sb("s2", [128, TB]); b_s2 = Buf()
    u_p = Pool_(nc, "u", 2, [128, TBH], F32)
    sig_p = Pool_(nc, "sig", 1, [128, TBH], F32)
    acc_p = Pool_(nc, "acc", 2, [128, TB], F32)
    sq_p = Pool_(nc, "sq", 2, [128, TB], F32)
    ob_p = Pool_(nc, "ob", 4, [128, TB], BF16)
    eT_p = Pool_(nc, "eT", 2, [128, 2, 512], BF16)
    rden_p = Pool_(nc, "rden", 2, [128, 512], F32)

    def inproj(col0, halo):
        wb, bwb = stream_w(w_in, col0, 16)
        res = []
        if halo:
            chunks = [(i * 352, 352) for i in range(3)]
        else:
            chunks = [(H + i * 512, 512) for i in range(2)]
        for (t0, n) in chunks:
            pt, bpt = psum.get()
            for k in range(16):
                S.op("pe", lambda e, k=k, t0=t0, n=n, pt=pt: e.matmul(pt[:, 0:n], lhsT=wb[:, k, :], rhs=hT[:, k, t0:t0 + n],
                                                                       start=(k == 0), stop=(k == 15)),
                     reads=[bwb, b_hT], writes=[bpt])
            res.append((pt, bpt, t0, n))
        return res

    def colsum_acc(src, bsrc, acc, bacc, first):
        for hh in range(2):
            pt, bpt = psum.get()
            S.op("pe", lambda e: e.matmul(pt[:], lhsT=ones_f[:], rhs=src[:, hh * 512:(hh + 1) * 512], start=True, stop=True),
                 reads=[b_ones_f, bsrc], writes=[bpt])
            if first:
                S.op("dve", lambda e: e.tensor_copy(acc[:, hh * 512:(hh + 1) * 512], pt[:]), reads=[bpt], writes=[bacc])
            else:
                S.op("dve", lambda e: e.tensor_tensor(acc[:, hh * 512:(hh + 1) * 512], acc[:, hh * 512:(hh + 1) * 512], pt[:], ALU.add),
                     reads=[bpt, bacc], writes=[bacc])

    def rstd_from(acc, bacc, n):
        S.op("dve", lambda e: e.tensor_scalar(acc[:], acc[:], 1.0 / n, EPS, ALU.mult, ALU.add), reads=[bacc], writes=[bacc])
        S.op("dve", lambda e: e.reciprocal(acc[:], acc[:]), reads=[bacc], writes=[bacc])
        S.op("act", lambda e: e.activation(acc[:], acc[:], AF.Sqrt), reads=[bacc], writes=[bacc])

    def gated_out(ysrc, bys, j, gcol, rstd, brstd, gate_col0, odram, blk):
        res = inproj(gate_col0 + j * 128, False)
        ob, bob = ob_p.get()
        sg, bsg = sq_p.get()
        for (pt, bpt, t0, n) in res:
            o = t0 - H
            S.op("act", lambda e, pt=pt, o=o: e.activation(sg[:, o:o + 512], pt[:], AF.Silu), reads=[bpt], writes=[bsg])
        tmp, btmp = acc_p.get()
        S.op("dve", lambda e: e.scalar_tensor_tensor(tmp[:], ysrc, gcol, rstd[:], ALU.mult, ALU.mult),
             reads=[bys, brstd, b_par], writes=[btmp])
        S.op("dve", lambda e: e.tensor_tensor(ob[:], tmp[:], sg[:], ALU.mult), reads=[btmp, bsg], writes=[bob])
        out_dma(odram[j * 128:(j + 1) * 128, blk * TB:(blk + 1) * TB], ob, bob)

    for blk in range(NB):
        for i in range(9):
            rows = 128 if i < 8 else TBH - 8 * 128
            xs, bxs = xs_p.get()
            S.dma("sp", xs[:rows], xp[blk, i * 128:i * 128 + rows, :], writes=[bxs])
            hb, bhb = hb_p.get()
            rms_rows(xs, bxs, rows, g_bc, b_gbc, hb, bhb)
            transpose_into(hb, bhb, rows, hT, b_hT, i * 128)

        for j in range(24):
            res = inproj(C_HY + j * 128, True)
            ob, bob = ob_p.get()
            tmp, btmp = acc_p.get()
            u, bu = u_p.get()
            for (pt, bpt, t0, n) in res:
                S.op("act", lambda e, pt=pt, t0=t0, n=n: e.copy(u[:, t0:t0 + n], pt[:, 0:n]), reads=[bpt], writes=[bu])
            S.op("dve", lambda e: e.tensor_scalar(tmp[:], u[:, H - 1:H - 1 + TB], hyw_s[:, j, 0:1], hyw_s[:, j, 3:4], ALU.mult, ALU.add),
                 reads=[bu, b_par], writes=[btmp])
            S.op("dve", lambda e: e.scalar_tensor_tensor(tmp[:], u[:, H:H + TB], hyw_s[:, j, 1:2], tmp[:], ALU.mult, ALU.add),
                 reads=[bu, b_par, btmp], writes=[btmp])
            S.op("dve", lambda e: e.scalar_tensor_tensor(ob[:], u[:, H + 1:H + 1 + TB], hyw_s[:, j, 2:3], tmp[:], ALU.mult, ALU.add),
                 reads=[bu, b_par, btmp], writes=[bob])
            out_dma(o_hy[j * 128:(j + 1) * 128, blk * TB:(blk + 1) * TB], ob, bob)

        for gi, odram in ((1, o_sgB), (2, o_sgC)):
            for j in range(8):
                res = inproj(C_GATE + gi * G + j * 128, False)
                ob, bob = ob_p.get()
                for (pt, bpt, t0, n) in res:
                    o = t0 - H
                    S.op("act", lambda e, pt=pt, o=o: e.activation(ob[:, o:o + 512], pt[:], AF.Silu), reads=[bpt], writes=[bob])
                out_dma(odram[j * 128:(j + 1) * 128, blk * TB:(blk + 1) * TB], ob, bob)

        for j in range(8):
            res = inproj(C_F + j * 128, False)
            for (pt, bpt, t0, n) in res:
                o = t0 - H
                S.op("act", lambda e, pt=pt, o=o: e.copy(fT[:, j, o:o + 512], pt[:]), reads=[bpt], writes=[b_fT])
        for j in range(16):
            wb, bwb = stream_w(dftG, j * 128, 8)
            ob, bob = ob_p.get()
            for hh in range(2):
                pt, bpt = psum.get()
                for k in range(8):
                    S.op("pe", lambda e, k=k: e.matmul(pt[:], lhsT=wb[:, k, :], rhs=fT[:, k, hh * 512:(hh + 1) * 512],
                                                       start=(k == 0), stop=(k == 7)), reads=[bwb, b_fT], writes=[bpt])
                S.op("act", lambda e: e.copy(ob[:, hh * 512:(hh + 1) * 512], pt[:]), reads=[bpt], writes=[bob])
            out_dma(o_Z[j * 128:(j + 1) * 128, blk * TB:(blk + 1) * TB], ob, bob)

        for i in range(8):
            resg = inproj(C_AGATE + i * 128, True)
            sig, bsig = sig_p.get()
            for (pt, bpt, t0, n) in resg:
                S.op("act", lambda e, pt=pt, t0=t0, n=n: e.activation(sig[:, t0:t0 + n], pt[:, 0:n], AF.Sigmoid), reads=[bpt], writes=[bsig])
            resv = inproj(C_AVAL + i * 128, True)
            u, bu = u_p.get()
            for (pt, bpt, t0, n) in resv:
                S.op("dve", lambda e, pt=pt, t0=t0, n=n: e.tensor_tensor(u[:, t0:t0 + n], pt[:, 0:n], sig[:, t0:t0 + n], ALU.mult),
                     reads=[bpt, bsig], writes=[bu])
            acc, bacc = acc_p.get()
            S.op("dve", lambda e: e.tensor_scalar(acc[:], u[:, H - 15:H - 15 + TB], cw_s[:, i, 0:1], cvec_s[:, 0, i:i + 1], ALU.mult, ALU.add),
                 reads=[bu, b_par], writes=[bacc])
            for tap in range(1, 31):
                S.op("dve", lambda e, tap=tap: e.scalar_tensor_tensor(acc[:], u[:, H - 15 + tap:H - 15 + tap + TB], cw_s[:, i, tap:tap + 1], acc[:],
                                                                      ALU.mult, ALU.add), reads=[bu, b_par, bacc], writes=[bacc])
            sq, bsq = sq_p.get()
            S.op("act", lambda e: e.activation(sq[:], acc[:], AF.Square), reads=[bacc], writes=[bsq])
            S.op("act", lambda e: e.copy(cvb[:, i, :], acc[:]), reads=[bacc], writes=[b_cv[i], b_memT])
            colsum_acc(acc, bacc, s1, b_s1, i == 0)
            colsum_acc(sq, bsq, s2, b_s2, i == 0)
        S.op("dve", lambda e: e.tensor_scalar(s1[:], s1[:], 1.0 / G, None, ALU.mult), reads=[b_s1], writes=[b_s1])
        sq, bsq = sq_p.get()
        S.op("dve", lambda e: e.tensor_tensor(sq[:], s1[:], s1[:], ALU.mult), reads=[b_s1], writes=[bsq])
        S.op("dve", lambda e: e.scalar_tensor_tensor(s2[:], s2[:], 1.0 / G, sq[:], ALU.mult, ALU.subtract), reads=[b_s2, bsq], writes=[b_s2])
        S.op("dve", lambda e: e.tensor_scalar(s2[:], s2[:], EPS, None, ALU.add), reads=[b_s2], writes=[b_s2])
        S.op("dve", lambda e: e.reciprocal(s2[:], s2[:]), reads=[b_s2], writes=[b_s2])
        S.op("act", lambda e: e.activation(s2[:], s2[:], AF.Sqrt), reads=[b_s2], writes=[b_s2])
        for i in range(8):
            tmp, btmp = acc_p.get()
            S.op("dve", lambda e: e.tensor_tensor(tmp[:], cvb[:, i, :], s1[:], ALU.subtract), reads=[b_cv[i], b_s1], writes=[btmp])
            S.op("dve", lambda e: e.tensor_tensor(tmp[:], tmp[:], s2[:], ALU.mult), reads=[btmp, b_s2], writes=[btmp])
            S.op("act", lambda e: e.activation(cvb[:, i, :], tmp[:], AF.Silu, scale=cvec_s[:, 1, i:i + 1], bias=cvec_s[:, 2, i:i + 1]),
                 reads=[btmp, b_par], writes=[b_cv[i]])
        for j in range(8):
            wb, bwb = stream_w(pw, j * 128, 8)
            ya, bya = acc_p.get()
            for hh in range(2):
                pt, bpt = psum.get()
                for k in range(8):
                    S.op("pe", lambda e, k=k: e.matmul(pt[:], lhsT=wb[:, k, :], rhs=cvb[:, k, hh * 512:(hh + 1) * 512], start=(k == 0), stop=(k == 7)),
                         reads=[bwb, b_cv[k]], writes=[bpt])
                S.op("act", lambda e: e.activation(ya[:, hh * 512:(hh + 1) * 512], pt[:], AF.Identity, bias=cvec_s[:, 3, j:j + 1]),
                     reads=[bpt, b_par], writes=[bya])
            sq, bsq = sq_p.get()
            S.op("act", lambda e: e.activation(sq[:], ya[:], AF.Square), reads=[bya], writes=[bsq])
            S.op("dve", lambda e: e.tensor_copy(yAb[:, j, :], ya[:]), reads=[bya], writes=[b_yA[j]])
            colsum_acc(sq, bsq, s1, b_s1, j == 0)
        rstd_from(s1, b_s1, G)
        for j in range(8):
            gated_out(yAb[:, j, :], b_yA[j], j, cvec_s[:, 4, j:j + 1], s1, b_s1, C_GATE + 0 * G, o_yA, blk)

        for j in range(8):
            res = inproj(C_Q + j * 128, False)
            for (pt, bpt, t0, n) in res:
                o = t0 - H
                S.op("act", lambda e, pt=pt, o=o: e.copy(fT[:, j, o:o + 512], pt[:]), reads=[bpt], writes=[b_fT])
        first = True
        for hd in range(4):
            for hh in range(2):
                eT, beT = eT_p.get()
                for m in range(2):
                    pt, bpt = psum.get()
                    for dc in range(2):
                        S.op("pe", lambda e, dc=dc, m=m: e.matmul(pt[:], lhsT=kT[:, hd * 2 + dc, m * 128:(m + 1) * 128],
                                                                  rhs=fT[:, hd * 2 + dc, hh * 512:(hh + 1) * 512], start=(dc == 0), stop=(dc == 1)),
                             reads=[b_kT, b_fT], writes=[bpt])
                    S.op("act", lambda e, m=m: e.activation(eT[:, m, :], pt[:], AF.Exp, scale=1.0 / 16.0), reads=[bpt], writes=[beT])
                pd, bpd = psum.get()
                for m in range(2):
                    S.op("pe", lambda e, m=m: e.matmul(pd[:], lhsT=ones_b[:], rhs=eT[:, m, :], start=(m == 0), stop=(m == 1)),
                         reads=[b_ones_b, beT], writes=[bpd])
                rd, brd = rden_p.get()
                S.op("dve", lambda e: e.reciprocal(rd[:], pd[:]), reads=[bpd], writes=[brd])
                for cc in range(2):
                    j = hd * 2 + cc
                    pt, bpt = psum.get()
                    for m in range(2):
                        S.op("pe", lambda e, m=m: e.matmul(pt[:], lhsT=vS[:, m, j * 128:(j + 1) * 128], rhs=eT[:, m, :], start=(m == 0), stop=(m == 1)),
                             reads=[b_vS, beT], writes=[bpt])
                    S.op("dve", lambda e, j=j: e.tensor_tensor(yAb[:, j, hh * 512:(hh + 1) * 512], pt[:], rd[:], ALU.mult),
                         reads=[bpt, brd], writes=[b_yA[j]])
        for j in range(8):
            sq, bsq = sq_p.get()
            S.op("act", lambda e: e.activation(sq[:], yAb[:, j, :], AF.Square), reads=[b_yA[j]], writes=[bsq])
            colsum_acc(sq, bsq, s1, b_s1, j == 0)
        rstd_from(s1, b_s1, G)
        for j in range(8):
            gated_out(yAb[:, j, :], b_yA[j], j, gD_s[:, j:j + 1], s1, b_s1, C_GATE + 3 * G, o_yM, blk)

    flush(0)
    S.finish(outs_b, "sp")
    return S
import math

L = 8192
NCH = 128
CB = 8
NBATCH = NCH // CB
N16 = 16384
import os
SES = bool(int(os.environ.get("SES", "1")))

CO = {}
_o = 0
for _n, _w in (("FA", 256), ("FAhi", 256), ("C", 128), ("S", 128), ("nS", 128), ("IA1", 256), ("IA2", 256),
               ("IBc", 64), ("IBs", 64), ("FN1", 128), ("FN2", 128), ("FAi", 256), ("IBsp", 64)):
    CO[_n] = (_o, _w)
    _o += _w
CBF_W = _o
CF = {"T16r": (0, 128), "T16i": (128, 128), "T16ci": (256, 128), "T8r": (384, 64), "T8i": (448, 64)}
CF_W = 512


def build_L2(nc):
    S = Sched(nc, same_engine_sync=SES)
    dt_in = lambda name, shape, dt=F32: nc.dram_tensor(name, shape, dt, kind="ExternalInput").ap()
    dt_out = lambda name, shape, dt=BF16: nc.dram_tensor(name, shape, dt, kind="ExternalOutput").ap()
    zr_d = dt_in("zr", [2 * NCH, L], BF16); zi_d = dt_in("zi", [2 * NCH, L], BF16)
    hv_d = dt_in("hv", [2 * NCH, L], BF16); hx1_d = dt_in("hx1", [2 * NCH, L], BF16); hx2_d = dt_in("hx2", [2 * NCH, L], BF16)
    pos_rep = dt_in("pos_rep", [2, 32, L], I32)
    pos_t = dt_in("pos_t", [2, 64, 128], I32)
    bsc = dt_in("bsc", [32, 2])
    fw1a = dt_in("fw1a", [1, 64]); fw1b = dt_in("fw1b", [32, 64])
    mvec = dt_in("mvec", [64, 4])
    fw2 = dt_in("fw2", [64, 64])
    fw3c = dt_in("fw3c", [64, 2, NBATCH, 2, CB])
    dec_rep = dt_in("dec_rep", [64, 2, NBATCH, 2, CB])
    skip_rep = dt_in("skip_rep", [64, 2, NCH])
    cbf_d = dt_in("cbf", [128, CBF_W], BF16)
    cf_d = dt_in("cf", [128, CF_W])
    o_ff = dt_out("ff", [2 * NCH, L])
    o_yc = dt_out("yc", [2 * NCH, L])
    outs_b = []

    sb = lambda name, shape, dt=F32: nc.alloc_sbuf_tensor(name, shape, dt)
    cbf = sb("cbf_s", [128, CBF_W], BF16); cf = sb("cf_s", [128, CF_W]); b_c = Buf()
    S.dma("sp", cbf[:], cbf_d, writes=[b_c])
    S.dma("sp", cf[:], cf_d, writes=[b_c])
    cm = lambda n, rows=128: cbf[0:rows, CO[n][0]:CO[n][0] + CO[n][1]]
    cfm = lambda n: cf[:, CF[n][0]:CF[n][0] + CF[n][1]]
    ones_f = sb("ones_f", [64, 64]); b_ones = Buf()
    S.op("pool", lambda e: e.memset(ones_f[:], 1.0), writes=[b_ones])

    psum = Pool_(nc, "ps", 8, [128, 512], F32, psum=True)

    h2 = [sb("h2f", [64, L], BF16), sb("h2b", [64, L], BF16)]; b_h2 = [Buf(), Buf()]
    fw3s = sb("fw3s", [64, 2, NBATCH, 2 * CB], BF16); b_fw3 = Buf()
    dec = sb("dec", [64, 2, NBATCH, 2 * CB]); b_dec = Buf()
    skp = sb("skp", [64, 2, NCH]); b_skp = Buf()
    tpos = sb("tpos", [64, 2, 128]); b_tpos = Buf()
    S.dma("sp", skp[:], skip_rep, writes=[b_skp])
    S.dma("sp", dec[:], dec_rep.rearrange("p d b o c -> p d b (o c)"), writes=[b_dec])
    par = sb("mlp_par", [64, 4 + 64 + 64 + 64 + 2]); b_par = Buf()
    fq = sb("fq", [64, 2]); b_fq = Buf()
    f3st = sb("f3st", [64, 2, NBATCH, 2 * CB]); b_f3st = Buf()
    tpi = sb("tpi", [64, 2, 128], I32); b_tpi = Buf()
    b_ft = Buf()
    MW = 2048

    with nc.sbuf_tensor("mlp_tmp", [64, 16384], F32) as mt, nc.sbuf_tensor("fr_i", [64, MW], I32) as fr_i, \
            nc.sbuf_tensor("fr_f", [64, MW], F32) as fr_f:
        b_mt = Buf(); b_fr = Buf()

        def frac_neg(a, rows, bufs):
            S.op("dve", lambda e: e.tensor_copy(fr_i[0:rows, :], a), reads=bufs, writes=[b_fr])
            S.op("dve", lambda e: e.tensor_copy(fr_f[0:rows, :], fr_i[0:rows, :]), reads=[b_fr], writes=[b_fr])
            S.op("dve", lambda e: e.tensor_tensor(a, a, fr_f[0:rows, :], ALU.subtract), reads=bufs + [b_fr], writes=bufs)
            S.op("dve", lambda e: e.scalar_tensor_tensor(a, a, 0.5, a, ALU.is_gt, ALU.subtract), reads=bufs, writes=bufs)

        posi = mt[0:32, 0:8192].bitcast(I32)
        posi_t = mt[32:33, 0:8192].bitcast(I32)
        posf = mt[0:32, 8192:16384]
        feat_t = mt[32:33, 0:8192]
        S.dma("sp", par[:, 0:4], mvec, writes=[b_par])
        S.dma("sp", par[:, 4:68], fw2, writes=[b_par])
        S.dma("sp", par[0:32, 68:132], fw1b, writes=[b_par])
        S.dma("sp", par[32:33, 132:196], fw1a, writes=[b_par])
        S.dma("sp", par[0:32, 196:198], bsc, writes=[b_par])
        S.op("dve", lambda e: e.tensor_scalar(fq[:, 0:1], par[:, 1:2], 1.0 / (2 * math.pi), None, ALU.mult), reads=[b_par], writes=[b_fq])
        S.op("dve", lambda e: e.tensor_scalar(fq[:, 1:2], par[:, 3:4], 1.0 / (2 * math.pi), None, ALU.mult), reads=[b_par], writes=[b_fq])
        S.dma("sp", f3st[:], fw3c.rearrange("j d b o c -> j d b (o c)"), writes=[b_f3st])
        S.op("pool", lambda e: e.tensor_copy(fw3s[:], f3st[:]), reads=[b_f3st], writes=[b_fw3])
        S.dma("sp", tpi[:], pos_t.rearrange("d p s -> p d s"), writes=[b_tpi])
        S.op("dve", lambda e: e.tensor_copy(tpos[:], tpi[:]), reads=[b_tpi], writes=[b_tpos])
        S.op("dve", lambda e: e.tensor_scalar(tpos[:], tpos[:], 1.0 / L, None, ALU.mult), reads=[b_tpos], writes=[b_tpos])
        h1 = mt[0:64, 8192:16384]
        for d in range(2):
            S.dma("sp", posi, pos_rep[d], writes=[b_mt])
            S.dma("sp", posi_t, pos_rep[d, 0:1, :], writes=[b_mt])
            S.op("dve", lambda e: e.tensor_copy(posf, posi), reads=[b_mt], writes=[b_mt])
            S.op("dve", lambda e: e.tensor_copy(mt[32:33, 8192:16384], posi_t), reads=[b_mt], writes=[b_mt])
            S.op("dve", lambda e: e.tensor_scalar(feat_t, mt[32:33, 8192:16384], 1.0 / L, None, ALU.mult), reads=[b_mt], writes=[b_ft])
            S.op("dve", lambda e: e.tensor_scalar(posf, posf, par[0:32, 196:197], par[0:32, 197:198], ALU.mult, ALU.add), reads=[b_mt, b_par], writes=[b_mt])
            feats = mt[0:32, 0:8192]
            for q in range(L // MW):
                ws = slice(q * MW, (q + 1) * MW)
                frac_neg(posf[:, ws], 32, [b_mt])
                S.op("act", lambda e: e.activation(feats[:, ws], posf[:, ws], AF.Sin, scale=-2 * math.pi), reads=[b_mt], writes=[b_mt])
            for q in range(L // MW):
                hs = h1[:, q * MW:(q + 1) * MW]
                for ch in range(MW // 512):
                    sl = slice(q * MW + ch * 512, q * MW + (ch + 1) * 512)
                    pt, bpt = psum.get()
                    S.op("pe", lambda e: e.matmul(pt[0:64, :], lhsT=par[32:33, 132:196], rhs=feat_t[:, sl], start=True, stop=False),
                         reads=[b_par, b_ft], writes=[bpt])
                    S.op("pe", lambda e: e.matmul(pt[0:64, :], lhsT=par[0:32, 68:132], rhs=feats[:, sl], start=False, stop=True),
                         reads=[b_par, b_mt], writes=[bpt])
                    S.op("dve", lambda e: e.tensor_scalar(h1[:, sl], pt[0:64, :], par[:, 0:1], fq[:, 0:1], ALU.add, ALU.mult), reads=[bpt, b_par, b_fq], writes=[b_mt])
                S.op("dve", lambda e: e.tensor_scalar(hs, hs, 8.0, None, ALU.add), reads=[b_mt], writes=[b_mt])
                frac_neg(hs, 64, [b_mt])
                S.op("act", lambda e: e.activation(hs, hs, AF.Sin, scale=-2 * math.pi), reads=[b_mt], writes=[b_mt])
                pts = []
                for ch in range(MW // 512):
                    sl = slice(q * MW + ch * 512, q * MW + (ch + 1) * 512)
                    pt2, bpt2 = psum.get()
                    S.op("pe", lambda e: e.matmul(pt2[0:64, :], lhsT=par[:, 4:68], rhs=h1[:, sl], start=True, stop=True), reads=[b_par, b_mt], writes=[bpt2])
                    pts.append((pt2, bpt2, sl))
                for (pt2, bpt2, sl) in pts:
                    S.op("dve", lambda e: e.tensor_scalar(h1[:, sl], pt2[0:64, :], par[:, 2:3], fq[:, 1:2], ALU.add, ALU.mult), reads=[bpt2, b_par, b_fq], writes=[b_mt])
                S.op("dve", lambda e: e.tensor_scalar(hs, hs, 8.0, None, ALU.add), reads=[b_mt], writes=[b_mt])
                frac_neg(hs, 64, [b_mt])
                S.op("act", lambda e: e.activation(h2[d][:, q * MW:(q + 1) * MW], hs, AF.Sin, scale=-2 * math.pi), reads=[b_mt], writes=[b_h2[d]])
        b_mt_final = Buf()
        b_mt_final.w = b_mt.w
        b_mt_final.r = dict(b_mt.r)
        for k_, v_ in list(b_fr.r.items()) + ([b_fr.w] if b_fr.w else []):
            b_mt_final.r[k_] = max(b_mt_final.r.get(k_, 0), v_)

    S.op("dve", lambda e: e.tensor_scalar(f3st[:], dec[:], -1.0, None, ALU.mult), reads=[b_dec], writes=[b_f3st])
    S.op("dve", lambda e: e.tensor_tensor(dec[:], dec[:], f3st[:], ALU.max), reads=[b_dec, b_f3st], writes=[b_dec])
    Kf = [sb("Kf0", [128, 2, 2, CB, 128]), sb("Kf1", [128, 2, 2, CB, 128])]
    b_Kf = [[Buf(), Buf()], [Buf(), Buf()]]
    stag_p = Pool_(nc, "stag", 3, [128, CB, 256], F32)
    tmp_p = {e: Pool_(nc, "tmp" + e, n_, [128, CB * 128], F32) for e, n_ in (("dve", 3), ("pool", 2))}
    cb_p = Pool_(nc, "cb", 8, [128, CB, 128], BF16)
    xin_p = Pool_(nc, "xin", 12, [64, CB, 128], BF16)
    kt = [sb("ktf", [64, 2 * CB, 128]), sb("ktb", [64, 2 * CB, 128])]; b_kt = [Buf(), Buf()]
    ktb16 = [sb("ktf16", [64, 2 * CB, 128], BF16), sb("ktb16", [64, 2 * CB, 128], BF16)]; b_ktb = [Buf(), Buf()]
    win = sb("win", [64, 2 * CB, 128]); b_win = Buf()
    nrm = sb("nrm", [64, 4, 2 * CB]); b_nrm = Buf()
    yst = [sb("yst_r", [64, CB, 128]), sb("yst_i", [64, CB, 128])]; b_yst = [Buf(), Buf()]
    fo_p = Pool_(nc, "fo", 2, [128, CB, 64], BF16)
    for b in b_Kf[0] + b_Kf[1] + [b_kt[0], b_kt[1], b_ktb[0], b_ktb[1], b_win, b_nrm] + b_yst + \
            [t[1] for t in stag_p.t + tmp_p["dve"].t + tmp_p["pool"].t + cb_p.t + xin_p.t + fo_p.t]:
        b.w = b_mt_final.w
        b.r = dict(b_mt_final.r)

    def out_dma(dst, src, bsrc):
        b = Buf(); outs_b.append(b)
        S.dma("sp", dst, src, reads=[bsrc], writes=[b])

    def stage_data_stationary(chan_mms, nch, width, rows=128):
        st, bst = stag_p.get()
        per = 512 // width
        for c0 in range(0, nch, per):
            pt, bpt = psum.get()
            n = min(per, nch - c0)
            for i in range(n):
                mms = chan_mms(c0 + i)
                for q, (lhsT, rhs, rd) in enumerate(mms):
                    S.op("pe", lambda e, lhsT=lhsT, rhs=rhs, i=i, q=q: e.matmul(pt[0:rows, i * width:(i + 1) * width], lhsT=lhsT, rhs=rhs,
                                                                                 start=(q == 0), stop=(q == len(mms) - 1)),
                         reads=rd + [b_c], writes=[bpt])
            S.op("act", lambda e: e.copy(st[0:rows, c0:c0 + n, 0:width], pt[0:rows, 0:n * width].rearrange("p (c w) -> p c w", c=n)),
                 reads=[bpt], writes=[bst])
        return st, bst

    def cplx_mul(sr, si, tr, ti, rd, n1, nch, rows=128):
        dr, bdr = cb_p.get(); di, bdi = cb_p.get()
        drv = dr[0:rows, 0:nch, 0:n1]; div = di[0:rows, 0:nch, 0:n1]
        ta, bta = tmp_p["dve"].get(); tb, btb = tmp_p["dve"].get()
        tav = ta[0:rows, 0:nch * n1].rearrange("p (c k) -> p c k", c=nch); tbv = tb[0:rows, 0:nch * n1].rearrange("p (c k) -> p c k", c=nch)
        S.op("dve", lambda e: e.tensor_tensor(tav, sr, tr, ALU.mult), reads=rd, writes=[bta])
        S.op("dve", lambda e: e.tensor_tensor(tbv, si, ti, ALU.mult), reads=rd, writes=[btb])
        S.op("dve", lambda e: e.tensor_tensor(drv, tav, tbv, ALU.subtract), reads=[bta, btb], writes=[bdr])
        tc, btc = tmp_p["pool"].get(); td, btd = tmp_p["pool"].get()
        tcv = tc[0:rows, 0:nch * n1].rearrange("p (c k) -> p c k", c=nch); tdv = td[0:rows, 0:nch * n1].rearrange("p (c k) -> p c k", c=nch)
        S.op("pool", lambda e: e.tensor_tensor(tcv, sr, ti, ALU.mult), reads=rd, writes=[btc])
        S.op("pool", lambda e: e.tensor_tensor(tdv, si, tr, ALU.mult), reads=rd, writes=[btd])
        S.op("pool", lambda e: e.tensor_tensor(div, tcv, tdv, ALU.add), reads=[btc, btd], writes=[bdi])
        return (dr, bdr), (di, bdi)

    def bc(tab, nch, n1):
        return tab.unsqueeze(1).broadcast_to([128, nch, n1])

    def stageB_fwd(Ar, Ai, nch, n1, want_imag, evac):
        per = 512 // n1
        for g0 in range(0, nch, per):
            ng = min(per, nch - g0)
            rr = Ar[0][:, g0:g0 + ng, 0:n1]; ri = Ai[0][:, g0:g0 + ng, 0:n1]
            ptr, bptr = psum.get()
            o = ptr[:, 0:ng * n1].rearrange("p (c k) -> p c k", c=ng)
            S.op("pe", lambda e: e.matmul(o, lhsT=cm("C"), rhs=rr, start=True, stop=False), reads=[Ar[1], b_c], writes=[bptr])
            S.op("pe", lambda e: e.matmul(o, lhsT=cm("S"), rhs=ri, start=False, stop=True), reads=[Ai[1], b_c], writes=[bptr])
            pti = bpti = None
            if want_imag:
                pti, bpti = psum.get()
                o2 = pti[:, 0:ng * n1].rearrange("p (c k) -> p c k", c=ng)
                S.op("pe", lambda e: e.matmul(o2, lhsT=cm("C"), rhs=ri, start=True, stop=False), reads=[Ai[1], b_c], writes=[bpti])
                S.op("pe", lambda e: e.matmul(o2, lhsT=cm("nS"), rhs=rr, start=False, stop=True), reads=[Ar[1], b_c], writes=[bpti])
            evac(g0, ng, ptr, bptr, pti, bpti)

    def fwd_fft16k(chan_mms, nch, evac):
        st, bst = stage_data_stationary(chan_mms, nch, 256)
        yield
        Ar, Ai = cplx_mul(st[:, 0:nch, 0:128], st[:, 0:nch, 128:256], bc(cfm("T16r"), nch, 128), bc(cfm("T16i"), nch, 128), [bst, b_c], 128, nch)
        yield
        stageB_fwd(Ar, Ai, nch, 128, True, evac)
        yield

    def load_x(dram, row0):
        t, bt = xin_p.get()
        S.dma("sp", t[:], dram[row0:row0 + CB, :].rearrange("c (s1 s2) -> s1 c s2", s2=128), writes=[bt])
        return t, bt

    def long_conv(xa, xb, o, ga, gb, b):
        Kt = Kf[b % 2]; bK = b_Kf[b % 2][o]
        stB, bstB = stag_p.get()

        def evacB(g0, ng, ptr, bptr, pti, bpti):
            S.op("act", lambda e: e.copy(stB[:, g0:g0 + ng, 0:128], ptr[:, 0:ng * 128].rearrange("p (c k) -> p c k", c=ng)), reads=[bptr], writes=[bstB])
            S.op("act", lambda e: e.copy(stB[:, g0:g0 + ng, 128:256], pti[:, 0:ng * 128].rearrange("p (c k) -> p c k", c=ng)), reads=[bpti], writes=[bstB])
        yield from fwd_fft16k(lambda c: [(xa[0][:, c, :], cm("FA", 64), [xa[1]]), (xb[0][:, c, :], cm("FAi", 64), [xb[1]])], CB, evacB)
        Pr, Pi = cplx_mul(stB[:, :, 0:128], stB[:, :, 128:256], Kt[:, o, 0, :, :], Kt[:, o, 1, :, :], [bstB, bK], 128, CB)
        yield
        st, bst = stage_data_stationary(lambda c: [(Pr[0][:, c, :], cm("IA1"), [Pr[1]]), (Pi[0][:, c, :], cm("IA2"), [Pi[1]])], CB, 256)
        yield
        Br, Bi = cplx_mul(st[:, :, 0:128], st[:, :, 128:256], bc(cfm("T16r"), CB, 128), bc(cfm("T16ci"), CB, 128), [bst, b_c], 128, CB)
        yield
        for g0 in range(0, CB, 4):
            pt, bpt = psum.get()
            o4 = pt[0:64, :].rearrange("p (c k) -> p c k", c=4)
            S.op("pe", lambda e: e.matmul(o4, lhsT=cm("IBc"), rhs=Br[0][:, g0:g0 + 4, :], start=True, stop=False), reads=[Br[1], b_c], writes=[bpt])
            S.op("pe", lambda e: e.matmul(o4, lhsT=cm("IBs"), rhs=Bi[0][:, g0:g0 + 4, :], start=False, stop=True), reads=[Bi[1], b_c], writes=[bpt])
            S.op("act", lambda e: e.copy(yst[0][:, g0:g0 + 4, :], o4), reads=[bpt], writes=[b_yst[0]])
            pt2, bpt2 = psum.get()
            o5 = pt2[0:64, :].rearrange("p (c k) -> p c k", c=4)
            S.op("pe", lambda e: e.matmul(o5, lhsT=cm("IBc"), rhs=Bi[0][:, g0:g0 + 4, :], start=True, stop=False), reads=[Bi[1], b_c], writes=[bpt2])
            S.op("pe", lambda e: e.matmul(o5, lhsT=cm("IBsp"), rhs=Br[0][:, g0:g0 + 4, :], start=False, stop=True), reads=[Br[1], b_c], writes=[bpt2])
            S.op("act", lambda e: e.copy(yst[1][:, g0:g0 + 4, :], o5), reads=[bpt2], writes=[b_yst[1]])
        yield
        res = []
        skb = skp[:, o, b * CB:(b + 1) * CB].unsqueeze(2).broadcast_to([64, CB, 128])
        for h, (xin, gate) in enumerate(((xa, ga), (xb, gb))):
            z, bz = xin_p.get()
            tq, btq = tmp_p["dve"].get()
            tv = tq[0:64, :].rearrange("p (c k) -> p c k", c=CB)
            S.op("dve", lambda e: e.tensor_tensor(tv, xin[0][:], skb, ALU.mult), reads=[xin[1], b_skp], writes=[btq])
            S.op("dve", lambda e: e.tensor_tensor(tv, tv, yst[h][:], ALU.add), reads=[btq, b_yst[h]], writes=[btq])
            S.op("dve", lambda e: e.tensor_tensor(z[:], tv, gate[0][:], ALU.mult), reads=[btq, gate[1]], writes=[bz])
            res.append((z, bz))
        yield
        return res

    def filter_chain(b):
        Kt = Kf[b % 2]
        for d in range(2):
            S.op("dve", lambda e: e.tensor_tensor(win[:], tpos[:, d, :].unsqueeze(1).broadcast_to([64, 2 * CB, 128]),
                                                  dec[:, d, b, :].unsqueeze(2).broadcast_to([64, 2 * CB, 128]), ALU.mult),
                 reads=[b_tpos, b_dec], writes=[b_win])
            S.op("act", lambda e: e.activation(win[:], win[:], AF.Exp, scale=-1.0), reads=[b_win], writes=[b_win])
            for s20 in range(0, 128, 32):
                pt, bpt = psum.get()
                for q in range(32):
                    s2 = s20 + q
                    S.op("pe", lambda e, s2=s2, q=q: e.matmul(pt[0:64, q * 16:(q + 1) * 16], lhsT=h2[d][:, s2:L:128], rhs=fw3s[:, d, b, :],
                                                              start=True, stop=True), reads=[b_h2[d], b_fw3], writes=[bpt])
                S.op("dve", lambda e: e.tensor_tensor(kt[d][:, :, s20:s20 + 32].rearrange("p c s -> p s c"),
                                                      pt[0:64, :].rearrange("p (s c) -> p s c", c=16),
                                                      win[:, :, s20:s20 + 32].rearrange("p c s -> p s c"), ALU.mult),
                     reads=[bpt, b_win], writes=[b_kt[d]])
                yield
            if d == 1:
                S.op("dve", lambda e: e.memset(kt[1][0:1, :, 0:1], 0.0), reads=[], writes=[b_kt[1]])
            S.op("dve", lambda e: e.tensor_tensor(win[:], kt[d][:], kt[d][:], ALU.mult), reads=[b_kt[d]], writes=[b_win])
            S.op("dve", lambda e: e.tensor_reduce(nrm[:, d, :], win[:], axis=AX.X, op=ALU.add), reads=[b_win], writes=[b_nrm])
            yield
        S.op("dve", lambda e: e.tensor_tensor(nrm[:, 2, :], nrm[:, 0, :], nrm[:, 1, :], ALU.add), reads=[b_nrm], writes=[b_nrm])
        pt, bpt = psum.get()
        S.op("pe", lambda e: e.matmul(pt[0:64, 0:2 * CB], lhsT=ones_f[:], rhs=nrm[:, 2, :], start=True, stop=True), reads=[b_ones, b_nrm], writes=[bpt])
        S.op("dve", lambda e: e.tensor_scalar(nrm[:, 3, :], pt[0:64, 0:2 * CB], 1e-6, None, ALU.add), reads=[bpt], writes=[b_nrm])
        S.op("dve", lambda e: e.reciprocal(nrm[:, 3, :], nrm[:, 3, :]), reads=[b_nrm], writes=[b_nrm])
        S.op("act", lambda e: e.activation(nrm[:, 3, :], nrm[:, 3, :], AF.Sqrt), reads=[b_nrm], writes=[b_nrm])
        yield
        for d in range(2):
            S.op("dve", lambda e: e.tensor_tensor(ktb16[d][:], kt[d][:], nrm[:, 3, :].unsqueeze(2).broadcast_to([64, 2 * CB, 128]), ALU.mult),
                 reads=[b_kt[d], b_nrm], writes=[b_ktb[d]])
        yield
        for o in range(2):
            def evacK(g0, ng, ptr, bptr, pti, bpti, o=o):
                S.op("act", lambda e: e.copy(Kt[:, o, 0, g0:g0 + ng, :], ptr[:, 0:ng * 128].rearrange("p (c k) -> p c k", c=ng)), reads=[bptr], writes=[b_Kf[b % 2][o]])
                S.op("act", lambda e: e.copy(Kt[:, o, 1, g0:g0 + ng, :], pti[:, 0:ng * 128].rearrange("p (c k) -> p c k", c=ng)), reads=[bpti], writes=[b_Kf[b % 2][o]])
            yield from fwd_fft16k(lambda c, o=o: [(ktb16[0][:, o * CB + c, :], cm("FA", 64), [b_ktb[0]]),
                                                  (ktb16[1][:, o * CB + c, :], cm("FAhi", 64), [b_ktb[1]])], CB, evacK)

    def conv_chain(b):
        r0 = [b * CB, NCH + b * CB]
        v = [load_x(hv_d, r) for r in r0]
        x1 = [load_x(hx1_d, r) for r in r0]
        x2 = [load_x(hx2_d, r) for r in r0]
        yield
        z1 = yield from long_conv(v[0], v[1], 0, x1[0], x1[1], b)
        z2 = yield from long_conv(z1[0], z1[1], 1, x2[0], x2[1], b)
        for h in range(2):
            out_dma(o_yc[r0[h]:r0[h] + CB, :].rearrange("c (s1 s2) -> s1 c s2", s2=128), z2[h][0][:], z2[h][1])
        yield

    def fnet_chain(b):
        for h in range(2):
            row = h * NCH + b * CB
            zr, bzr = load_x(zr_d, row)
            zi, bzi = load_x(zi_d, row)
            yield
            st, bst = stage_data_stationary(lambda c: [(zr[:, c, :], cm("FN1", 64), [bzr]), (zi[:, c, :], cm("FN2", 64), [bzi])], CB, 128)
            yield
            Ar, Ai = cplx_mul(st[:, :, 0:64], st[:, :, 64:128], bc(cfm("T8r"), CB, 64), bc(cfm("T8i"), CB, 64), [bst, b_c], 64, CB)
            yield
            fo, bfo = fo_p.get()

            def evacF(g0, ng, ptr, bptr, pti, bpti):
                S.op("act", lambda e: e.copy(fo[:, g0:g0 + ng, :], ptr[:, 0:ng * 64].rearrange("p (c k) -> p c k", c=ng)), reads=[bptr], writes=[bfo])
            stageB_fwd(Ar, Ai, CB, 64, False, evacF)
            out_dma(o_ff[row:row + CB, :].rearrange("c (k2 k1) -> k2 c k1", k1=64), fo[:], bfo)
            yield

    def run_all(gens):
        gens = [g for g in gens if g is not None]
        while gens:
            for g in list(gens):
                try:
                    next(g)
                except StopIteration:
                    gens.remove(g)

    run_all([filter_chain(0)])
    for b in range(NBATCH):
        run_all([conv_chain(b), filter_chain(b + 1) if b + 1 < NBATCH else None, fnet_chain(b)])

    S.finish(outs_b, "sp")
    return S

D = 2048
G = 1024
DM = 4096
NT = 2048
TBK = 512
EPS = 1e-6


def build_L3(nc):
    S = Sched(nc)
    dt_in = lambda name, shape, dt=F32: nc.dram_tensor(name, shape, dt, kind="ExternalInput").ap()
    ffc = dt_in("ffc", [G, NT], BF16); ycc = dt_in("ycc", [G, NT], BF16)
    sgB = dt_in("sgB", [G, NT], BF16); sgC = dt_in("sgC", [G, NT], BF16)
    yAg = dt_in("yAg", [G, NT], BF16); yMg = dt_in("yMg", [G, NT], BF16)
    x_d = dt_in("x", [NT, D])
    fnet_w = dt_in("fnet_w", [G, G])
    vec = dt_in("vec3", [128, 3, 8])
    w_out = dt_in("w_out", [DM, D])
    post_g_bc = dt_in("post_g_bc", [128, D])
    xo = nc.dram_tensor("xo", [NT, D], F32, kind="ExternalOutput").ap()
    wsc = nc.dram_tensor("w_out_bf", [DM, D], BF16, kind="Internal").ap()
    outs_b = []

    sb = lambda name, shape, dt=F32: nc.alloc_sbuf_tensor(name, shape, dt)
    ones_f = sb("ones_f", [128, 128]); b_ones = Buf()
    S.op("pool", lambda e: e.memset(ones_f[:], 1.0), writes=[b_ones])
    vec_s = sb("vec_s", [128, 3, 8]); b_vec = Buf()
    S.dma("sp", vec_s[:], vec, writes=[b_vec])
    pg = sb("pg", [128, D]); b_pg = Buf()
    S.dma("sp", pg[:], post_g_bc, writes=[b_pg])
    psum = Pool_(nc, "ps", 8, [128, 512], F32, psum=True)
    fw_bf = sb("fw_bf", [128, 8, G], BF16); b_fw = Buf()

    b_wsc = [Buf() for _ in range(32)]
    for k in range(8):
        S.dma("pool", fw_bf[:, k, :], fnet_w[k * 128:(k + 1) * 128, :], writes=[b_fw])
    for q in range(8):
        S.dma("pool", wsc[q * 512:(q + 1) * 512, :], w_out[q * 512:(q + 1) * 512, :], writes=b_wsc[q * 4:(q + 1) * 4])
    last_scr = []

    ygall = sb("ygall", [128, 32, TBK], BF16); b_yg = [Buf() for _ in range(4)]
    wt_p = Pool_(nc, "wt", 4, [128, 8, 512], BF16)
    outraw = sb("outraw", [128, 4, D]); b_or = [Buf() for _ in range(4)]
    x_p = Pool_(nc, "xt", 2, [128, D], F32)
    in_p = Pool_(nc, "inb", 2, [128, 8, TBK], BF16)
    sg_p = Pool_(nc, "sgb", 2, [128, 8, TBK], BF16)
    yb = sb("yb", [128, 8, TBK]); b_yb = Buf()
    acc = sb("acc", [128, TBK]); b_acc = Buf()
    sq_p = Pool_(nc, "sq", 2, [128, TBK], F32)
    st_p = Pool_(nc, "st", 4, [128, 4], F32)
    junk = sb("junk", [128, D], BF16); b_junk = Buf()
    for b in b_yg + b_or + [b_yb, b_acc, b_junk] + [t[1] for t in wt_p.t + x_p.t + in_p.t + sg_p.t + sq_p.t + st_p.t]:
        for ls in last_scr:
            if ls.w is not None:
                b.r[ls.w[0]] = max(b.r.get(ls.w[0], 0), ls.w[1])
            for k_, v_ in ls.r.items():
                b.r[k_] = max(b.r.get(k_, 0), v_)

    def colsum(src_ap, bsrc, first):
        pt, bpt = psum.get()
        S.op("pe", lambda e: e.matmul(pt[:], lhsT=ones_f[:], rhs=src_ap, start=True, stop=True), reads=[b_ones] + bsrc, writes=[bpt])
        if first:
            S.op("dve", lambda e: e.tensor_copy(acc[:], pt[:]), reads=[bpt], writes=[b_acc])
        else:
            S.op("dve", lambda e: e.tensor_tensor(acc[:], acc[:], pt[:], ALU.add), reads=[bpt, b_acc], writes=[b_acc])

    def rstd_acc():
        S.op("dve", lambda e: e.tensor_scalar(acc[:], acc[:], 1.0 / G, EPS, ALU.mult, ALU.add), reads=[b_acc], writes=[b_acc])
        S.op("dve", lambda e: e.reciprocal(acc[:], acc[:]), reads=[b_acc], writes=[b_acc])
        S.op("act", lambda e: e.activation(acc[:], acc[:], AF.Sqrt), reads=[b_acc], writes=[b_acc])

    for blk in range(NT // TBK):
        ts = slice(blk * TBK, (blk + 1) * TBK)
        ld = lambda dram: dram[:, ts].rearrange("(k p) t -> p k t", p=128)
        S.dma("sp", ygall[:, 0:8, :], ld(yAg), writes=[b_yg[0]])
        S.dma("sp", ygall[:, 24:32, :], ld(yMg), writes=[b_yg[3]])
        ff, bff = in_p.get()
        S.dma("sp", ff[:], ld(ffc), writes=[bff])
        sg, bsg = sg_p.get()
        S.dma("sp", sg[:], ld(sgB), writes=[bsg])
        for j in range(8):
            pt, bpt = psum.get()
            for k in range(8):
                S.op("pe", lambda e, k=k: e.matmul(pt[:], lhsT=fw_bf[:, k, j * 128:(j + 1) * 128], rhs=ff[:, k, :], start=(k == 0), stop=(k == 7)),
                     reads=[b_fw, bff], writes=[bpt])
            S.op("act", lambda e: e.activation(yb[:, j, :], pt[:], AF.Identity, bias=vec_s[:, 0, j:j + 1]), reads=[bpt, b_vec], writes=[b_yb])
            sq, bsq = sq_p.get()
            S.op("act", lambda e: e.activation(sq[:], yb[:, j, :], AF.Square), reads=[b_yb], writes=[bsq])
            colsum(sq[:], [bsq], j == 0)
        rstd_acc()
        for j in range(8):
            sq, bsq = sq_p.get()
            S.op("dve", lambda e: e.scalar_tensor_tensor(sq[:], yb[:, j, :], vec_s[:, 1, j:j + 1], acc[:], ALU.mult, ALU.mult),
                 reads=[b_yb, b_vec, b_acc], writes=[bsq])
            S.op("dve", lambda e: e.tensor_tensor(ygall[:, 8 + j, :], sq[:], sg[:, j, :], ALU.mult), reads=[bsq, bsg], writes=[b_yg[1]])
        yc, byc = in_p.get()
        S.dma("sp", yc[:], ld(ycc), writes=[byc])
        sg, bsg = sg_p.get()
        S.dma("sp", sg[:], ld(sgC), writes=[bsg])
        for j in range(8):
            sq, bsq = sq_p.get()
            S.op("act", lambda e: e.activation(sq[:], yc[:, j, :], AF.Square), reads=[byc], writes=[bsq])
            colsum(sq[:], [bsq], j == 0)
        rstd_acc()
        for j in range(8):
            sq, bsq = sq_p.get()
            S.op("dve", lambda e: e.scalar_tensor_tensor(sq[:], yc[:, j, :], vec_s[:, 2, j:j + 1], acc[:], ALU.mult, ALU.mult),
                 reads=[byc, b_vec, b_acc], writes=[bsq])
            S.op("dve", lambda e: e.tensor_tensor(ygall[:, 16 + j, :], sq[:], sg[:, j, :], ALU.mult), reads=[bsq, bsg], writes=[b_yg[2]])
        for ng in range(4):
            pts = [psum.get() for _ in range(4)]
            for kq in range(4):
                wt, bwt = wt_p.get()
                S.dma("sp", wt[:], wsc[kq * 1024:(kq + 1) * 1024, ng * 512:(ng + 1) * 512].rearrange("(k p) n -> p k n", p=128),
                      reads=b_wsc[kq * 8:(kq + 1) * 8], writes=[bwt])
                for tt in range(4):
                    pt, bpt = pts[tt]
                    for k in range(8):
                        kk = kq * 8 + k
                        S.op("pe", lambda e, k=k, kk=kk, tt=tt, pt=pt: e.matmul(pt[:], lhsT=ygall[:, kk, tt * 128:(tt + 1) * 128], rhs=wt[:, k, :],
                                                                                 start=(kk == 0), stop=(kk == 31)),
                             reads=[b_yg[kq], bwt], writes=[bpt])
            for tt in range(4):
                pt, bpt = pts[tt]
                S.op("act", lambda e, tt=tt, pt=pt: e.copy(outraw[:, tt, ng * 512:(ng + 1) * 512], pt[:]), reads=[bpt], writes=[b_or[tt]])
        for tt in range(4):
            r0 = blk * TBK + tt * 128
            xt, bxt = x_p.get()
            S.dma("sp", xt[:], x_d[r0:r0 + 128, :], writes=[bxt])
            stt, bstt = st_p.get()
            S.op("dve", lambda e: e.memset(stt[:], 0.0), writes=[bstt])
            S.op("act", lambda e: e.activation(junk[:], outraw[:, tt, :], AF.Square, accum_out=stt[:, 0:1]), reads=[b_or[tt], bstt], writes=[b_junk, bstt])
            S.op("dve", lambda e: e.tensor_scalar(stt[:, 1:2], stt[:, 0:1], 1.0 / D, EPS, ALU.mult, ALU.add), reads=[bstt], writes=[bstt])
            S.op("dve", lambda e: e.reciprocal(stt[:, 2:3], stt[:, 1:2]), reads=[bstt], writes=[bstt])
            S.op("act", lambda e: e.activation(stt[:, 2:3], stt[:, 2:3], AF.Sqrt), reads=[bstt], writes=[bstt])
            S.op("dve", lambda e: e.scalar_tensor_tensor(outraw[:, tt, :], outraw[:, tt, :], stt[:, 2:3], pg[:], ALU.mult, ALU.mult),
                 reads=[b_or[tt], bstt, b_pg], writes=[b_or[tt]])
            S.op("dve", lambda e: e.tensor_tensor(xt[:], xt[:], outraw[:, tt, :], ALU.add), reads=[bxt, b_or[tt]], writes=[bxt])
            b = Buf(); outs_b.append(b)
            S.dma("sp", xo[r0:r0 + 128, :], xt[:], reads=[bxt], writes=[b])

    S.finish(outs_b, "sp")
    return S
import numpy as np
D=2048; G=1024; L=8192; TB=1024; H=16; NB=2

def chunked(v, nch):
    return np.ascontiguousarray(v.reshape(nch, 128).T)

def dftG_const():
    k = np.arange(G)
    ang = 2*np.pi*((k[:,None]*k[None,:]) % G)/G
    sc = 1.0/np.sqrt(float(L)*G)
    return np.concatenate([np.cos(ang)*sc, -np.sin(ang)*sc], axis=1).astype(np.float32)

def l1_inputs(inp, l, core, x_cur):
    b = core // 4; j = core % 4
    xb = x_cur[b]
    xpad = np.concatenate([np.zeros((H, D), np.float32), xb, np.zeros((H, D), np.float32)], 0)
    xp = np.stack([xpad[j*2048 + blk*TB : j*2048 + blk*TB + TB + 2*H] for blk in range(NB)], 0)
    cw = np.ascontiguousarray(inp['conv_dw_w'][l].T.reshape(8, 128, 31).transpose(1, 0, 2))
    gn = inp['group_norm_g'][l]
    cvec = np.stack([chunked(inp['conv_dw_b'][l], 8), chunked(inp['conv_ln_g'][l], 8), chunked(inp['conv_ln_b'][l], 8),
                     chunked(inp['conv_pw_b'][l], 8), chunked(gn[0:G], 8)], axis=1)
    hw = np.concatenate([inp['hy_short_w'][l], inp['hy_short_b'][l][None]], 0)
    hyw = np.ascontiguousarray(hw.T.reshape(24, 128, 4).transpose(1, 0, 2))
    return {
        "xp": np.ascontiguousarray(xp), "w_in": inp['w_in'][l],
        "pre_g_bc": np.ascontiguousarray(np.broadcast_to(inp['pre_norm_g'][l], (128, D))),
        "conv_w": cw, "conv_vec": np.ascontiguousarray(cvec), "conv_pw_w": inp['conv_pw_w'][l],
        "hy_w": hyw, "mem": inp['mem'][b],
        "mem_g_bc": np.ascontiguousarray(np.broadcast_to(inp['mem_norm_g'], (128, D))),
        "mem_wk": inp['mem_wk'][l], "mem_wv": inp['mem_wv'][l],
        "gD": chunked(gn[3*G:4*G], 8), "dftG": dftG_const(),
    }

BF = ml_dtypes.bfloat16
NCH=128; CB=8; NBATCH=NCH//CB

def l2_consts():
    j = np.arange(128)
    ang = 2*np.pi*((j[:,None]*j[None,:]) % 128)/128.0
    C = np.cos(ang); Sn = np.sin(ang)
    j64 = np.arange(64)
    ang64 = 2*np.pi*((j64[:,None]*j64[None,:]) % 64)/64.0
    C64 = np.cos(ang64); S64 = np.sin(ang64)
    cb = np.zeros((128, CBF_W), np.float64)
    def put(name, m):
        o, w = CO[name]; assert m.shape[1] == w, (name, m.shape); cb[:m.shape[0], o:o+w] = m
    FA = np.concatenate([C, -Sn], 1)
    put("FA", FA); put("FAhi", FA[64:128]); put("C", C); put("S", Sn); put("nS", -Sn)
    put("IA1", np.concatenate([C, Sn], 1)); put("IA2", np.concatenate([-Sn, C], 1))
    put("IBc", C[:, :64]/16384.0); put("IBs", -Sn[:, :64]/16384.0)
    put("FN1", np.concatenate([C64, -S64], 1)); put("FN2", np.concatenate([S64, C64], 1))
    put("FAi", np.concatenate([Sn, C], 1)); put("IBsp", Sn[:, :64]/16384.0)
    cf = np.zeros((128, CF_W), np.float64)
    a16 = 2*np.pi*(j[:,None]*j[None,:])/16384.0
    a8 = 2*np.pi*(j[:,None]*j64[None,:])/8192.0
    def putf(name, m):
        o, w = CF[name]; cf[:, o:o+w] = m
    putf("T16r", np.cos(a16)); putf("T16i", -np.sin(a16)); putf("T16ci", np.sin(a16))
    putf("T8r", np.cos(a8)); putf("T8i", -np.sin(a8))
    return cb.astype(BF), cf.astype(np.float32)

def l2_inputs(inp, l, core, Zb, hyb):
    cs = slice(core*NCH, (core+1)*NCH)
    def rows(arrs, off):
        return np.ascontiguousarray(np.concatenate([a[off + core*NCH: off + (core+1)*NCH] for a in arrs], 0))
    pos = inp['positions'].astype(np.int32)
    posb = pos[(L - np.arange(L)) % L]
    pos_rep = np.stack([np.broadcast_to(pos, (32, L)), np.broadcast_to(posb, (32, L))], 0)
    pos_t = np.stack([pos.reshape(64, 128), posb.reshape(64, 128)], 0)
    bands = np.linspace(1e-4, 15, 16, dtype=np.float32)
    bsc = np.zeros((32, 2), np.float32)
    bsc[:, 0] = np.concatenate([bands, bands]) / np.float32(L)
    bsc[:16, 1] = 0.25
    fw1 = inp['hy_fw1'][l]
    mvec = np.stack([inp['hy_fb1'][l], inp['hy_freq1'][l], inp['hy_fb2'][l], inp['hy_freq2'][l]], 1)
    fw3 = inp['hy_fw3'][l].reshape(64, 2, 2, 1024)[:, :, :, cs]
    fw3c = fw3.reshape(64, 2, 2, NBATCH, CB).transpose(0, 2, 3, 1, 4)
    dec = inp['hy_decay'][l][:, :, cs]
    decc = dec.reshape(2, 2, NBATCH, CB).transpose(1, 2, 0, 3)
    cbf, cf = l2_consts()
    return {
        "zr": rows(Zb, 0), "zi": rows(Zb, 1024),
        "hv": rows(hyb, 0), "hx1": rows(hyb, 1024), "hx2": rows(hyb, 2048),
        "pos_rep": np.ascontiguousarray(pos_rep), "pos_t": np.ascontiguousarray(pos_t), "bsc": bsc,
        "fw1a": np.ascontiguousarray(fw1[0:1]), "fw1b": np.ascontiguousarray(fw1[1:33]), "mvec": np.ascontiguousarray(mvec),
        "fw2": inp['hy_fw2'][l], "fw3c": np.ascontiguousarray(fw3c),
        "dec_rep": np.ascontiguousarray(np.broadcast_to(decc, (64,) + decc.shape)),
        "skip_rep": np.ascontiguousarray(np.broadcast_to(inp['hy_skip'][l][:, cs], (64, 2, NCH))),
        "cbf": cbf, "cf": cf,
    }


def l3_inputs(inp, l, core, x_cur, ffb, ycb, l1o):
    b = core // 4; j = core % 4
    ts = slice(j*2048, (j+1)*2048)
    gn = inp['group_norm_g'][l]
    vec3 = np.stack([chunked(inp['fnet_b'][l], 8), chunked(gn[1024:2048], 8), chunked(gn[2048:3072], 8)], axis=1)
    return {
        "ffc": np.ascontiguousarray(ffb[:, ts]), "ycc": np.ascontiguousarray(ycb[:, ts]),
        "sgB": l1o["sgB"], "sgC": l1o["sgC"], "yAg": l1o["yAg"], "yMg": l1o["yMg"],
        "x": np.ascontiguousarray(x_cur[b][ts]), "fnet_w": inp['fnet_w'][l], "vec3": np.ascontiguousarray(vec3),
        "w_out": inp['w_out'][l],
        "post_g_bc": np.ascontiguousarray(np.broadcast_to(inp['post_norm_g'][l], (128, 2048))),
    }

_PROGS = {}


def _prog(name, builder):
    if name not in _PROGS:
        nc = bass.Bass("TRN2", target_bir_lowering=False)
        builder(nc)
        _PROGS[name] = nc
    return _PROGS[name]


def kernel(**inputs):
    inp = {k: np.asarray(v) for k, v in inputs.items()}
    x_cur = np.ascontiguousarray(inp['x'], dtype=np.float32)
    cores = list(range(8))
    nc1 = _prog("L1", build_L1); nc2 = _prog("L2", build_L2); nc3 = _prog("L3", build_L3)
    for l in range(2):
        r1 = run_bass_kernel_spmd(nc1, [l1_inputs(inp, l, c, x_cur) for c in cores], core_ids=cores).results
        Zb = [np.concatenate([np.asarray(r1[4 * b + j]["Z"]) for j in range(4)], axis=1) for b in range(2)]
        hyb = [np.concatenate([np.asarray(r1[4 * b + j]["hyc"]) for j in range(4)], axis=1) for b in range(2)]
        r2 = run_bass_kernel_spmd(nc2, [l2_inputs(inp, l, c, Zb, hyb) for c in cores], core_ids=cores).results
        ffb = [np.concatenate([np.asarray(r2[c]["ff"])[b * NCH:(b + 1) * NCH] for c in cores], axis=0) for b in range(2)]
        ycb = [np.concatenate([np.asarray(r2[c]["yc"])[b * NCH:(b + 1) * NCH] for c in cores], axis=0) for b in range(2)]
        l1o = [{k: np.asarray(r1[c][k]) for k in ("sgB", "sgC", "yAg", "yMg")} for c in cores]
        r3 = run_bass_kernel_spmd(nc3, [l3_inputs(inp, l, c, x_cur, ffb[c // 4], ycb[c // 4], l1o[c]) for c in cores],
                                  core_ids=cores).results
        x_cur = np.stack([np.concatenate([np.asarray(r3[4 * b + j]["xo"]) for j in range(4)], axis=0) for b in range(2)], axis=0)
    return np.ascontiguousarray(x_cur, dtype=np.float32)
```

```python
import math
import ml_dtypes
import numpy as np
import concourse.bass as bass
import concourse.mybir as mybir
from concourse.bass_utils import run_bass_kernel_spmd

F32 = mybir.dt.float32
BF16 = mybir.dt.bfloat16
I32 = mybir.dt.int32
AF = mybir.ActivationFunctionType
ALU = mybir.AluOpType
AX = mybir.AxisListType


class Buf:
    __slots__ = ("name", "w", "r")

    def __init__(self, name=""):
        self.name = name
        self.w = None
        self.r = {}


class Sched:
    SEM_LIMIT = 30000

    def __init__(self, nc, n_dma_sems=40, same_engine_sync=True):
        self.nc = nc
        self.engs = {"pe": nc.tensor, "act": nc.scalar, "dve": nc.vector,
                     "pool": nc.gpsimd, "sp": nc.sync}
        self.same_engine_sync = same_engine_sync
        self.sems = {}
        self.cur = {}
        self.nsem = 0
        for e in self.engs:
            self._new_eng_sem(e)
        self.dma_sems = []
        for i in range(n_dma_sems):
            k = ("dma", i)
            self.sems[k] = nc.alloc_semaphore(f"dq{i}")
            self.dma_sems.append([k, 0])
        self.dma_rr = 0
        self.waited = {e: {} for e in self.engs}
        self.n_inst = {e: 0 for e in self.engs}
        self.n_wait = {e: 0 for e in self.engs}

    def _new_eng_sem(self, e):
        k = (e, self.nsem)
        self.nsem += 1
        self.sems[k] = self.nc.alloc_semaphore(f"s_{e}_{k[1]}")
        self.cur[e] = [k, 0]

    def _wait(self, e, tok):
        if tok is None:
            return
        k, v = tok
        if not self.same_engine_sync and k[0] == e:
            return
        if e == "pe" and k[0] == "pe":
            return
        if self.waited[e].get(k, 0) >= v:
            return
        self.engs[e].wait_ge(self.sems[k], v)
        self.waited[e][k] = v
        self.n_wait[e] += 1

    def _deps(self, e, reads, writes):
        for b in reads:
            self._wait(e, b.w)
        for b in writes:
            self._wait(e, b.w)
            for k, v in b.r.items():
                self._wait(e, (k, v))

    def _mark(self, tok, reads, writes):
        k, v = tok
        for b in reads:
            if b.r.get(k, 0) < v:
                b.r[k] = v
        for b in writes:
            b.w = tok
            b.r = {}

    def op(self, e, fn, reads=(), writes=()):
        self._deps(e, reads, writes)
        ins = fn(self.engs[e])
        c = self.cur[e]
        c[1] += 1
        ins.then_inc(self.sems[c[0]], 1)
        tok = (c[0], c[1])
        self._mark(tok, reads, writes)
        self.n_inst[e] += 1
        if c[1] >= self.SEM_LIMIT:
            self._new_eng_sem(e)
        return tok

    def dma(self, q, out, in_, reads=(), writes=(), **kw):
        self._deps(q, reads, writes)
        slot = self.dma_sems[self.dma_rr]
        self.dma_rr = (self.dma_rr + 1) % len(self.dma_sems)
        k, uses = slot
        if uses > 0:
            self._wait(q, (k, 16 * uses))
        ins = self.engs[q].dma_start(out=out, in_=in_, **kw)
        slot[1] = uses + 1
        ins.then_inc(self.sems[k], 16)
        tok = (k, 16 * (uses + 1))
        self._mark(tok, reads, writes)
        self.n_inst[q] += 1
        return tok

    def finish(self, bufs, e="sp"):
        for b in bufs:
            self._wait(e, b.w)

D = 2048
NIN = 11264
G = 1024
TB = 1024
H = 16
TBH = TB + 2 * H
NB = 2
EPS = 1e-6
C_AVAL, C_AGATE, C_F, C_HY, C_Q, C_GATE = 0, 1024, 2048, 3072, 6144, 7168


class Pool_:
    def __init__(self, nc, name, n, shape, dtype, psum=False):
        self.t = []
        for i in range(n):
            if psum:
                h = nc.alloc_psum_tensor(f"{name}{i}", shape, dtype)
            else:
                h = nc.alloc_sbuf_tensor(f"{name}{i}", shape, dtype)
            self.t.append((h, Buf(f"{name}{i}")))
        self.i = 0

    def get(self):
        r = self.t[self.i]
        self.i = (self.i + 1) % len(self.t)
        return r


def build_L1(nc):
    S = Sched(nc)
    dt_in = lambda name, shape, dt=F32: nc.dram_tensor(name, shape, dt, kind="ExternalInput").ap()
    dt_out = lambda name, shape, dt=BF16: nc.dram_tensor(name, shape, dt, kind="ExternalOutput").ap()
    xp = dt_in("xp", [NB, TBH, D])
    w_in = dt_in("w_in", [D, NIN])
    pre_g_bc = dt_in("pre_g_bc", [128, D])
    cw = dt_in("conv_w", [128, 8, 31])
    cvec = dt_in("conv_vec", [128, 5, 8])
    pw = dt_in("conv_pw_w", [G, G])
    hyw = dt_in("hy_w", [128, 24, 4])
    memx = dt_in("mem", [256, D])
    mem_g_bc = dt_in("mem_g_bc", [128, D])
    wk = dt_in("mem_wk", [D, G])
    wv = dt_in("mem_wv", [D, G])
    gD = dt_in("gD", [128, 8])
    dftG = dt_in("dftG", [G, 2 * G])
    o_yA = dt_out("yAg", [G, NB * TB])
    o_yM = dt_out("yMg", [G, NB * TB])
    o_sgB = dt_out("sgB", [G, NB * TB])
    o_sgC = dt_out("sgC", [G, NB * TB])
    o_Z = dt_out("Z", [2 * G, NB * TB])
    o_hy = dt_out("hyc", [3 * G, NB * TB])
    outs_b = []

    sb = lambda name, shape, dt=F32: nc.alloc_sbuf_tensor(name, shape, dt)
    ident = sb("ident", [128, 128], BF16); b_ident = Buf()
    ones_f = sb("ones_f", [128, 128], F32); b_ones_f = Buf()
    ones_b = sb("ones_b", [128, 128], BF16); b_ones_b = Buf()
    S.op("pool", lambda e: e.memset(ident[:], 1.0), writes=[b_ident])
    S.op("pool", lambda e: e.affine_select(ident[:], ident[:], pattern=[[-1, 128]], compare_op=ALU.is_equal,
                                           fill=0.0, base=0, channel_multiplier=1), reads=[b_ident], writes=[b_ident])
    S.op("pool", lambda e: e.memset(ones_f[:], 1.0), writes=[b_ones_f])
    S.op("pool", lambda e: e.memset(ones_b[:], 1.0), writes=[b_ones_b])
    g_bc = sb("g_bc", [128, D]); b_gbc = Buf()
    cw_s = sb("cw_s", [128, 8, 31]); cvec_s = sb("cvec_s", [128, 5, 8]); hyw_s = sb("hyw_s", [128, 24, 4])
    gD_s = sb("gD_s", [128, 8])
    b_par = Buf()
    S.dma("sp", cw_s[:], cw, writes=[b_par])
    S.dma("sp", cvec_s[:], cvec, writes=[b_par])
    S.dma("sp", hyw_s[:], hyw, writes=[b_par])
    S.dma("sp", gD_s[:], gD, writes=[b_par])

    psum = Pool_(nc, "ps", 8, [128, 512], F32, psum=True)
    wbf = Pool_(nc, "wbf", 3, [128, 16, 256], BF16)
    wkeys = {}
    xs_p = Pool_(nc, "xs", 2, [128, D], F32)
    hb_p = Pool_(nc, "hb", 2, [128, D], BF16)
    st_p = Pool_(nc, "st", 4, [128, 4], F32)
    hT = sb("hT", [128, 16, TBH], BF16); b_hT = Buf()

    pending = []
    tick = [0]

    def flush(delay):
        while pending and pending[0][0] + delay <= tick[0]:
            _, dst, ob, bob = pending.pop(0)
            b = Buf()
            outs_b.append(b)
            S.dma("sp", dst, ob[:], reads=[bob], writes=[b])

    def out_dma(dst, ob, bob):
        pending.append((tick[0], dst, ob, bob))

    def stream_w(dram, c0, nk):
        tick[0] += 1
        flush(2)
        key = (dram.tensor.name, c0 // 256)
        if key in wkeys:
            wb, bwb = wbf.t[wkeys[key]]
        else:
            slot = wbf.i
            for k_ in [k_ for k_, v_ in wkeys.items() if v_ == slot]:
                del wkeys[k_]
            wb, bwb = wbf.get()
            wkeys[key] = slot
            cbase = (c0 // 256) * 256
            S.dma("pool", wb[:, 0:nk, :], dram.rearrange("(k p) c -> p k c", p=128)[:, :, cbase:cbase + 256], writes=[bwb])
        off = c0 % 256
        return wb[:, :, off:off + 128], bwb

    def rms_rows(xs, bxs, rows, gtile, bg, out_bf, bout):
        stt, bstt = st_p.get()
        S.op("dve", lambda e: e.memset(stt[:], 0.0), writes=[bstt])
        S.op("act", lambda e: e.activation(out_bf[:rows], xs[:rows], AF.Square, accum_out=stt[:rows, 0:1]),
             reads=[bxs, bstt], writes=[bout, bstt])
        S.op("dve", lambda e: e.tensor_scalar(stt[:rows, 1:2], stt[:rows, 0:1], 1.0 / D, EPS, ALU.mult, ALU.add),
             reads=[bstt], writes=[bstt])
        S.op("dve", lambda e: e.reciprocal(stt[:rows, 2:3], stt[:rows, 1:2]), reads=[bstt], writes=[bstt])
        S.op("act", lambda e: e.activation(stt[:rows, 2:3], stt[:rows, 2:3], AF.Sqrt), reads=[bstt], writes=[bstt])
        S.op("dve", lambda e: e.scalar_tensor_tensor(out_bf[:rows], xs[:rows], stt[:rows, 2:3], gtile[:rows],
                                                     ALU.mult, ALU.mult), reads=[bxs, bstt, bg], writes=[bout])

    def transpose_into(src_bf, bsrc, rows, dstT, bdst, col0):
        for half in range(2):
            pt, bpt = psum.get()
            ptb = pt[:].bitcast(BF16)
            for kk in range(8):
                k = half * 8 + kk
                S.op("pe", lambda e, k=k, kk=kk: e.transpose(ptb[:, kk * 128:kk * 128 + rows],
                                                             src_bf[:rows, k * 128:(k + 1) * 128], ident[:rows, :rows]),
                     reads=[bsrc, b_ident], writes=[bpt])
            S.op("act", lambda e: e.copy(dstT[:, half * 8:half * 8 + 8, col0:col0 + rows],
                                         ptb.rearrange("p (k t) -> p k t", k=8)[:, :, 0:rows]),
                 reads=[bpt], writes=[bdst])

    cvb = sb("cvb", [128, 8, TB], BF16); b_cv = [Buf() for _ in range(8)]
    memT = cvb[:, 0:4, :].rearrange("p a (b m) -> p (a b) m", m=256); b_memT = Buf()
    mg_bc, b_mg = g_bc, b_gbc
    S.dma("sp", mg_bc[:], mem_g_bc, writes=[b_mg])
    for i in range(2):
        xs, bxs = xs_p.get()
        S.dma("sp", xs[:], memx[i * 128:(i + 1) * 128, :], writes=[bxs])
        hb, bhb = hb_p.get()
        rms_rows(xs, bxs, 128, mg_bc, b_mg, hb, bhb)
        transpose_into(hb, bhb, 128, memT, b_memT, i * 128)
    kT = sb("kT", [128, 8, 256], BF16); b_kT = Buf()
    vS = sb("vS", [128, 2, G], BF16); b_vS = Buf()
    for j in range(8):
        wb, bwb = stream_w(wk, j * 128, 16)
        pt, bpt = psum.get()
        for k in range(16):
            S.op("pe", lambda e, k=k: e.matmul(pt[:, 0:256], lhsT=wb[:, k, :], rhs=memT[:, k, :], start=(k == 0), stop=(k == 15)),
                 reads=[bwb, b_memT], writes=[bpt])
        S.op("act", lambda e: e.copy(kT[:, j, :], pt[:, 0:256]), reads=[bpt], writes=[b_kT])
    for j in range(8):
        wb, bwb = stream_w(wv, j * 128, 16)
        pt, bpt = psum.get()
        for m in range(2):
            for k in range(16):
                S.op("pe", lambda e, k=k, m=m: e.matmul(pt[:, m * 128:(m + 1) * 128], lhsT=memT[:, k, m * 128:(m + 1) * 128],
                                                        rhs=wb[:, k, :], start=(k == 0), stop=(k == 15)),
                     reads=[bwb, b_memT], writes=[bpt])
        S.op("act", lambda e: e.copy(vS[:, :, j * 128:(j + 1) * 128], pt[:, 0:256].rearrange("p (m c) -> p m c", m=2)),
             reads=[bpt], writes=[b_vS])

    S.dma("sp", g_bc[:], pre_g_bc, reads=[], writes=[b_gbc])
    yAb = sb("yAb", [128, 8, TB], BF16); b_yA = [Buf() for _ in range(8)]
    fT = sb("fT", [128, 8, TB], BF16); b_fT = Buf()
    s1 = sb("s1", [128, TB]); b_s1 = Buf()
    s2 = sb("s2", [128, TB]); b_s2 = Buf()
    u_p = Pool_(nc, "u", 1, [128, TBH], F32)
    sig_p = Pool_(nc, "sig", 1, [128, TBH], F32)
    acc_p = Pool_(nc, "acc", 2, [128, TB], F32)
    sq_p = Pool_(nc, "sq", 2, [128, TB], F32)
    ob_p = Pool_(nc, "ob", 4, [128, TB], BF16)
    eT_p = Pool_(nc, "eT", 2, [128, 2, 512], BF16)
    hytmp_p = Pool_(nc, "hytmp", 1, [128, TB], F32)
    ub_p = Pool_(nc, "ub", 2, [128, TBH], BF16)
    dg_p = Pool_(nc, "dg", 6, [128, 128], BF16)
    rden_p = Pool_(nc, "rden", 2, [128, 512], F32)

    def inproj(col0, halo):
        wb, bwb = stream_w(w_in, col0, 16)
        res = []
        if halo:
            chunks = [(i * 352, 352) for i in range(3)]
        else:
            chunks = [(H + i * 512, 512) for i in range(2)]
        for (t0, n) in chunks:
            pt, bpt = psum.get()
            for k in range(16):
                S.op("pe", lambda e, k=k, t0=t0, n=n, pt=pt: e.matmul(pt[:, 0:n], lhsT=wb[:, k, :], rhs=hT[:, k, t0:t0 + n],
                                                                       start=(k == 0), stop=(k == 15)),
                     reads=[bwb, b_hT], writes=[bpt])
            res.append((pt, bpt, t0, n))
        return res

    def colsum_acc(src, bsrc, acc, bacc, first):
        for hh in range(2):
            pt, bpt = psum.get()
            S.op("pe", lambda e: e.matmul(pt[:], lhsT=ones_f[:], rhs=src[:, hh * 512:(hh + 1) * 512], start=True, stop=True),
                 reads=[b_ones_f, bsrc], writes=[bpt])
            if first:
                S.op("dve", lambda e: e.tensor_copy(acc[:, hh * 512:(hh + 1) * 512], pt[:]), reads=[bpt], writes=[bacc])
            else:
                S.op("dve", lambda e: e.tensor_tensor(acc[:, hh * 512:(hh + 1) * 512], acc[:, hh * 512:(hh + 1) * 512], pt[:], ALU.add),
                     reads=[bpt, bacc], writes=[bacc])

    def rstd_from(acc, bacc, n):
        S.op("dve", lambda e: e.tensor_scalar(acc[:], acc[:], 1.0 / n, EPS, ALU.mult, ALU.add), reads=[bacc], writes=[bacc])
        S.op("dve", lambda e: e.reciprocal(acc[:], acc[:]), reads=[bacc], writes=[bacc])
        S.op("act", lambda e: e.activation(acc[:], acc[:], AF.Sqrt), reads=[bacc], writes=[bacc])

    def gated_out(ysrc, bys, j, gcol, rstd, brstd, gate_col0, odram, blk):
        res = inproj(gate_col0 + j * 128, False)
        ob, bob = ob_p.get()
        sg, bsg = sq_p.get()
        for (pt, bpt, t0, n) in res:
            o = t0 - H
            S.op("act", lambda e, pt=pt, o=o: e.activation(sg[:, o:o + 512], pt[:], AF.Silu), reads=[bpt], writes=[bsg])
        tmp, btmp = acc_p.get()
        S.op("dve", lambda e: e.scalar_tensor_tensor(tmp[:], ysrc, gcol, rstd[:], ALU.mult, ALU.mult),
             reads=[bys, brstd, b_par], writes=[btmp])
        S.op("dve", lambda e: e.tensor_tensor(ob[:], tmp[:], sg[:], ALU.mult), reads=[btmp, bsg], writes=[bob])
        out_dma(odram[j * 128:(j + 1) * 128, blk * TB:(blk + 1) * TB], ob, bob)

    def run_all(gens):
        gens = list(gens)
        while gens:
            for g in list(gens):
                try:
                    next(g)
                except StopIteration:
                    gens.remove(g)

    for blk in range(NB):
        for i in range(9):
            rows = 128 if i < 8 else TBH - 8 * 128
            xs, bxs = xs_p.get()
            S.dma("sp", xs[:rows], xp[blk, i * 128:i * 128 + rows, :], writes=[bxs])
            hb, bhb = hb_p.get()
            rms_rows(xs, bxs, rows, g_bc, b_gbc, hb, bhb)
            transpose_into(hb, bhb, rows, hT, b_hT, i * 128)

        def g_hy():
            for j in range(24):
                res = inproj(C_HY + j * 128, True)
                ob, bob = ob_p.get()
                tmp, btmp = hytmp_p.get()
                u, bu = u_p.get()
                for (pt, bpt, t0, n) in res:
                    S.op("act", lambda e, pt=pt, t0=t0, n=n: e.copy(u[:, t0:t0 + n], pt[:, 0:n]), reads=[bpt], writes=[bu])
                S.op("dve", lambda e: e.tensor_scalar(tmp[:], u[:, H - 1:H - 1 + TB], hyw_s[:, j, 0:1], hyw_s[:, j, 3:4], ALU.mult, ALU.add),
                     reads=[bu, b_par], writes=[btmp])
                S.op("dve", lambda e: e.scalar_tensor_tensor(tmp[:], u[:, H:H + TB], hyw_s[:, j, 1:2], tmp[:], ALU.mult, ALU.add),
                     reads=[bu, b_par, btmp], writes=[btmp])
                S.op("dve", lambda e: e.scalar_tensor_tensor(ob[:], u[:, H + 1:H + 1 + TB], hyw_s[:, j, 2:3], tmp[:], ALU.mult, ALU.add),
                     reads=[bu, b_par, btmp], writes=[bob])
                out_dma(o_hy[j * 128:(j + 1) * 128, blk * TB:(blk + 1) * TB], ob, bob)
                if j % 2 == 1:
                    yield

        def g_gate():
            for gi, odram in ((1, o_sgB), (2, o_sgC)):
                for j in range(8):
                    res = inproj(C_GATE + gi * G + j * 128, False)
                    ob, bob = ob_p.get()
                    for (pt, bpt, t0, n) in res:
                        o = t0 - H
                        S.op("act", lambda e, pt=pt, o=o: e.activation(ob[:, o:o + 512], pt[:], AF.Silu), reads=[bpt], writes=[bob])
                    out_dma(odram[j * 128:(j + 1) * 128, blk * TB:(blk + 1) * TB], ob, bob)
                    if j % 2 == 1:
                        yield

        def g_fz():
            for j in range(8):
                res = inproj(C_F + j * 128, False)
                for (pt, bpt, t0, n) in res:
                    o = t0 - H
                    S.op("act", lambda e, pt=pt, o=o: e.copy(fT[:, j, o:o + 512], pt[:]), reads=[bpt], writes=[b_fT])
                if j % 2 == 1:
                    yield
            for j in range(16):
                wb, bwb = stream_w(dftG, j * 128, 8)
                ob, bob = ob_p.get()
                for hh in range(2):
                    pt, bpt = psum.get()
                    for k in range(8):
                        S.op("pe", lambda e, k=k: e.matmul(pt[:], lhsT=wb[:, k, :], rhs=fT[:, k, hh * 512:(hh + 1) * 512],
                                                           start=(k == 0), stop=(k == 7)), reads=[bwb, b_fT], writes=[bpt])
                    S.op("act", lambda e: e.copy(ob[:, hh * 512:(hh + 1) * 512], pt[:]), reads=[bpt], writes=[bob])
                out_dma(o_Z[j * 128:(j + 1) * 128, blk * TB:(blk + 1) * TB], ob, bob)
                if j % 2 == 1:
                    yield

            for j in range(8):
                res = inproj(C_Q + j * 128, False)
                for (pt, bpt, t0, n) in res:
                    o = t0 - H
                    S.op("act", lambda e, pt=pt, o=o: e.copy(fT[:, j, o:o + 512], pt[:]), reads=[bpt], writes=[b_fT])
                if j % 2 == 1:
                    yield
        def g_A():
            pend = None
            for i in range(8):
                resg = inproj(C_AGATE + i * 128, True)
                sig, bsig = sig_p.get()
                for (pt, bpt, t0, n) in resg:
                    S.op("act", lambda e, pt=pt, t0=t0, n=n: e.activation(sig[:, t0:t0 + n], pt[:, 0:n], AF.Sigmoid), reads=[bpt], writes=[bsig])
                resv = inproj(C_AVAL + i * 128, True)
                ub, bub = ub_p.get()
                for (pt, bpt, t0, n) in resv:
                    S.op("dve", lambda e, pt=pt, t0=t0, n=n: e.tensor_tensor(ub[:, t0:t0 + n], pt[:, 0:n], sig[:, t0:t0 + n], ALU.mult),
                         reads=[bpt, bsig], writes=[bub])
                cps = [psum.get() for _ in range(2)]
                for tap in range(31):
                    dg, bdg = dg_p.get()
                    S.op("dve", lambda e, tap=tap: e.tensor_scalar(dg[:], ident[:], cw_s[:, i, tap:tap + 1], None, ALU.mult),
                         reads=[b_ident, b_par], writes=[bdg])
                    for hh in range(2):
                        S.op("pe", lambda e, tap=tap, hh=hh: e.matmul(cps[hh][0][:], lhsT=dg[:], rhs=ub[:, H - 15 + tap + hh * 512:H - 15 + tap + hh * 512 + 512],
                                                                      start=(tap == 0), stop=(tap == 30)), reads=[bdg, bub], writes=[cps[hh][1]])
                acc, bacc = acc_p.get()
                for hh in range(2):
                    S.op("act", lambda e, hh=hh: e.activation(acc[:, hh * 512:(hh + 1) * 512], cps[hh][0][:], AF.Identity, bias=cvec_s[:, 0, i:i + 1]),
                         reads=[cps[hh][1], b_par], writes=[bacc])
                sq, bsq = sq_p.get()
                S.op("act", lambda e: e.activation(sq[:], acc[:], AF.Square), reads=[bacc], writes=[bsq])
                S.op("act", lambda e: e.copy(cvb[:, i, :], acc[:]), reads=[bacc], writes=[b_cv[i], b_memT])
                if pend is not None:
                    colsum_acc(pend[0], pend[1], s1, b_s1, pend[4] == 0)
                    colsum_acc(pend[2], pend[3], s2, b_s2, pend[4] == 0)
                pend = (acc, bacc, sq, bsq, i)
                if i % 2 == 1:
                    yield
            colsum_acc(pend[0], pend[1], s1, b_s1, False)
            colsum_acc(pend[2], pend[3], s2, b_s2, False)
            S.op("dve", lambda e: e.tensor_scalar(s1[:], s1[:], 1.0 / G, None, ALU.mult), reads=[b_s1], writes=[b_s1])
            sq, bsq = sq_p.get()
            S.op("dve", lambda e: e.tensor_tensor(sq[:], s1[:], s1[:], ALU.mult), reads=[b_s1], writes=[bsq])
            S.op("dve", lambda e: e.scalar_tensor_tensor(s2[:], s2[:], 1.0 / G, sq[:], ALU.mult, ALU.subtract), reads=[b_s2, bsq], writes=[b_s2])
            S.op("dve", lambda e: e.tensor_scalar(s2[:], s2[:], EPS, None, ALU.add), reads=[b_s2], writes=[b_s2])
            S.op("dve", lambda e: e.reciprocal(s2[:], s2[:]), reads=[b_s2], writes=[b_s2])
            S.op("act", lambda e: e.activation(s2[:], s2[:], AF.Sqrt), reads=[b_s2], writes=[b_s2])
            for i in range(8):
                tmp, btmp = acc_p.get()
                S.op("dve", lambda e: e.tensor_tensor(tmp[:], cvb[:, i, :], s1[:], ALU.subtract), reads=[b_cv[i], b_s1], writes=[btmp])
                S.op("dve", lambda e: e.tensor_tensor(tmp[:], tmp[:], s2[:], ALU.mult), reads=[btmp, b_s2], writes=[btmp])
                S.op("act", lambda e: e.activation(cvb[:, i, :], tmp[:], AF.Silu, scale=cvec_s[:, 1, i:i + 1], bias=cvec_s[:, 2, i:i + 1]),
                     reads=[btmp, b_par], writes=[b_cv[i]])
            for j in range(8):
                wb, bwb = stream_w(pw, j * 128, 8)
                ya, bya = acc_p.get()
                for hh in range(2):
                    pt, bpt = psum.get()
                    for k in range(8):
                        S.op("pe", lambda e, k=k: e.matmul(pt[:], lhsT=wb[:, k, :], rhs=cvb[:, k, hh * 512:(hh + 1) * 512], start=(k == 0), stop=(k == 7)),
                             reads=[bwb, b_cv[k]], writes=[bpt])
                    S.op("act", lambda e: e.activation(ya[:, hh * 512:(hh + 1) * 512], pt[:], AF.Identity, bias=cvec_s[:, 3, j:j + 1]),
                         reads=[bpt, b_par], writes=[bya])
                sq, bsq = sq_p.get()
                S.op("act", lambda e: e.activation(sq[:], ya[:], AF.Square), reads=[bya], writes=[bsq])
                S.op("dve", lambda e: e.tensor_copy(yAb[:, j, :], ya[:]), reads=[bya], writes=[b_yA[j]])
                colsum_acc(sq, bsq, s1, b_s1, j == 0)
                if j % 2 == 1:
                    yield
            rstd_from(s1, b_s1, G)
            for j in range(8):
                gated_out(yAb[:, j, :], b_yA[j], j, cvec_s[:, 4, j:j + 1], s1, b_s1, C_GATE + 0 * G, o_yA, blk)
                if j % 2 == 1:
                    yield

        run_all([g_hy(), g_A(), g_gate(), g_fz()])
        first = True
        for hd in range(4):
            for hh in range(2):
                eT, beT = eT_p.get()
                for m in range(2):
                    pt, bpt = psum.get()
                    for dc in range(2):
                        S.op("pe", lambda e, dc=dc, m=m: e.matmul(pt[:], lhsT=kT[:, hd * 2 + dc, m * 128:(m + 1) * 128],
                                                                  rhs=fT[:, hd * 2 + dc, hh * 512:(hh + 1) * 512], start=(dc == 0), stop=(dc == 1)),
                             reads=[b_kT, b_fT], writes=[bpt])
                    S.op("act", lambda e, m=m: e.activation(eT[:, m, :], pt[:], AF.Exp, scale=1.0 / 16.0), reads=[bpt], writes=[beT])
                pd, bpd = psum.get()
                for m in range(2):
                    S.op("pe", lambda e, m=m: e.matmul(pd[:], lhsT=ones_b[:], rhs=eT[:, m, :], start=(m == 0), stop=(m == 1)),
                         reads=[b_ones_b, beT], writes=[bpd])
                rd, brd = rden_p.get()
                S.op("dve", lambda e: e.reciprocal(rd[:], pd[:]), reads=[bpd], writes=[brd])
                for cc in range(2):
                    j = hd * 2 + cc
                    pt, bpt = psum.get()
                    for m in range(2):
                        S.op("pe", lambda e, m=m: e.matmul(pt[:], lhsT=vS[:, m, j * 128:(j + 1) * 128], rhs=eT[:, m, :], start=(m == 0), stop=(m == 1)),
                             reads=[b_vS, beT], writes=[bpt])
                    S.op("dve", lambda e, j=j: e.tensor_tensor(yAb[:, j, hh * 512:(hh + 1) * 512], pt[:], rd[:], ALU.mult),
                         reads=[bpt, brd], writes=[b_yA[j]])
        for j in range(8):
            sq, bsq = sq_p.get()
            S.op("act", lambda e: e.activation(sq[:], yAb[:, j, :], AF.Square), reads=[b_yA[j]], writes=[bsq])
            colsum_acc(sq, bsq, s1, b_s1, j == 0)
        rstd_from(s1, b_s1, G)
        for j in range(8):
            gated_out(yAb[:, j, :], b_yA[j], j, gD_s[:, j:j + 1], s1, b_s1, C_GATE + 3 * G, o_yM, blk)

    flush(0)
    S.finish(outs_b, "sp")
    return S
import math

L = 8192
NCH = 128
CB = 8
NBATCH = NCH // CB
N16 = 16384
import os
SES = bool(int(os.environ.get("SES", "1")))

CO = {}
_o = 0
for _n, _w in (("FA", 256), ("FAhi", 256), ("C", 128), ("S", 128), ("nS", 128), ("IA1", 256), ("IA2", 256),
               ("IBc", 64), ("IBs", 64), ("FN1", 128), ("FN2", 128), ("FAi", 256), ("IBsp", 64)):
    CO[_n] = (_o, _w)
    _o += _w
CBF_W = _o
CF = {"T16r": (0, 128), "T16i": (128, 128), "T16ci": (256, 128), "T8r": (384, 64), "T8i": (448, 64)}
CF_W = 512


def build_L2(nc):
    S = Sched(nc, same_engine_sync=SES)
    dt_in = lambda name, shape, dt=F32: nc.dram_tensor(name, shape, dt, kind="ExternalInput").ap()
    dt_out = lambda name, shape, dt=BF16: nc.dram_tensor(name, shape, dt, kind="ExternalOutput").ap()
    zr_d = dt_in("zr", [2 * NCH, L], BF16); zi_d = dt_in("zi", [2 * NCH, L], BF16)
    hv_d = dt_in("hv", [2 * NCH, L], BF16); hx1_d = dt_in("hx1", [2 * NCH, L], BF16); hx2_d = dt_in("hx2", [2 * NCH, L], BF16)
    pos_rep = dt_in("pos_rep", [2, 32, L], I32)
    pos_t = dt_in("pos_t", [2, 64, 128], I32)
    bsc = dt_in("bsc", [32, 2])
    fw1a = dt_in("fw1a", [1, 64]); fw1b = dt_in("fw1b", [32, 64])
    mvec = dt_in("mvec", [64, 4])
    fw2 = dt_in("fw2", [64, 64])
    fw3c = dt_in("fw3c", [64, 2, NBATCH, 2, CB])
    dec_rep = dt_in("dec_rep", [64, 2, NBATCH, 2, CB])
    skip_rep = dt_in("skip_rep", [64, 2, NCH])
    cbf_d = dt_in("cbf", [128, CBF_W], BF16)
    cf_d = dt_in("cf", [128, CF_W])
    o_ff = dt_out("ff", [2 * NCH, L])
    o_yc = dt_out("yc", [2 * NCH, L])
    outs_b = []

    sb = lambda name, shape, dt=F32: nc.alloc_sbuf_tensor(name, shape, dt)
    cbf = sb("cbf_s", [128, CBF_W], BF16); cf = sb("cf_s", [128, CF_W]); b_c = Buf()
    S.dma("sp", cbf[:], cbf_d, writes=[b_c])
    S.dma("sp", cf[:], cf_d, writes=[b_c])
    cm = lambda n, rows=128: cbf[0:rows, CO[n][0]:CO[n][0] + CO[n][1]]
    cfm = lambda n: cf[:, CF[n][0]:CF[n][0] + CF[n][1]]
    ones_f = sb("ones_f", [64, 64]); b_ones = Buf()
    S.op("pool", lambda e: e.memset(ones_f[:], 1.0), writes=[b_ones])

    psum = Pool_(nc, "ps", 8, [128, 512], F32, psum=True)

    h2 = [sb("h2f", [64, L], BF16), sb("h2b", [64, L], BF16)]; b_h2 = [Buf(), Buf()]
    fw3s = sb("fw3s", [64, 2, NBATCH, 2 * CB], BF16); b_fw3 = Buf()
    dec = sb("dec", [64, 2, NBATCH, 2 * CB]); b_dec = Buf()
    skp = sb("skp", [64, 2, NCH]); b_skp = Buf()
    tpos = sb("tpos", [64, 2, 128]); b_tpos = Buf()
    S.dma("sp", skp[:], skip_rep, writes=[b_skp])
    S.dma("sp", dec[:], dec_rep.rearrange("p d b o c -> p d b (o c)"), writes=[b_dec])
    par = sb("mlp_par", [64, 4 + 64 + 64 + 64 + 2]); b_par = Buf()
    fq = sb("fq", [64, 2]); b_fq = Buf()
    f3st = sb("f3st", [64, 2, NBATCH, 2 * CB]); b_f3st = Buf()
    tpi = sb("tpi", [64, 2, 128], I32); b_tpi = Buf()
    b_ft = Buf()
    MW = 2048

    with nc.sbuf_tensor("mlp_tmp", [64, 16384], F32) as mt, nc.sbuf_tensor("fr_i", [64, MW], I32) as fr_i, \
            nc.sbuf_tensor("fr_f", [64, MW], F32) as fr_f:
        b_mt = Buf(); b_fr = Buf()

        def frac_neg(a, rows, bufs):
            S.op("dve", lambda e: e.tensor_copy(fr_i[0:rows, :], a), reads=bufs, writes=[b_fr])
            S.op("dve", lambda e: e.tensor_copy(fr_f[0:rows, :], fr_i[0:rows, :]), reads=[b_fr], writes=[b_fr])
            S.op("dve", lambda e: e.tensor_tensor(a, a, fr_f[0:rows, :], ALU.subtract), reads=bufs + [b_fr], writes=bufs)
            S.op("dve", lambda e: e.scalar_tensor_tensor(a, a, 0.5, a, ALU.is_gt, ALU.subtract), reads=bufs, writes=bufs)

        posi = mt[0:32, 0:8192].bitcast(I32)
        posi_t = mt[32:33, 0:8192].bitcast(I32)
        posf = mt[0:32, 8192:16384]
        feat_t = mt[32:33, 0:8192]
        S.dma("sp", par[:, 0:4], mvec, writes=[b_par])
        S.dma("sp", par[:, 4:68], fw2, writes=[b_par])
        S.dma("sp", par[0:32, 68:132], fw1b, writes=[b_par])
        S.dma("sp", par[32:33, 132:196], fw1a, writes=[b_par])
        S.dma("sp", par[0:32, 196:198], bsc, writes=[b_par])
        S.op("dve", lambda e: e.tensor_scalar(fq[:, 0:1], par[:, 1:2], 1.0 / (2 * math.pi), None, ALU.mult), reads=[b_par], writes=[b_fq])
        S.op("dve", lambda e: e.tensor_scalar(fq[:, 1:2], par[:, 3:4], 1.0 / (2 * math.pi), None, ALU.mult), reads=[b_par], writes=[b_fq])
        S.dma("sp", f3st[:], fw3c.rearrange("j d b o c -> j d b (o c)"), writes=[b_f3st])
        S.op("pool", lambda e: e.tensor_copy(fw3s[:], f3st[:]), reads=[b_f3st], writes=[b_fw3])
        S.dma("sp", tpi[:], pos_t.rearrange("d p s -> p d s"), writes=[b_tpi])
        S.op("dve", lambda e: e.tensor_copy(tpos[:], tpi[:]), reads=[b_tpi], writes=[b_tpos])
        S.op("dve", lambda e: e.tensor_scalar(tpos[:], tpos[:], 1.0 / L, None, ALU.mult), reads=[b_tpos], writes=[b_tpos])
        h1 = mt[0:64, 8192:16384]
        for d in range(2):
            S.dma("sp", posi, pos_rep[d], writes=[b_mt])
            S.dma("sp", posi_t, pos_rep[d, 0:1, :], writes=[b_mt])
            S.op("dve", lambda e: e.tensor_copy(posf, posi), reads=[b_mt], writes=[b_mt])
            S.op("dve", lambda e: e.tensor_copy(mt[32:33, 8192:16384], posi_t), reads=[b_mt], writes=[b_mt])
            S.op("dve", lambda e: e.tensor_scalar(feat_t, mt[32:33, 8192:16384], 1.0 / L, None, ALU.mult), reads=[b_mt], writes=[b_ft])
            S.op("dve", lambda e: e.tensor_scalar(posf, posf, par[0:32, 196:197], par[0:32, 197:198], ALU.mult, ALU.add), reads=[b_mt, b_par], writes=[b_mt])
            feats = mt[0:32, 0:8192]
            for q in range(L // MW):
                ws = slice(q * MW, (q + 1) * MW)
                frac_neg(posf[:, ws], 32, [b_mt])
                S.op("act", lambda e: e.activation(feats[:, ws], posf[:, ws], AF.Sin, scale=-2 * math.pi), reads=[b_mt], writes=[b_mt])
            for q in range(L // MW):
                hs = h1[:, q * MW:(q + 1) * MW]
                for ch in range(MW // 512):
                    sl = slice(q * MW + ch * 512, q * MW + (ch + 1) * 512)
                    pt, bpt = psum.get()
                    S.op("pe", lambda e: e.matmul(pt[0:64, :], lhsT=par[32:33, 132:196], rhs=feat_t[:, sl], start=True, stop=False),
                         reads=[b_par, b_ft], writes=[bpt])
                    S.op("pe", lambda e: e.matmul(pt[0:64, :], lhsT=par[0:32, 68:132], rhs=feats[:, sl], start=False, stop=True),
                         reads=[b_par, b_mt], writes=[bpt])
                    S.op("dve", lambda e: e.tensor_scalar(h1[:, sl], pt[0:64, :], par[:, 0:1], fq[:, 0:1], ALU.add, ALU.mult), reads=[bpt, b_par, b_fq], writes=[b_mt])
                S.op("dve", lambda e: e.tensor_scalar(hs, hs, 8.0, None, ALU.add), reads=[b_mt], writes=[b_mt])
                frac_neg(hs, 64, [b_mt])
                S.op("act", lambda e: e.activation(hs, hs, AF.Sin, scale=-2 * math.pi), reads=[b_mt], writes=[b_mt])
                pts = []
                for ch in range(MW // 512):
                    sl = slice(q * MW + ch * 512, q * MW + (ch + 1) * 512)
                    pt2, bpt2 = psum.get()
                    S.op("pe", lambda e: e.matmul(pt2[0:64, :], lhsT=par[:, 4:68], rhs=h1[:, sl], start=True, stop=True), reads=[b_par, b_mt], writes=[bpt2])
                    pts.append((pt2, bpt2, sl))
                for (pt2, bpt2, sl) in pts:
                    S.op("dve", lambda e: e.tensor_scalar(h1[:, sl], pt2[0:64, :], par[:, 2:3], fq[:, 1:2], ALU.add, ALU.mult), reads=[bpt2, b_par, b_fq], writes=[b_mt])
                S.op("dve", lambda e: e.tensor_scalar(hs, hs, 8.0, None, ALU.add), reads=[b_mt], writes=[b_mt])
                frac_neg(hs, 64, [b_mt])
                S.op("act", lambda e: e.activation(h2[d][:, q * MW:(q + 1) * MW], hs, AF.Sin, scale=-2 * math.pi), reads=[b_mt], writes=[b_h2[d]])
        b_mt_final = Buf()
        b_mt_final.w = b_mt.w
        b_mt_final.r = dict(b_mt.r)
        for k_, v_ in list(b_fr.r.items()) + ([b_fr.w] if b_fr.w else []):
            b_mt_final.r[k_] = max(b_mt_final.r.get(k_, 0), v_)

    S.op("dve", lambda e: e.tensor_scalar(f3st[:], dec[:], -1.0, None, ALU.mult), reads=[b_dec], writes=[b_f3st])
    S.op("dve", lambda e: e.tensor_tensor(dec[:], dec[:], f3st[:], ALU.max), reads=[b_dec, b_f3st], writes=[b_dec])
    Kf = [sb("Kf0", [128, 2, 2, CB, 128]), sb("Kf1", [128, 2, 2, CB, 128])]
    b_Kf = [[Buf(), Buf()], [Buf(), Buf()]]
    stag_p = Pool_(nc, "stag", 3, [128, CB, 256], F32)
    tmp_p = {e: Pool_(nc, "tmp" + e, n_, [128, CB * 128], F32) for e, n_ in (("dve", 3), ("pool", 2))}
    cb_p = Pool_(nc, "cb", 8, [128, CB, 128], BF16)
    xin_p = Pool_(nc, "xin", 12, [64, CB, 128], BF16)
    kt = [sb("ktf", [64, 2 * CB, 128]), sb("ktb", [64, 2 * CB, 128])]; b_kt = [Buf(), Buf()]
    ktb16 = [sb("ktf16", [64, 2 * CB, 128], BF16), sb("ktb16", [64, 2 * CB, 128], BF16)]; b_ktb = [Buf(), Buf()]
    win = sb("win", [64, 2 * CB, 128]); b_win = Buf()
    nrm = sb("nrm", [64, 4, 2 * CB]); b_nrm = Buf()
    yst = [sb("yst_r", [64, CB, 128]), sb("yst_i", [64, CB, 128])]; b_yst = [Buf(), Buf()]
    fo_p = Pool_(nc, "fo", 2, [128, CB, 64], BF16)
    for b in b_Kf[0] + b_Kf[1] + [b_kt[0], b_kt[1], b_ktb[0], b_ktb[1], b_win, b_nrm] + b_yst + \
            [t[1] for t in stag_p.t + tmp_p["dve"].t + tmp_p["pool"].t + cb_p.t + xin_p.t + fo_p.t]:
        b.w = b_mt_final.w
        b.r = dict(b_mt_final.r)

    def out_dma(dst, src, bsrc):
        b = Buf(); outs_b.append(b)
        S.dma("sp", dst, src, reads=[bsrc], writes=[b])

    def stage_data_stationary(chan_mms, nch, width, rows=128):
        st, bst = stag_p.get()
        per = 512 // width
        for c0 in range(0, nch, per):
            pt, bpt = psum.get()
            n = min(per, nch - c0)
            for i in range(n):
                mms = chan_mms(c0 + i)
                for q, (lhsT, rhs, rd) in enumerate(mms):
                    S.op("pe", lambda e, lhsT=lhsT, rhs=rhs, i=i, q=q: e.matmul(pt[0:rows, i * width:(i + 1) * width], lhsT=lhsT, rhs=rhs,
                                                                                 start=(q == 0), stop=(q == len(mms) - 1)),
                         reads=rd + [b_c], writes=[bpt])
            S.op("act", lambda e: e.copy(st[0:rows, c0:c0 + n, 0:width], pt[0:rows, 0:n * width].rearrange("p (c w) -> p c w", c=n)),
                 reads=[bpt], writes=[bst])
        return st, bst

    def cplx_mul(sr, si, tr, ti, rd, n1, nch, rows=128):
        dr, bdr = cb_p.get(); di, bdi = cb_p.get()
        drv = dr[0:rows, 0:nch, 0:n1]; div = di[0:rows, 0:nch, 0:n1]
        ta, bta = tmp_p["dve"].get(); tb, btb = tmp_p["dve"].get()
        tav = ta[0:rows, 0:nch * n1].rearrange("p (c k) -> p c k", c=nch); tbv = tb[0:rows, 0:nch * n1].rearrange("p (c k) -> p c k", c=nch)
        S.op("dve", lambda e: e.tensor_tensor(tav, sr, tr, ALU.mult), reads=rd, writes=[bta])
        S.op("dve", lambda e: e.tensor_tensor(tbv, si, ti, ALU.mult), reads=rd, writes=[btb])
        S.op("dve", lambda e: e.tensor_tensor(drv, tav, tbv, ALU.subtract), reads=[bta, btb], writes=[bdr])
        tc, btc = tmp_p["pool"].get(); td, btd = tmp_p["pool"].get()
        tcv = tc[0:rows, 0:nch * n1].rearrange("p (c k) -> p c k", c=nch); tdv = td[0:rows, 0:nch * n1].rearrange("p (c k) -> p c k", c=nch)
        S.op("pool", lambda e: e.tensor_tensor(tcv, sr, ti, ALU.mult), reads=rd, writes=[btc])
        S.op("pool", lambda e: e.tensor_tensor(tdv, si, tr, ALU.mult), reads=rd, writes=[btd])
        S.op("pool", lambda e: e.tensor_tensor(div, tcv, tdv, ALU.add), reads=[btc, btd], writes=[bdi])
        return (dr, bdr), (di, bdi)

    def bc(tab, nch, n1):
        return tab.unsqueeze(1).broadcast_to([128, nch, n1])

    def stageB_fwd(Ar, Ai, nch, n1, want_imag, evac):
        per = 512 // n1
        for g0 in range(0, nch, per):
            ng = min(per, nch - g0)
            rr = Ar[0][:, g0:g0 + ng, 0:n1]; ri = Ai[0][:, g0:g0 + ng, 0:n1]
            ptr, bptr = psum.get()
            o = ptr[:, 0:ng * n1].rearrange("p (c k) -> p c k", c=ng)
            S.op("pe", lambda e: e.matmul(o, lhsT=cm("C"), rhs=rr, start=True, stop=False), reads=[Ar[1], b_c], writes=[bptr])
            S.op("pe", lambda e: e.matmul(o, lhsT=cm("S"), rhs=ri, start=False, stop=True), reads=[Ai[1], b_c], writes=[bptr])
            pti = bpti = None
            if want_imag:
                pti, bpti = psum.get()
                o2 = pti[:, 0:ng * n1].rearrange("p (c k) -> p c k", c=ng)
                S.op("pe", lambda e: e.matmul(o2, lhsT=cm("C"), rhs=ri, start=True, stop=False), reads=[Ai[1], b_c], writes=[bpti])
                S.op("pe", lambda e: e.matmul(o2, lhsT=cm("nS"), rhs=rr, start=False, stop=True), reads=[Ar[1], b_c], writes=[bpti])
            evac(g0, ng, ptr, bptr, pti, bpti)

    def fwd_fft16k(chan_mms, nch, evac):
        st, bst = stage_data_stationary(chan_mms, nch, 256)
        yield
        Ar, Ai = cplx_mul(st[:, 0:nch, 0:128], st[:, 0:nch, 128:256], bc(cfm("T16r"), nch, 128), bc(cfm("T16i"), nch, 128), [bst, b_c], 128, nch)
        yield
        stageB_fwd(Ar, Ai, nch, 128, True, evac)
        yield

    def load_x(dram, row0):
        t, bt = xin_p.get()
        S.dma("sp", t[:], dram[row0:row0 + CB, :].rearrange("c (s1 s2) -> s1 c s2", s2=128), writes=[bt])
        return t, bt

    def long_conv(xa, xb, o, ga, gb, b):
        Kt = Kf[b % 2]; bK = b_Kf[b % 2][o]
        stB, bstB = stag_p.get()

        def evacB(g0, ng, ptr, bptr, pti, bpti):
            S.op("act", lambda e: e.copy(stB[:, g0:g0 + ng, 0:128], ptr[:, 0:ng * 128].rearrange("p (c k) -> p c k", c=ng)), reads=[bptr], writes=[bstB])
            S.op("act", lambda e: e.copy(stB[:, g0:g0 + ng, 128:256], pti[:, 0:ng * 128].rearrange("p (c k) -> p c k", c=ng)), reads=[bpti], writes=[bstB])
        yield from fwd_fft16k(lambda c: [(xa[0][:, c, :], cm("FA", 64), [xa[1]]), (xb[0][:, c, :], cm("FAi", 64), [xb[1]])], CB, evacB)
        Pr, Pi = cplx_mul(stB[:, :, 0:128], stB[:, :, 128:256], Kt[:, o, 0, :, :], Kt[:, o, 1, :, :], [bstB, bK], 128, CB)
        yield
        st, bst = stage_data_stationary(lambda c: [(Pr[0][:, c, :], cm("IA1"), [Pr[1]]), (Pi[0][:, c, :], cm("IA2"), [Pi[1]])], CB, 256)
        yield
        Br, Bi = cplx_mul(st[:, :, 0:128], st[:, :, 128:256], bc(cfm("T16r"), CB, 128), bc(cfm("T16ci"), CB, 128), [bst, b_c], 128, CB)
        yield
        for g0 in range(0, CB, 4):
            pt, bpt = psum.get()
            o4 = pt[0:64, :].rearrange("p (c k) -> p c k", c=4)
            S.op("pe", lambda e: e.matmul(o4, lhsT=cm("IBc"), rhs=Br[0][:, g0:g0 + 4, :], start=True, stop=False), reads=[Br[1], b_c], writes=[bpt])
            S.op("pe", lambda e: e.matmul(o4, lhsT=cm("IBs"), rhs=Bi[0][:, g0:g0 + 4, :], start=False, stop=True), reads=[Bi[1], b_c], writes=[bpt])
            S.op("act", lambda e: e.copy(yst[0][:, g0:g0 + 4, :], o4), reads=[bpt], writes=[b_yst[0]])
            pt2, bpt2 = psum.get()
            o5 = pt2[0:64, :].rearrange("p (c k) -> p c k", c=4)
            S.op("pe", lambda e: e.matmul(o5, lhsT=cm("IBc"), rhs=Bi[0][:, g0:g0 + 4, :], start=True, stop=False), reads=[Bi[1], b_c], writes=[bpt2])
            S.op("pe", lambda e: e.matmul(o5, lhsT=cm("IBsp"), rhs=Br[0][:, g0:g0 + 4, :], start=False, stop=True), reads=[Br[1], b_c], writes=[bpt2])
            S.op("act", lambda e: e.copy(yst[1][:, g0:g0 + 4, :], o5), reads=[bpt2], writes=[b_yst[1]])
        yield
        res = []
        skb = skp[:, o, b * CB:(b + 1) * CB].unsqueeze(2).broadcast_to([64, CB, 128])
        for h, (xin, gate) in enumerate(((xa, ga), (xb, gb))):
            z, bz = xin_p.get()
            tq, btq = tmp_p["dve"].get()
            tv = tq[0:64, :].rearrange("p (c k) -> p c k", c=CB)
            S.op("dve", lambda e: e.tensor_tensor(tv, xin[0][:], skb, ALU.mult), reads=[xin[1], b_skp], writes=[btq])
            S.op("dve", lambda e: e.tensor_tensor(tv, tv, yst[h][:], ALU.add), reads=[btq, b_yst[h]], writes=[btq])
            S.op("dve", lambda e: e.tensor_tensor(z[:], tv, gate[0][:], ALU.mult), reads=[btq, gate[1]], writes=[bz])
            res.append((z, bz))
        yield
        return res

    def filter_chain(b):
        Kt = Kf[b % 2]
        for d in range(2):
            S.op("dve", lambda e: e.tensor_tensor(win[:], tpos[:, d, :].unsqueeze(1).broadcast_to([64, 2 * CB, 128]),
                                                  dec[:, d, b, :].unsqueeze(2).broadcast_to([64, 2 * CB, 128]), ALU.mult),
                 reads=[b_tpos, b_dec], writes=[b_win])
            S.op("act", lambda e: e.activation(win[:], win[:], AF.Exp, scale=-1.0), reads=[b_win], writes=[b_win])
            for s20 in range(0, 128, 32):
                pt, bpt = psum.get()
                for q in range(32):
                    s2 = s20 + q
                    S.op("pe", lambda e, s2=s2, q=q: e.matmul(pt[0:64, q * 16:(q + 1) * 16], lhsT=h2[d][:, s2:L:128], rhs=fw3s[:, d, b, :],
                                                              start=True, stop=True), reads=[b_h2[d], b_fw3], writes=[bpt])
                S.op("dve", lambda e: e.tensor_tensor(kt[d][:, :, s20:s20 + 32].rearrange("p c s -> p s c"),
                                                      pt[0:64, :].rearrange("p (s c) -> p s c", c=16),
                                                      win[:, :, s20:s20 + 32].rearrange("p c s -> p s c"), ALU.mult),
                     reads=[bpt, b_win], writes=[b_kt[d]])
                yield
            if d == 1:
                S.op("dve", lambda e: e.memset(kt[1][0:1, :, 0:1], 0.0), reads=[], writes=[b_kt[1]])
            S.op("dve", lambda e: e.tensor_tensor(win[:], kt[d][:], kt[d][:], ALU.mult), reads=[b_kt[d]], writes=[b_win])
            S.op("dve", lambda e: e.tensor_reduce(nrm[:, d, :], win[:], axis=AX.X, op=ALU.add), reads=[b_win], writes=[b_nrm])
            yield
        S.op("dve", lambda e: e.tensor_tensor(nrm[:, 2, :], nrm[:, 0, :], nrm[:, 1, :], ALU.add), reads=[b_nrm], writes=[b_nrm])
        pt, bpt = psum.get()
        S.op("pe", lambda e: e.matmul(pt[0:64, 0:2 * CB], lhsT=ones_f[:], rhs=nrm[:, 2, :], start=True, stop=True), reads=[b_ones, b_nrm], writes=[bpt])
        S.op("dve", lambda e: e.tensor_scalar(nrm[:, 3, :], pt[0:64, 0:2 * CB], 1e-6, None, ALU.add), reads=[bpt], writes=[b_nrm])
        S.op("dve", lambda e: e.reciprocal(nrm[:, 3, :], nrm[:, 3, :]), reads=[b_nrm], writes=[b_nrm])
        S.op("act", lambda e: e.activation(nrm[:, 3, :], nrm[:, 3, :], AF.Sqrt), reads=[b_nrm], writes=[b_nrm])
        yield
        for d in range(2):
            S.op("dve", lambda e: e.tensor_tensor(ktb16[d][:], kt[d][:], nrm[:, 3, :].unsqueeze(2).broadcast_to([64, 2 * CB, 128]), ALU.mult),
                 reads=[b_kt[d], b_nrm], writes=[b_ktb[d]])
        yield
        for o in range(2):
            def evacK(g0, ng, ptr, bptr, pti, bpti, o=o):
                S.op("act", lambda e: e.copy(Kt[:, o, 0, g0:g0 + ng, :], ptr[:, 0:ng * 128].rearrange("p (c k) -> p c k", c=ng)), reads=[bptr], writes=[b_Kf[b % 2][o]])
                S.op("act", lambda e: e.copy(Kt[:, o, 1, g0:g0 + ng, :], pti[:, 0:ng * 128].rearrange("p (c k) -> p c k", c=ng)), reads=[bpti], writes=[b_Kf[b % 2][o]])
            yield from fwd_fft16k(lambda c, o=o: [(ktb16[0][:, o * CB + c, :], cm("FA", 64), [b_ktb[0]]),
                                                  (ktb16[1][:, o * CB + c, :], cm("FAhi", 64), [b_ktb[1]])], CB, evacK)

    def conv_chain(b):
        r0 = [b * CB, NCH + b * CB]
        v = [load_x(hv_d, r) for r in r0]
        x1 = [load_x(hx1_d, r) for r in r0]
        x2 = [load_x(hx2_d, r) for r in r0]
        yield
        z1 = yield from long_conv(v[0], v[1], 0, x1[0], x1[1], b)
        z2 = yield from long_conv(z1[0], z1[1], 1, x2[0], x2[1], b)
        for h in range(2):
            out_dma(o_yc[r0[h]:r0[h] + CB, :].rearrange("c (s1 s2) -> s1 c s2", s2=128), z2[h][0][:], z2[h][1])
        yield

    def fnet_chain(b):
        for h in range(2):
            row = h * NCH + b * CB
            zr, bzr = load_x(zr_d, row)
            zi, bzi = load_x(zi_d, row)
            yield
            st, bst = stage_data_stationary(lambda c: [(zr[:, c, :], cm("FN1", 64), [bzr]), (zi[:, c, :], cm("FN2", 64), [bzi])], CB, 128)
            yield
            Ar, Ai = cplx_mul(st[:, :, 0:64], st[:, :, 64:128], bc(cfm("T8r"), CB, 64), bc(cfm("T8i"), CB, 64), [bst, b_c], 64, CB)
            yield
            fo, bfo = fo_p.get()

            def evacF(g0, ng, ptr, bptr, pti, bpti):
                S.op("act", lambda e: e.copy(fo[:, g0:g0 + ng, :], ptr[:, 0:ng * 64].rearrange("p (c k) -> p c k", c=ng)), reads=[bptr], writes=[bfo])
            stageB_fwd(Ar, Ai, CB, 64, False, evacF)
            out_dma(o_ff[row:row + CB, :].rearrange("c (k2 k1) -> k2 c k1", k1=64), fo[:], bfo)
            yield

    def run_all(gens):
        gens = [g for g in gens if g is not None]
        while gens:
            for g in list(gens):
                try:
                    next(g)
                except StopIteration:
                    gens.remove(g)

    run_all([filter_chain(0)])
    for b in range(NBATCH):
        run_all([conv_chain(b), filter_chain(b + 1) if b + 1 < NBATCH else None, fnet_chain(b)])

    S.finish(outs_b, "sp")
    return S

D = 2048
G = 1024
DM = 4096
NT = 2048
TBK = 512
EPS = 1e-6


def build_L3(nc):
    S = Sched(nc)
    dt_in = lambda name, shape, dt=F32: nc.dram_tensor(name, shape, dt, kind="ExternalInput").ap()
    ffc = dt_in("ffc", [G, NT], BF16); ycc = dt_in("ycc", [G, NT], BF16)
    sgB = dt_in("sgB", [G, NT], BF16); sgC = dt_in("sgC", [G, NT], BF16)
    yAg = dt_in("yAg", [G, NT], BF16); yMg = dt_in("yMg", [G, NT], BF16)
    x_d = dt_in("x", [NT, D])
    fnet_w = dt_in("fnet_w", [G, G])
    vec = dt_in("vec3", [128, 3, 8])
    w_out = dt_in("w_out", [DM, D])
    post_g_bc = dt_in("post_g_bc", [128, D])
    xo = nc.dram_tensor("xo", [NT, D], F32, kind="ExternalOutput").ap()
    wsc = nc.dram_tensor("w_out_bf", [DM, D], BF16, kind="Internal").ap()
    outs_b = []

    sb = lambda name, shape, dt=F32: nc.alloc_sbuf_tensor(name, shape, dt)
    ones_f = sb("ones_f", [128, 128]); b_ones = Buf()
    S.op("pool", lambda e: e.memset(ones_f[:], 1.0), writes=[b_ones])
    vec_s = sb("vec_s", [128, 3, 8]); b_vec = Buf()
    S.dma("sp", vec_s[:], vec, writes=[b_vec])
    pg = sb("pg", [128, D]); b_pg = Buf()
    S.dma("sp", pg[:], post_g_bc, writes=[b_pg])
    psum = Pool_(nc, "ps", 8, [128, 512], F32, psum=True)
    fw_bf = sb("fw_bf", [128, 8, G], BF16); b_fw = Buf()

    b_wsc = [Buf() for _ in range(32)]
    for k in range(8):
        S.dma("pool", fw_bf[:, k, :], fnet_w[k * 128:(k + 1) * 128, :], writes=[b_fw])
    for q in range(8):
        S.dma("pool", wsc[q * 512:(q + 1) * 512, :], w_out[q * 512:(q + 1) * 512, :], writes=b_wsc[q * 4:(q + 1) * 4])
    last_scr = []

    ygall = sb("ygall", [128, 32, TBK], BF16); b_yg = [Buf() for _ in range(4)]
    wt_p = Pool_(nc, "wt", 4, [128, 8, 512], BF16)
    outraw = sb("outraw", [128, 4, D]); b_or = [Buf() for _ in range(4)]
    x_p = Pool_(nc, "xt", 2, [128, D], F32)
    in_p = Pool_(nc, "inb", 2, [128, 8, TBK], BF16)
    sg_p = Pool_(nc, "sgb", 2, [128, 8, TBK], BF16)
    yb = sb("yb", [128, 8, TBK]); b_yb = Buf()
    acc = sb("acc", [128, TBK]); b_acc = Buf()
    sq_p = Pool_(nc, "sq", 2, [128, TBK], F32)
    st_p = Pool_(nc, "st", 4, [128, 4], F32)
    junk = sb("junk", [128, D], BF16); b_junk = Buf()
    for b in b_yg + b_or + [b_yb, b_acc, b_junk] + [t[1] for t in wt_p.t + x_p.t + in_p.t + sg_p.t + sq_p.t + st_p.t]:
        for ls in last_scr:
            if ls.w is not None:
                b.r[ls.w[0]] = max(b.r.get(ls.w[0], 0), ls.w[1])
            for k_, v_ in ls.r.items():
                b.r[k_] = max(b.r.get(k_, 0), v_)

    def colsum(src_ap, bsrc, first):
        pt, bpt = psum.get()
        S.op("pe", lambda e: e.matmul(pt[:], lhsT=ones_f[:], rhs=src_ap, start=True, stop=True), reads=[b_ones] + bsrc, writes=[bpt])
        if first:
            S.op("dve", lambda e: e.tensor_copy(acc[:], pt[:]), reads=[bpt], writes=[b_acc])
        else:
            S.op("dve", lambda e: e.tensor_tensor(acc[:], acc[:], pt[:], ALU.add), reads=[bpt, b_acc], writes=[b_acc])

    def rstd_acc():
        S.op("dve", lambda e: e.tensor_scalar(acc[:], acc[:], 1.0 / G, EPS, ALU.mult, ALU.add), reads=[b_acc], writes=[b_acc])
        S.op("dve", lambda e: e.reciprocal(acc[:], acc[:]), reads=[b_acc], writes=[b_acc])
        S.op("act", lambda e: e.activation(acc[:], acc[:], AF.Sqrt), reads=[b_acc], writes=[b_acc])

    for blk in range(NT // TBK):
        ts = slice(blk * TBK, (blk + 1) * TBK)
        ld = lambda dram: dram[:, ts].rearrange("(k p) t -> p k t", p=128)
        S.dma("sp", ygall[:, 0:8, :], ld(yAg), writes=[b_yg[0]])
        S.dma("sp", ygall[:, 24:32, :], ld(yMg), writes=[b_yg[3]])
        ff, bff = in_p.get()
        S.dma("sp", ff[:], ld(ffc), writes=[bff])
        sg, bsg = sg_p.get()
        S.dma("sp", sg[:], ld(sgB), writes=[bsg])
        for j in range(8):
            pt, bpt = psum.get()
            for k in range(8):
                S.op("pe", lambda e, k=k: e.matmul(pt[:], lhsT=fw_bf[:, k, j * 128:(j + 1) * 128], rhs=ff[:, k, :], start=(k == 0), stop=(k == 7)),
                     reads=[b_fw, bff], writes=[bpt])
            S.op("act", lambda e: e.activation(yb[:, j, :], pt[:], AF.Identity, bias=vec_s[:, 0, j:j + 1]), reads=[bpt, b_vec], writes=[b_yb])
            sq, bsq = sq_p.get()
            S.op("act", lambda e: e.activation(sq[:], yb[:, j, :], AF.Square), reads=[b_yb], writes=[bsq])
            colsum(sq[:], [bsq], j == 0)
        rstd_acc()
        for j in range(8):
            sq, bsq = sq_p.get()
            S.op("dve", lambda e: e.scalar_tensor_tensor(sq[:], yb[:, j, :], vec_s[:, 1, j:j + 1], acc[:], ALU.mult, ALU.mult),
                 reads=[b_yb, b_vec, b_acc], writes=[bsq])
            S.op("dve", lambda e: e.tensor_tensor(ygall[:, 8 + j, :], sq[:], sg[:, j, :], ALU.mult), reads=[bsq, bsg], writes=[b_yg[1]])
        yc, byc = in_p.get()
        S.dma("sp", yc[:], ld(ycc), writes=[byc])
        sg, bsg = sg_p.get()
        S.dma("sp", sg[:], ld(sgC), writes=[bsg])
        for j in range(8):
            sq, bsq = sq_p.get()
            S.op("act", lambda e: e.activation(sq[:], yc[:, j, :], AF.Square), reads=[byc], writes=[bsq])
            colsum(sq[:], [bsq], j == 0)
        rstd_acc()
        for j in range(8):
            sq, bsq = sq_p.get()
            S.op("dve", lambda e: e.scalar_tensor_tensor(sq[:], yc[:, j, :], vec_s[:, 2, j:j + 1], acc[:], ALU.mult, ALU.mult),
                 reads=[byc, b_vec, b_acc], writes=[bsq])
            S.op("dve", lambda e: e.tensor_tensor(ygall[:, 16 + j, :], sq[:], sg[:, j, :], ALU.mult), reads=[bsq, bsg], writes=[b_yg[2]])
        for ng in range(4):
            pts = [psum.get() for _ in range(4)]
            for kq in range(4):
                wt, bwt = wt_p.get()
                S.dma("sp", wt[:], wsc[kq * 1024:(kq + 1) * 1024, ng * 512:(ng + 1) * 512].rearrange("(k p) n -> p k n", p=128),
                      reads=b_wsc[kq * 8:(kq + 1) * 8], writes=[bwt])
                for tt in range(4):
                    pt, bpt = pts[tt]
                    for k in range(8):
                        kk = kq * 8 + k
                        S.op("pe", lambda e, k=k, kk=kk, tt=tt, pt=pt: e.matmul(pt[:], lhsT=ygall[:, kk, tt * 128:(tt + 1) * 128], rhs=wt[:, k, :],
                                                                                 start=(kk == 0), stop=(kk == 31)),
                             reads=[b_yg[kq], bwt], writes=[bpt])
            for tt in range(4):
                pt, bpt = pts[tt]
                S.op("act", lambda e, tt=tt, pt=pt: e.copy(outraw[:, tt, ng * 512:(ng + 1) * 512], pt[:]), reads=[bpt], writes=[b_or[tt]])
        for tt in range(4):
            r0 = blk * TBK + tt * 128
            xt, bxt = x_p.get()
            S.dma("sp", xt[:], x_d[r0:r0 + 128, :], writes=[bxt])
            stt, bstt = st_p.get()
            S.op("dve", lambda e: e.memset(stt[:], 0.0), writes=[bstt])
            S.op("act", lambda e: e.activation(junk[:], outraw[:, tt, :], AF.Square, accum_out=stt[:, 0:1]), reads=[b_or[tt], bstt], writes=[b_junk, bstt])
            S.op("dve", lambda e: e.tensor_scalar(stt[:, 1:2], stt[:, 0:1], 1.0 / D, EPS, ALU.mult, ALU.add), reads=[bstt], writes=[bstt])
            S.op("dve", lambda e: e.reciprocal(stt[:, 2:3], stt[:, 1:2]), reads=[bstt], writes=[bstt])
            S.op("act", lambda e: e.activation(stt[:, 2:3], stt[:, 2:3], AF.Sqrt), reads=[bstt], writes=[bstt])
            S.op("dve", lambda e: e.scalar_tensor_tensor(outraw[:, tt, :], outraw[:, tt, :], stt[:, 2:3], pg[:], ALU.mult, ALU.mult),
                 reads=[b_or[tt], bstt, b_pg], writes=[b_or[tt]])
            S.op("dve", lambda e: e.tensor_tensor(xt[:], xt[:], outraw[:, tt, :], ALU.add), reads=[bxt, b_or[tt]], writes=[bxt])
            b = Buf(); outs_b.append(b)
            S.dma("sp", xo[r0:r0 + 128, :], xt[:], reads=[bxt], writes=[b])

    S.finish(outs_b, "sp")
    return S
import numpy as np
D=2048; G=1024; L=8192; TB=1024; H=16; NB=2

def chunked(v, nch):
    return np.ascontiguousarray(v.reshape(nch, 128).T)

def dftG_const():
    k = np.arange(G)
    ang = 2*np.pi*((k[:,None]*k[None,:]) % G)/G
    sc = 1.0/np.sqrt(float(L)*G)
    return np.concatenate([np.cos(ang)*sc, -np.sin(ang)*sc], axis=1).astype(np.float32)

def l1_inputs(inp, l, core, x_cur):
    b = core // 4; j = core % 4
    xb = x_cur[b]
    xpad = np.concatenate([np.zeros((H, D), np.float32), xb, np.zeros((H, D), np.float32)], 0)
    xp = np.stack([xpad[j*2048 + blk*TB : j*2048 + blk*TB + TB + 2*H] for blk in range(NB)], 0)
    cw = np.ascontiguousarray(inp['conv_dw_w'][l].T.reshape(8, 128, 31).transpose(1, 0, 2))
    gn = inp['group_norm_g'][l]
    cvec = np.stack([chunked(inp['conv_dw_b'][l], 8), chunked(inp['conv_ln_g'][l], 8), chunked(inp['conv_ln_b'][l], 8),
                     chunked(inp['conv_pw_b'][l], 8), chunked(gn[0:G], 8)], axis=1)
    hw = np.concatenate([inp['hy_short_w'][l], inp['hy_short_b'][l][None]], 0)
    hyw = np.ascontiguousarray(hw.T.reshape(24, 128, 4).transpose(1, 0, 2))
    return {
        "xp": np.ascontiguousarray(xp), "w_in": inp['w_in'][l],
        "pre_g_bc": np.ascontiguousarray(np.broadcast_to(inp['pre_norm_g'][l], (128, D))),
        "conv_w": cw, "conv_vec": np.ascontiguousarray(cvec), "conv_pw_w": inp['conv_pw_w'][l],
        "hy_w": hyw, "mem": inp['mem'][b],
        "mem_g_bc": np.ascontiguousarray(np.broadcast_to(inp['mem_norm_g'], (128, D))),
        "mem_wk": inp['mem_wk'][l], "mem_wv": inp['mem_wv'][l],
        "gD": chunked(gn[3*G:4*G], 8), "dftG": dftG_const(),
    }

BF = ml_dtypes.bfloat16
NCH=128; CB=8; NBATCH=NCH//CB

def l2_consts():
    j = np.arange(128)
    ang = 2*np.pi*((j[:,None]*j[None,:]) % 128)/128.0
    C = np.cos(ang); Sn = np.sin(ang)
    j64 = np.arange(64)
    ang64 = 2*np.pi*((j64[:,None]*j64[None,:]) % 64)/64.0
    C64 = np.cos(ang64); S64 = np.sin(ang64)
    cb = np.zeros((128, CBF_W), np.float64)
    def put(name, m):
        o, w = CO[name]; assert m.shape[1] == w, (name, m.shape); cb[:m.shape[0], o:o+w] = m
    FA = np.concatenate([C, -Sn], 1)
    put("FA", FA); put("FAhi", FA[64:128]); put("C", C); put("S", Sn); put("nS", -Sn)
    put("IA1", np.concatenate([C, Sn], 1)); put("IA2", np.concatenate([-Sn, C], 1))
    put("IBc", C[:, :64]/16384.0); put("IBs", -Sn[:, :64]/16384.0)
    put("FN1", np.concatenate([C64, -S64], 1)); put("FN2", np.concatenate([S64, C64], 1))
    put("FAi", np.concatenate([Sn, C], 1)); put("IBsp", Sn[:, :64]/16384.0)
    cf = np.zeros((128, CF_W), np.float64)
    a16 = 2*np.pi*(j[:,None]*j[None,:])/16384.0
    a8 = 2*np.pi*(j[:,None]*j64[None,:])/8192.0
    def putf(name, m):
        o, w = CF[name]; cf[:, o:o+w] = m
    putf("T16r", np.cos(a16)); putf("T16i", -np.sin(a16)); putf("T16ci", np.sin(a16))
    putf("T8r", np.cos(a8)); putf("T8i", -np.sin(a8))
    return cb.astype(BF), cf.astype(np.float32)

def l2_inputs(inp, l, core, Zb, hyb):
    cs = slice(core*NCH, (core+1)*NCH)
    def rows(arrs, off):
        return np.ascontiguousarray(np.concatenate([a[off + core*NCH: off + (core+1)*NCH] for a in arrs], 0))
    pos = inp['positions'].astype(np.int32)
    posb = pos[(L - np.arange(L)) % L]
    pos_rep = np.stack([np.broadcast_to(pos, (32, L)), np.broadcast_to(posb, (32, L))], 0)
    pos_t = np.stack([pos.reshape(64, 128), posb.reshape(64, 128)], 0)
    bands = np.linspace(1e-4, 15, 16, dtype=np.float32)
    bsc = np.zeros((32, 2), np.float32)
    bsc[:, 0] = np.concatenate([bands, bands]) / np.float32(L)
    bsc[:16, 1] = 0.25
    fw1 = inp['hy_fw1'][l]
    mvec = np.stack([inp['hy_fb1'][l], inp['hy_freq1'][l], inp['hy_fb2'][l], inp['hy_freq2'][l]], 1)
    fw3 = inp['hy_fw3'][l].reshape(64, 2, 2, 1024)[:, :, :, cs]
    fw3c = fw3.reshape(64, 2, 2, NBATCH, CB).transpose(0, 2, 3, 1, 4)
    dec = inp['hy_decay'][l][:, :, cs]
    decc = dec.reshape(2, 2, NBATCH, CB).transpose(1, 2, 0, 3)
    cbf, cf = l2_consts()
    return {
        "zr": rows(Zb, 0), "zi": rows(Zb, 1024),
        "hv": rows(hyb, 0), "hx1": rows(hyb, 1024), "hx2": rows(hyb, 2048),
        "pos_rep": np.ascontiguousarray(pos_rep), "pos_t": np.ascontiguousarray(pos_t), "bsc": bsc,
        "fw1a": np.ascontiguousarray(fw1[0:1]), "fw1b": np.ascontiguousarray(fw1[1:33]), "mvec": np.ascontiguousarray(mvec),
        "fw2": inp['hy_fw2'][l], "fw3c": np.ascontiguousarray(fw3c),
        "dec_rep": np.ascontiguousarray(np.broadcast_to(decc, (64,) + decc.shape)),
        "skip_rep": np.ascontiguousarray(np.broadcast_to(inp['hy_skip'][l][:, cs], (64, 2, NCH))),
        "cbf": cbf, "cf": cf,
    }


def l3_inputs(inp, l, core, x_cur, ffb, ycb, l1o):
    b = core // 4; j = core % 4
    ts = slice(j*2048, (j+1)*2048)
    gn = inp['group_norm_g'][l]
    vec3 = np.stack([chunked(inp['fnet_b'][l], 8), chunked(gn[1024:2048], 8), chunked(gn[2048:3072], 8)], axis=1)
    return {
        "ffc": np.ascontiguousarray(ffb[:, ts]), "ycc": np.ascontiguousarray(ycb[:, ts]),
        "sgB": l1o["sgB"], "sgC": l1o["sgC"], "yAg": l1o["yAg"], "yMg": l1o["yMg"],
        "x": np.ascontiguousarray(x_cur[b][ts]), "fnet_w": inp['fnet_w'][l], "vec3": np.ascontiguousarray(vec3),
        "w_out": inp['w_out'][l],
        "post_g_bc": np.ascontiguousarray(np.broadcast_to(inp['post_norm_g'][l], (128, 2048))),
    }

_PROGS = {}


def _prog(name, builder):
    if name not in _PROGS:
        nc = bass.Bass("TRN2", target_bir_lowering=False)
        builder(nc)
        _PROGS[name] = nc
    return _PROGS[name]


def kernel(**inputs):
    inp = {k: np.asarray(v) for k, v in inputs.items()}
    x_cur = np.ascontiguousarray(inp['x'], dtype=np.float32)
    cores = list(range(8))
    nc1 = _prog("L1", build_L1); nc2 = _prog("L2", build_L2); nc3 = _prog("L3", build_L3)
    for l in range(2):
        r1 = run_bass_kernel_spmd(nc1, [l1_inputs(inp, l, c, x_cur) for c in cores], core_ids=cores).results
        Zb = [np.concatenate([np.asarray(r1[4 * b + j]["Z"]) for j in range(4)], axis=1) for b in range(2)]
        hyb = [np.concatenate([np.asarray(r1[4 * b + j]["hyc"]) for j in range(4)], axis=1) for b in range(2)]
        r2 = run_bass_kernel_spmd(nc2, [l2_inputs(inp, l, c, Zb, hyb) for c in cores], core_ids=cores).results
        ffb = [np.concatenate([np.asarray(r2[c]["ff"])[b * NCH:(b + 1) * NCH] for c in cores], axis=0) for b in range(2)]
        ycb = [np.concatenate([np.asarray(r2[c]["yc"])[b * NCH:(b + 1) * NCH] for c in cores], axis=0) for b in range(2)]
        l1o = [{k: np.asarray(r1[c][k]) for k in ("sgB", "sgC", "yAg", "yMg")} for c in cores]
        r3 = run_bass_kernel_spmd(nc3, [l3_inputs(inp, l, c, x_cur, ffb[c // 4], ycb[c // 4], l1o[c]) for c in cores],
                                  core_ids=cores).results
        x_cur = np.stack([np.concatenate([np.asarray(r3[4 * b + j]["xo"]) for j in range(4)], axis=0) for b in range(2)], axis=0)
    return np.ascontiguousarray(x_cur, dtype=np.float32)
```

```python
import math
import ml_dtypes
import numpy as np
import concourse.bass as bass
import concourse.mybir as mybir
from concourse.bass_utils import run_bass_kernel_spmd

F32 = mybir.dt.float32
BF16 = mybir.dt.bfloat16
I32 = mybir.dt.int32
AF = mybir.ActivationFunctionType
ALU = mybir.AluOpType
AX = mybir.AxisListType


class Buf:
    __slots__ = ("name", "w", "r")

    def __init__(self, name=""):
        self.name = name
        self.w = None
        self.r = {}


class Sched:
    SEM_LIMIT = 30000

    def __init__(self, nc, n_dma_sems=40, same_engine_sync=True):
        self.nc = nc
        self.engs = {"pe": nc.tensor, "act": nc.scalar, "dve": nc.vector,
                     "pool": nc.gpsimd, "sp": nc.sync}
        self.same_engine_sync = same_engine_sync
        self.sems = {}
        self.cur = {}
        self.nsem = 0
        for e in self.engs:
            self._new_eng_sem(e)
        self.dma_sems = []
        for i in range(n_dma_sems):
            k = ("dma", i)
            self.sems[k] = nc.alloc_semaphore(f"dq{i}")
            self.dma_sems.append([k, 0])
        self.dma_rr = 0
        self.waited = {e: {} for e in self.engs}
        self.n_inst = {e: 0 for e in self.engs}
        self.n_wait = {e: 0 for e in self.engs}

    def _new_eng_sem(self, e):
        k = (e, self.nsem)
        self.nsem += 1
        self.sems[k] = self.nc.alloc_semaphore(f"s_{e}_{k[1]}")
        self.cur[e] = [k, 0]

    def _wait(self, e, tok):
        if tok is None:
            return
        k, v = tok
        if not self.same_engine_sync and k[0] == e:
            return
        if e == "pe" and k[0] == "pe":
            return
        if self.waited[e].get(k, 0) >= v:
            return
        self.engs[e].wait_ge(self.sems[k], v)
        self.waited[e][k] = v
        self.n_wait[e] += 1

    def _deps(self, e, reads, writes):
        for b in reads:
            self._wait(e, b.w)
        for b in writes:
            self._wait(e, b.w)
            for k, v in b.r.items():
                self._wait(e, (k, v))

    def _mark(self, tok, reads, writes):
        k, v = tok
        for b in reads:
            if b.r.get(k, 0) < v:
                b.r[k] = v
        for b in writes:
            b.w = tok
            b.r = {}

    def op(self, e, fn, reads=(), writes=()):
        self._deps(e, reads, writes)
        ins = fn(self.engs[e])
        c = self.cur[e]
        c[1] += 1
        ins.then_inc(self.sems[c[0]], 1)
        tok = (c[0], c[1])
        self._mark(tok, reads, writes)
        self.n_inst[e] += 1
        if c[1] >= self.SEM_LIMIT:
            self._new_eng_sem(e)
        return tok

    def dma(self, q, out, in_, reads=(), writes=(), **kw):
        self._deps(q, reads, writes)
        slot = self.dma_sems[self.dma_rr]
        self.dma_rr = (self.dma_rr + 1) % len(self.dma_sems)
        k, uses = slot
        if uses > 0:
            self._wait(q, (k, 16 * uses))
        ins = self.engs[q].dma_start(out=out, in_=in_, **kw)
        slot[1] = uses + 1
        ins.then_inc(self.sems[k], 16)
        tok = (k, 16 * (uses + 1))
        self._mark(tok, reads, writes)
        self.n_inst[q] += 1
        return tok

    def finish(self, bufs, e="sp"):
        for b in bufs:
            self._wait(e, b.w)

D = 2048
NIN = 11264
G = 1024
TB = 1024
H = 16
TBH = TB + 2 * H
NB = 2
EPS = 1e-6
C_AVAL, C_AGATE, C_F, C_HY, C_Q, C_GATE = 0, 1024, 2048, 3072, 6144, 7168


class Pool_:
    def __init__(self, nc, name, n, shape, dtype, psum=False):
        self.t = []
        for i in range(n):
            if psum:
                h = nc.alloc_psum_tensor(f"{name}{i}", shape, dtype)
            else:
                h = nc.alloc_sbuf_tensor(f"{name}{i}", shape, dtype)
            self.t.append((h, Buf(f"{name}{i}")))
        self.i = 0

    def get(self):
        r = self.t[self.i]
        self.i = (self.i + 1) % len(self.t)
        return r


def build_L1(nc):
    S = Sched(nc)
    dt_in = lambda name, shape, dt=F32: nc.dram_tensor(name, shape, dt, kind="ExternalInput").ap()
    dt_out = lambda name, shape, dt=BF16: nc.dram_tensor(name, shape, dt, kind="ExternalOutput").ap()
    xp = dt_in("xp", [NB, TBH, D])
    w_in = dt_in("w_in", [D, NIN])
    pre_g_bc = dt_in("pre_g_bc", [128, D])
    cw = dt_in("conv_w", [128, 8, 31])
    cvec = dt_in("conv_vec", [128, 5, 8])
    pw = dt_in("conv_pw_w", [G, G])
    hyw = dt_in("hy_w", [128, 24, 4])
    memx = dt_in("mem", [256, D])
    mem_g_bc = dt_in("mem_g_bc", [128, D])
    wk = dt_in("mem_wk", [D, G])
    wv = dt_in("mem_wv", [D, G])
    gD = dt_in("gD", [128, 8])
    dftG = dt_in("dftG", [G, 2 * G])
    o_yA = dt_out("yAg", [G, NB * TB])
    o_yM = dt_out("yMg", [G, NB * TB])
    o_sgB = dt_out("sgB", [G, NB * TB])
    o_sgC = dt_out("sgC", [G, NB * TB])
    o_Z = dt_out("Z", [2 * G, NB * TB])
    o_hy = dt_out("hyc", [3 * G, NB * TB])
    outs_b = []

    sb = lambda name, shape, dt=F32: nc.alloc_sbuf_tensor(name, shape, dt)
    ident = sb("ident", [128, 128], BF16); b_ident = Buf()
    ones_f = sb("ones_f", [128, 128], F32); b_ones_f = Buf()
    ones_b = sb("ones_b", [128, 128], BF16); b_ones_b = Buf()
    S.op("pool", lambda e: e.memset(ident[:], 1.0), writes=[b_ident])
    S.op("pool", lambda e: e.affine_select(ident[:], ident[:], pattern=[[-1, 128]], compare_op=ALU.is_equal,
                                           fill=0.0, base=0, channel_multiplier=1), reads=[b_ident], writes=[b_ident])
    S.op("pool", lambda e: e.memset(ones_f[:], 1.0), writes=[b_ones_f])
    S.op("pool", lambda e: e.memset(ones_b[:], 1.0), writes=[b_ones_b])
    g_bc = sb("g_bc", [128, D]); b_gbc = Buf()
    cw_s = sb("cw_s", [128, 8, 31]); cvec_s = sb("cvec_s", [128, 5, 8]); hyw_s = sb("hyw_s", [128, 24, 4])
    gD_s = sb("gD_s", [128, 8])
    b_par = Buf()
    S.dma("sp", cw_s[:], cw, writes=[b_par])
    S.dma("sp", cvec_s[:], cvec, writes=[b_par])
    S.dma("sp", hyw_s[:], hyw, writes=[b_par])
    S.dma("sp", gD_s[:], gD, writes=[b_par])

    psum = Pool_(nc, "ps", 8, [128, 512], F32, psum=True)
    wbf = Pool_(nc, "wbf", 3, [128, 16, 256], BF16)
    wkeys = {}
    xs_p = Pool_(nc, "xs", 2, [128, D], F32)
    hb_p = Pool_(nc, "hb", 2, [128, D], BF16)
    st_p = Pool_(nc, "st", 4, [128, 4], F32)
    hT = sb("hT", [128, 16, TBH], BF16); b_hT = Buf()

    pending = []
    tick = [0]

    def flush(delay):
        while pending and pending[0][0] + delay <= tick[0]:
            _, dst, ob, bob = pending.pop(0)
            b = Buf()
            outs_b.append(b)
            S.dma("sp", dst, ob[:], reads=[bob], writes=[b])

    def out_dma(dst, ob, bob):
        pending.append((tick[0], dst, ob, bob))

    def stream_w(dram, c0, nk):
        tick[0] += 1
        flush(2)
        key = (dram.tensor.name, c0 // 256)
        if key in wkeys:
            wb, bwb = wbf.t[wkeys[key]]
        else:
            slot = wbf.i
            for k_ in [k_ for k_, v_ in wkeys.items() if v_ == slot]:
                del wkeys[k_]
            wb, bwb = wbf.get()
            wkeys[key] = slot
            cbase = (c0 // 256) * 256
            S.dma("pool", wb[:, 0:nk, :], dram.rearrange("(k p) c -> p k c", p=128)[:, :, cbase:cbase + 256], writes=[bwb])
        off = c0 % 256
        return wb[:, :, off:off + 128], bwb

    def rms_rows(xs, bxs, rows, gtile, bg, out_bf, bout):
        stt, bstt = st_p.get()
        S.op("dve", lambda e: e.memset(stt[:], 0.0), writes=[bstt])
        S.op("act", lambda e: e.activation(out_bf[:rows], xs[:rows], AF.Square, accum_out=stt[:rows, 0:1]),
             reads=[bxs, bstt], writes=[bout, bstt])
        S.op("dve", lambda e: e.tensor_scalar(stt[:rows, 1:2], stt[:rows, 0:1], 1.0 / D, EPS, ALU.mult, ALU.add),
             reads=[bstt], writes=[bstt])
        S.op("dve", lambda e: e.reciprocal(stt[:rows, 2:3], stt[:rows, 1:2]), reads=[bstt], writes=[bstt])
        S.op("act", lambda e: e.activation(stt[:rows, 2:3], stt[:rows, 2:3], AF.Sqrt), reads=[bstt], writes=[bstt])
        S.op("dve", lambda e: e.scalar_tensor_tensor(out_bf[:rows], xs[:rows], stt[:rows, 2:3], gtile[:rows],
                                                     ALU.mult, ALU.mult), reads=[bxs, bstt, bg], writes=[bout])

    def transpose_into(src_bf, bsrc, rows, dstT, bdst, col0):
        for half in range(2):
            pt, bpt = psum.get()
            ptb = pt[:].bitcast(BF16)
            for kk in range(8):
                k = half * 8 + kk
                S.op("pe", lambda e, k=k, kk=kk: e.transpose(ptb[:, kk * 128:kk * 128 + rows],
                                                             src_bf[:rows, k * 128:(k + 1) * 128], ident[:rows, :rows]),
                     reads=[bsrc, b_ident], writes=[bpt])
            S.op("act", lambda e: e.copy(dstT[:, half * 8:half * 8 + 8, col0:col0 + rows],
                                         ptb.rearrange("p (k t) -> p k t", k=8)[:, :, 0:rows]),
                 reads=[bpt], writes=[bdst])

    cvb = sb("cvb", [128, 8, TB], BF16); b_cv = [Buf() for _ in range(8)]
    memT = cvb[:, 0:4, :].rearrange("p a (b m) -> p (a b) m", m=256); b_memT = Buf()
    mg_bc, b_mg = g_bc, b_gbc
    S.dma("sp", mg_bc[:], mem_g_bc, writes=[b_mg])
    for i in range(2):
        xs, bxs = xs_p.get()
        S.dma("sp", xs[:], memx[i * 128:(i + 1) * 128, :], writes=[bxs])
        hb, bhb = hb_p.get()
        rms_rows(xs, bxs, 128, mg_bc, b_mg, hb, bhb)
        transpose_into(hb, bhb, 128, memT, b_memT, i * 128)
    kT = sb("kT", [128, 8, 256], BF16); b_kT = Buf()
    vS = sb("vS", [128, 2, G], BF16); b_vS = Buf()
    for j in range(8):
        wb, bwb = stream_w(wk, j * 128, 16)
        pt, bpt = psum.get()
        for k in range(16):
            S.op("pe", lambda e, k=k: e.matmul(pt[:, 0:256], lhsT=wb[:, k, :], rhs=memT[:, k, :], start=(k == 0), stop=(k == 15)),
                 reads=[bwb, b_memT], writes=[bpt])
        S.op("act", lambda e: e.copy(kT[:, j, :], pt[:, 0:256]), reads=[bpt], writes=[b_kT])
    for j in range(8):
        wb, bwb = stream_w(wv, j * 128, 16)
        pt, bpt = psum.get()
        for m in range(2):
            for k in range(16):
                S.op("pe", lambda e, k=k, m=m: e.matmul(pt[:, m * 128:(m + 1) * 128], lhsT=memT[:, k, m * 128:(m + 1) * 128],
                                                        rhs=wb[:, k, :], start=(k == 0), stop=(k == 15)),
                     reads=[bwb, b_memT], writes=[bpt])
        S.op("act", lambda e: e.copy(vS[:, :, j * 128:(j + 1) * 128], pt[:, 0:256].rearrange("p (m c) -> p m c", m=2)),
             reads=[bpt], writes=[b_vS])

    S.dma("sp", g_bc[:], pre_g_bc, reads=[], writes=[b_gbc])
    yAb = sb("yAb", [128, 8, TB], BF16); b_yA = [Buf() for _ in range(8)]
    fT = sb("fT", [128, 8, TB], BF16); b_fT = Buf()
    s1 = sb("s1", [128, TB]); b_s1 = Buf()
    s2 = sb("s2", [128, TB]); b_s2 = Buf()
    u_p = Pool_(nc, "u", 1, [128, TBH], F32)
    sig_p = Pool_(nc, "sig", 1, [128, TBH], F32)
    acc_p = Pool_(nc, "acc", 2, [128, TB], F32)
    sq_p = Pool_(nc, "sq", 2, [128, TB], F32)
    ob_p = Pool_(nc, "ob", 4, [128, TB], BF16)
    eT_p = Pool_(nc, "eT", 2, [128, 2, 512], BF16)
    hytmp_p = Pool_(nc, "hytmp", 1, [128, TB], F32)
    ub_p = Pool_(nc, "ub", 2, [128, TBH], BF16)
    dg_p = Pool_(nc, "dg", 6, [128, 128], BF16)
    rden_p = Pool_(nc, "rden", 2, [128, 512], F32)

    def inproj(col0, halo):
        wb, bwb = stream_w(w_in, col0, 16)
        res = []
        if halo:
            chunks = [(i * 352, 352) for i in range(3)]
        else:
            chunks = [(H + i * 512, 512) for i in range(2)]
        for (t0, n) in chunks:
            pt, bpt = psum.get()
            for k in range(16):
                S.op("pe", lambda e, k=k, t0=t0, n=n, pt=pt: e.matmul(pt[:, 0:n], lhsT=wb[:, k, :], rhs=hT[:, k, t0:t0 + n],
                                                                       start=(k == 0), stop=(k == 15)),
                     reads=[bwb, b_hT], writes=[bpt])
            res.append((pt, bpt, t0, n))
        return res

    def colsum_acc(src, bsrc, acc, bacc, first):
        for hh in range(2):
            pt, bpt = psum.get()
            S.op("pe", lambda e: e.matmul(pt[:], lhsT=ones_f[:], rhs=src[:, hh * 512:(hh + 1) * 512], start=True, stop=True),
                 reads=[b_ones_f, bsrc], writes=[bpt])
            if first:
                S.op("dve", lambda e: e.tensor_copy(acc[:, hh * 512:(hh + 1) * 512], pt[:]), reads=[bpt], writes=[bacc])
            else:
                S.op("dve", lambda e: e.tensor_tensor(acc[:, hh * 512:(hh + 1) * 512], acc[:, hh * 512:(hh + 1) * 512], pt[:], ALU.add),
                     reads=[bpt, bacc], writes=[bacc])

    def rstd_from(acc, bacc, n):
        S.op("dve", lambda e: e.tensor_scalar(acc[:], acc[:], 1.0 / n, EPS, ALU.mult, ALU.add), reads=[bacc], writes=[bacc])
        S.op("dve", lambda e: e.reciprocal(acc[:], acc[:]), reads=[bacc], writes=[bacc])
        S.op("act", lambda e: e.activation(acc[:], acc[:], AF.Sqrt), reads=[bacc], writes=[bacc])

    def gated_out(ysrc, bys, j, gcol, rstd, brstd, gate_col0, odram, blk):
        res = inproj(gate_col0 + j * 128, False)
        ob, bob = ob_p.get()
        sg, bsg = sq_p.get()
        for (pt, bpt, t0, n) in res:
            o = t0 - H
            S.op("act", lambda e, pt=pt, o=o: e.activation(sg[:, o:o + 512], pt[:], AF.Silu), reads=[bpt], writes=[bsg])
        tmp, btmp = acc_p.get()
        S.op("dve", lambda e: e.scalar_tensor_tensor(tmp[:], ysrc, gcol, rstd[:], ALU.mult, ALU.mult),
             reads=[bys, brstd, b_par], writes=[btmp])
        S.op("dve", lambda e: e.tensor_tensor(ob[:], tmp[:], sg[:], ALU.mult), reads=[btmp, bsg], writes=[bob])
        out_dma(odram[j * 128:(j + 1) * 128, blk * TB:(blk + 1) * TB], ob, bob)

    def run_all(gens):
        gens = list(gens)
        while gens:
            for g in list(gens):
                try:
                    next(g)
                except StopIteration:
                    gens.remove(g)

    for blk in range(NB):
        for i in range(9):
            rows = 128 if i < 8 else TBH - 8 * 128
            xs, bxs = xs_p.get()
            S.dma("sp", xs[:rows], xp[blk, i * 128:i * 128 + rows, :], writes=[bxs])
            hb, bhb = hb_p.get()
            rms_rows(xs, bxs, rows, g_bc, b_gbc, hb, bhb)
            transpose_into(hb, bhb, rows, hT, b_hT, i * 128)

        def g_hy():
            for j in range(24):
                res = inproj(C_HY + j * 128, True)
                ob, bob = ob_p.get()
                tmp, btmp = hytmp_p.get()
                u, bu = u_p.get()
                for (pt, bpt, t0, n) in res:
                    S.op("act", lambda e, pt=pt, t0=t0, n=n: e.copy(u[:, t0:t0 + n], pt[:, 0:n]), reads=[bpt], writes=[bu])
                S.op("dve", lambda e: e.tensor_scalar(tmp[:], u[:, H - 1:H - 1 + TB], hyw_s[:, j, 0:1], hyw_s[:, j, 3:4], ALU.mult, ALU.add),
                     reads=[bu, b_par], writes=[btmp])
                S.op("dve", lambda e: e.scalar_tensor_tensor(tmp[:], u[:, H:H + TB], hyw_s[:, j, 1:2], tmp[:], ALU.mult, ALU.add),
                     reads=[bu, b_par, btmp], writes=[btmp])
                S.op("dve", lambda e: e.scalar_tensor_tensor(ob[:], u[:, H + 1:H + 1 + TB], hyw_s[:, j, 2:3], tmp[:], ALU.mult, ALU.add),
                     reads=[bu, b_par, btmp], writes=[bob])
                out_dma(o_hy[j * 128:(j + 1) * 128, blk * TB:(blk + 1) * TB], ob, bob)
                if j % 2 == 1:
                    yield

        def g_gate():
            for gi, odram in ((1, o_sgB), (2, o_sgC)):
                for j in range(8):
                    res = inproj(C_GATE + gi * G + j * 128, False)
                    ob, bob = ob_p.get()
                    for (pt, bpt, t0, n) in res:
                        o = t0 - H
                        S.op("act", lambda e, pt=pt, o=o: e.activation(ob[:, o:o + 512], pt[:], AF.Silu), reads=[bpt], writes=[bob])
                    out_dma(odram[j * 128:(j + 1) * 128, blk * TB:(blk + 1) * TB], ob, bob)
                    if j % 2 == 1:
                        yield

        def g_fz():
            for j in range(8):
                res = inproj(C_F + j * 128, False)
                for (pt, bpt, t0, n) in res:
                    o = t0 - H
                    S.op("act", lambda e, pt=pt, o=o: e.copy(fT[:, j, o:o + 512], pt[:]), reads=[bpt], writes=[b_fT])
                if j % 2 == 1:
                    yield
            for j in range(16):
                wb, bwb = stream_w(dftG, j * 128, 8)
                ob, bob = ob_p.get()
                for hh in range(2):
                    pt, bpt = psum.get()
                    for k in range(8):
                        S.op("pe", lambda e, k=k: e.matmul(pt[:], lhsT=wb[:, k, :], rhs=fT[:, k, hh * 512:(hh + 1) * 512],
                                                           start=(k == 0), stop=(k == 7)), reads=[bwb, b_fT], writes=[bpt])
                    S.op("act", lambda e: e.copy(ob[:, hh * 512:(hh + 1) * 512], pt[:]), reads=[bpt], writes=[bob])
                out_dma(o_Z[j * 128:(j + 1) * 128, blk * TB:(blk + 1) * TB], ob, bob)
                if j % 2 == 1:
                    yield

            for j in range(8):
                res = inproj(C_Q + j * 128, False)
                for (pt, bpt, t0, n) in res:
                    o = t0 - H
                    S.op("act", lambda e, pt=pt, o=o: e.copy(fT[:, j, o:o + 512], pt[:]), reads=[bpt], writes=[b_fT])
                if j % 2 == 1:
                    yield
        def g_A():
            pend = None
            for i in range(8):
                resg = inproj(C_AGATE + i * 128, True)
                sig, bsig = sig_p.get()
                for (pt, bpt, t0, n) in resg:
                    S.op("act", lambda e, pt=pt, t0=t0, n=n: e.activation(sig[:, t0:t0 + n], pt[:, 0:n], AF.Sigmoid), reads=[bpt], writes=[bsig])
                resv = inproj(C_AVAL + i * 128, True)
                ub, bub = ub_p.get()
                for (pt, bpt, t0, n) in resv:
                    S.op("dve", lambda e, pt=pt, t0=t0, n=n: e.tensor_tensor(ub[:, t0:t0 + n], pt[:, 0:n], sig[:, t0:t0 + n], ALU.mult),
                         reads=[bpt, bsig], writes=[bub])
                cps = [psum.get() for _ in range(2)]
                for tap in range(31):
                    dg, bdg = dg_p.get()
                    S.op("dve", lambda e, tap=tap: e.tensor_scalar(dg[:], ident[:], cw_s[:, i, tap:tap + 1], None, ALU.mult),
                         reads=[b_ident, b_par], writes=[bdg])
                    for hh in range(2):
                        S.op("pe", lambda e, tap=tap, hh=hh: e.matmul(cps[hh][0][:], lhsT=dg[:], rhs=ub[:, H - 15 + tap + hh * 512:H - 15 + tap + hh * 512 + 512],
                                                                      start=(tap == 0), stop=(tap == 30)), reads=[bdg, bub], writes=[cps[hh][1]])
                acc, bacc = acc_p.get()
                for hh in range(2):
                    S.op("act", lambda e, hh=hh: e.activation(acc[:, hh * 512:(hh + 1) * 512], cps[hh][0][:], AF.Identity, bias=cvec_s[:, 0, i:i + 1]),
                         reads=[cps[hh][1], b_par], writes=[bacc])
                sq, bsq = sq_p.get()
                S.op("act", lambda e: e.activation(sq[:], acc[:], AF.Square), reads=[bacc], writes=[bsq])
                S.op("act", lambda e: e.copy(cvb[:, i, :], acc[:]), reads=[bacc], writes=[b_cv[i], b_memT])
                if pend is not None:
                    colsum_acc(pend[0], pend[1], s1, b_s1, pend[4] == 0)
                    colsum_acc(pend[2], pend[3], s2, b_s2, pend[4] == 0)
                pend = (acc, bacc, sq, bsq, i)
                if i % 2 == 1:
                    yield
            colsum_acc(pend[0], pend[1], s1, b_s1, False)
            colsum_acc(pend[2], pend[3], s2, b_s2, False)
            S.op("dve", lambda e: e.tensor_scalar(s1[:], s1[:], 1.0 / G, None, ALU.mult), reads=[b_s1], writes=[b_s1])
            sq, bsq = sq_p.get()
            S.op("dve", lambda e: e.tensor_tensor(sq[:], s1[:], s1[:], ALU.mult), reads=[b_s1], writes=[bsq])
            S.op("dve", lambda e: e.scalar_tensor_tensor(s2[:], s2[:], 1.0 / G, sq[:], ALU.mult, ALU.subtract), reads=[b_s2, bsq], writes=[b_s2])
            S.op("dve", lambda e: e.tensor_scalar(s2[:], s2[:], EPS, None, ALU.add), reads=[b_s2], writes=[b_s2])
            S.op("dve", lambda e: e.reciprocal(s2[:], s2[:]), reads=[b_s2], writes=[b_s2])
            S.op("act", lambda e: e.activation(s2[:], s2[:], AF.Sqrt), reads=[b_s2], writes=[b_s2])
            for i in range(8):
                tmp, btmp = acc_p.get()
                S.op("dve", lambda e: e.tensor_tensor(tmp[:], cvb[:, i, :], s1[:], ALU.subtract), reads=[b_cv[i], b_s1], writes=[btmp])
                S.op("dve", lambda e: e.tensor_tensor(tmp[:], tmp[:], s2[:], ALU.mult), reads=[btmp, b_s2], writes=[btmp])
                S.op("act", lambda e: e.activation(cvb[:, i, :], tmp[:], AF.Silu, scale=cvec_s[:, 1, i:i + 1], bias=cvec_s[:, 2, i:i + 1]),
                     reads=[btmp, b_par], writes=[b_cv[i]])
            for j in range(8):
                wb, bwb = stream_w(pw, j * 128, 8)
                ya, bya = acc_p.get()
                for hh in range(2):
                    pt, bpt = psum.get()
                    for k in range(8):
                        S.op("pe", lambda e, k=k: e.matmul(pt[:], lhsT=wb[:, k, :], rhs=cvb[:, k, hh * 512:(hh + 1) * 512], start=(k == 0), stop=(k == 7)),
                             reads=[bwb, b_cv[k]], writes=[bpt])
                    S.op("act", lambda e: e.activation(ya[:, hh * 512:(hh + 1) * 512], pt[:], AF.Identity, bias=cvec_s[:, 3, j:j + 1]),
                         reads=[bpt, b_par], writes=[bya])
                sq, bsq = sq_p.get()
                S.op("act", lambda e: e.activation(sq[:], ya[:], AF.Square), reads=[bya], writes=[bsq])
                S.op("dve", lambda e: e.tensor_copy(yAb[:, j, :], ya[:]), reads=[bya], writes=[b_yA[j]])
                colsum_acc(sq, bsq, s1, b_s1, j == 0)
                if j % 2 == 1:
                    yield
            rstd_from(s1, b_s1, G)
            for j in range(8):
                gated_out(yAb[:, j, :], b_yA[j], j, cvec_s[:, 4, j:j + 1], s1, b_s1, C_GATE + 0 * G, o_yA, blk)
                if j % 2 == 1:
                    yield

        run_all([g_hy(), g_A(), g_gate(), g_fz()])
        first = True
        for hd in range(4):
            for hh in range(2):
                eT, beT = eT_p.get()
                for m in range(2):
                    pt, bpt = psum.get()
                    for dc in range(2):
                        S.op("pe", lambda e, dc=dc, m=m: e.matmul(pt[:], lhsT=kT[:, hd * 2 + dc, m * 128:(m + 1) * 128],
                                                                  rhs=fT[:, hd * 2 + dc, hh * 512:(hh + 1) * 512], start=(dc == 0), stop=(dc == 1)),
                             reads=[b_kT, b_fT], writes=[bpt])
                    S.op("act", lambda e, m=m: e.activation(eT[:, m, :], pt[:], AF.Exp, scale=1.0 / 16.0), reads=[bpt], writes=[beT])
                pd, bpd = psum.get()
                for m in range(2):
                    S.op("pe", lambda e, m=m: e.matmul(pd[:], lhsT=ones_b[:], rhs=eT[:, m, :], start=(m == 0), stop=(m == 1)),
                         reads=[b_ones_b, beT], writes=[bpd])
                rd, brd = rden_p.get()
                S.op("dve", lambda e: e.reciprocal(rd[:], pd[:]), reads=[bpd], writes=[brd])
                for cc in range(2):
                    j = hd * 2 + cc
                    pt, bpt = psum.get()
                    for m in range(2):
                        S.op("pe", lambda e, m=m: e.matmul(pt[:], lhsT=vS[:, m, j * 128:(j + 1) * 128], rhs=eT[:, m, :], start=(m == 0), stop=(m == 1)),
                             reads=[b_vS, beT], writes=[bpt])
                    S.op("dve", lambda e, j=j: e.tensor_tensor(yAb[:, j, hh * 512:(hh + 1) * 512], pt[:], rd[:], ALU.mult),
                         reads=[bpt, brd], writes=[b_yA[j]])
        for j in range(8):
            sq, bsq = sq_p.get()
            S.op("act", lambda e: e.activation(sq[:], yAb[:, j, :], AF.Square), reads=[b_yA[j]], writes=[bsq])
            colsum_acc(sq, bsq, s1, b_s1, j == 0)
        rstd_from(s1, b_s1, G)
        for j in range(8):
            gated_out(yAb[:, j, :], b_yA[j], j, gD_s[:, j:j + 1], s1, b_s1, C_GATE + 3 * G, o_yM, blk)

    flush(0)
    S.finish(outs_b, "sp")
    return S
import math

L = 8192
NCH = 128
CB = 8
NBATCH = NCH // CB
N16 = 16384
import os
SES = bool(int(os.environ.get("SES", "1")))

CO = {}
_o = 0
for _n, _w in (("FA", 256), ("FAhi", 256), ("C", 128), ("S", 128), ("nS", 128), ("IA1", 256), ("IA2", 256),
               ("IBc", 64), ("IBs", 64), ("FN1", 128), ("FN2", 128), ("FAi", 256), ("IBsp", 64)):
    CO[_n] = (_o, _w)
    _o += _w
CBF_W = _o
CF = {"T16r": (0, 128), "T16i": (128, 128), "T16ci": (256, 128), "T8r": (384, 64), "T8i": (448, 64)}
CF_W = 512


def build_L2(nc):
    S = Sched(nc, same_engine_sync=SES)
    dt_in = lambda name, shape, dt=F32: nc.dram_tensor(name, shape, dt, kind="ExternalInput").ap()
    dt_out = lambda name, shape, dt=BF16: nc.dram_tensor(name, shape, dt, kind="ExternalOutput").ap()
    zr_d = dt_in("zr", [2 * NCH, L], BF16); zi_d = dt_in("zi", [2 * NCH, L], BF16)
    hv_d = dt_in("hv", [2 * NCH, L], BF16); hx1_d = dt_in("hx1", [2 * NCH, L], BF16); hx2_d = dt_in("hx2", [2 * NCH, L], BF16)
    pos_rep = dt_in("pos_rep", [2, 32, L], I32)
    pos_t = dt_in("pos_t", [2, 64, 128], I32)
    bsc = dt_in("bsc", [32, 2])
    fw1a = dt_in("fw1a", [1, 64]); fw1b = dt_in("fw1b", [32, 64])
    mvec = dt_in("mvec", [64, 4])
    fw2 = dt_in("fw2", [64, 64])
    fw3c = dt_in("fw3c", [64, 2, NBATCH, 2, CB])
    dec_rep = dt_in("dec_rep", [64, 2, NBATCH, 2, CB])
    skip_rep = dt_in("skip_rep", [64, 2, NCH])
    cbf_d = dt_in("cbf", [128, CBF_W], BF16)
    cf_d = dt_in("cf", [128, CF_W])
    o_ff = dt_out("ff", [2 * NCH, L])
    o_yc = dt_out("yc", [2 * NCH, L])
    outs_b = []

    sb = lambda name, shape, dt=F32: nc.alloc_sbuf_tensor(name, shape, dt)
    cbf = sb("cbf_s", [128, CBF_W], BF16); cf = sb("cf_s", [128, CF_W]); b_c = Buf()
    S.dma("sp", cbf[:], cbf_d, writes=[b_c])
    S.dma("sp", cf[:], cf_d, writes=[b_c])
    cm = lambda n, rows=128: cbf[0:rows, CO[n][0]:CO[n][0] + CO[n][1]]
    cfm = lambda n: cf[:, CF[n][0]:CF[n][0] + CF[n][1]]
    ones_f = sb("ones_f", [64, 64]); b_ones = Buf()
    S.op("pool", lambda e: e.memset(ones_f[:], 1.0), writes=[b_ones])

    psum = Pool_(nc, "ps", 8, [128, 512], F32, psum=True)

    h2 = [sb("h2f", [64, L], BF16), sb("h2b", [64, L], BF16)]; b_h2 = [Buf(), Buf()]
    fw3s = sb("fw3s", [64, 2, NBATCH, 2 * CB], BF16); b_fw3 = Buf()
    dec = sb("dec", [64, 2, NBATCH, 2 * CB]); b_dec = Buf()
    skp = sb("skp", [64, 2, NCH]); b_skp = Buf()
    tpos = sb("tpos", [64, 2, 128]); b_tpos = Buf()
    S.dma("sp", skp[:], skip_rep, writes=[b_skp])
    S.dma("sp", dec[:], dec_rep.rearrange("p d b o c -> p d b (o c)"), writes=[b_dec])
    par = sb("mlp_par", [64, 4 + 64 + 64 + 64 + 2]); b_par = Buf()
    fq = sb("fq", [64, 2]); b_fq = Buf()
    f3st = sb("f3st", [64, 2, NBATCH, 2 * CB]); b_f3st = Buf()
    tpi = sb("tpi", [64, 2, 128], I32); b_tpi = Buf()
    b_ft = Buf()
    MW = 2048

    with nc.sbuf_tensor("mlp_tmp", [64, 16384], F32) as mt, nc.sbuf_tensor("fr_i", [64, MW], I32) as fr_i, \
            nc.sbuf_tensor("fr_f", [64, MW], F32) as fr_f:
        b_mt = Buf(); b_fr = Buf()

        def frac_neg(a, rows, bufs):
            S.op("dve", lambda e: e.tensor_copy(fr_i[0:rows, :], a), reads=bufs, writes=[b_fr])
            S.op("dve", lambda e: e.tensor_copy(fr_f[0:rows, :], fr_i[0:rows, :]), reads=[b_fr], writes=[b_fr])
            S.op("dve", lambda e: e.tensor_tensor(a, a, fr_f[0:rows, :], ALU.subtract), reads=bufs + [b_fr], writes=bufs)
            S.op("dve", lambda e: e.scalar_tensor_tensor(a, a, 0.5, a, ALU.is_gt, ALU.subtract), reads=bufs, writes=bufs)

        posi = mt[0:32, 0:8192].bitcast(I32)
        posi_t = mt[32:33, 0:8192].bitcast(I32)
        posf = mt[0:32, 8192:16384]
        feat_t = mt[32:33, 0:8192]
        S.dma("sp", par[:, 0:4], mvec, writes=[b_par])
        S.dma("sp", par[:, 4:68], fw2, writes=[b_par])
        S.dma("sp", par[0:32, 68:132], fw1b, writes=[b_par])
        S.dma("sp", par[32:33, 132:196], fw1a, writes=[b_par])
        S.dma("sp", par[0:32, 196:198], bsc, writes=[b_par])
        S.op("dve", lambda e: e.tensor_scalar(fq[:, 0:1], par[:, 1:2], 1.0 / (2 * math.pi), None, ALU.mult), reads=[b_par], writes=[b_fq])
        S.op("dve", lambda e: e.tensor_scalar(fq[:, 1:2], par[:, 3:4], 1.0 / (2 * math.pi), None, ALU.mult), reads=[b_par], writes=[b_fq])
        S.dma("sp", f3st[:], fw3c.rearrange("j d b o c -> j d b (o c)"), writes=[b_f3st])
        S.op("pool", lambda e: e.tensor_copy(fw3s[:], f3st[:]), reads=[b_f3st], writes=[b_fw3])
        S.dma("sp", tpi[:], pos_t.rearrange("d p s -> p d s"), writes=[b_tpi])
        S.op("dve", lambda e: e.tensor_copy(tpos[:], tpi[:]), reads=[b_tpi], writes=[b_tpos])
        S.op("dve", lambda e: e.tensor_scalar(tpos[:], tpos[:], 1.0 / L, None, ALU.mult), reads=[b_tpos], writes=[b_tpos])
        h1 = mt[0:64, 8192:16384]
        for d in range(2):
            S.dma("sp", posi, pos_rep[d], writes=[b_mt])
            S.dma("sp", posi_t, pos_rep[d, 0:1, :], writes=[b_mt])
            S.op("dve", lambda e: e.tensor_copy(posf, posi), reads=[b_mt], writes=[b_mt])
            S.op("dve", lambda e: e.tensor_copy(mt[32:33, 8192:16384], posi_t), reads=[b_mt], writes=[b_mt])
            S.op("dve", lambda e: e.tensor_scalar(feat_t, mt[32:33, 8192:16384], 1.0 / L, None, ALU.mult), reads=[b_mt], writes=[b_ft])
            S.op("dve", lambda e: e.tensor_scalar(posf, posf, par[0:32, 196:197], par[0:32, 197:198], ALU.mult, ALU.add), reads=[b_mt, b_par], writes=[b_mt])
            feats = mt[0:32, 0:8192]
            for q in range(L // MW):
                ws = slice(q * MW, (q + 1) * MW)
                frac_neg(posf[:, ws], 32, [b_mt])
                S.op("act", lambda e: e.activation(feats[:, ws], posf[:, ws], AF.Sin, scale=-2 * math.pi), reads=[b_mt], writes=[b_mt])
            for q in range(L // MW):
                hs = h1[:, q * MW:(q + 1) * MW]
                for ch in range(MW // 512):
                    sl = slice(q * MW + ch * 512, q * MW + (ch + 1) * 512)
                    pt, bpt = psum.get()
                    S.op("pe", lambda e: e.matmul(pt[0:64, :], lhsT=par[32:33, 132:196], rhs=feat_t[:, sl], start=True, stop=False),
                         reads=[b_par, b_ft], writes=[bpt])
                    S.op("pe", lambda e: e.matmul(pt[0:64, :], lhsT=par[0:32, 68:132], rhs=feats[:, sl], start=False, stop=True),
                         reads=[b_par, b_mt], writes=[bpt])
                    S.op("dve", lambda e: e.tensor_scalar(h1[:, sl], pt[0:64, :], par[:, 0:1], fq[:, 0:1], ALU.add, ALU.mult), reads=[bpt, b_par, b_fq], writes=[b_mt])
                S.op("dve", lambda e: e.tensor_scalar(hs, hs, 8.0, None, ALU.add), reads=[b_mt], writes=[b_mt])
                frac_neg(hs, 64, [b_mt])
                S.op("act", lambda e: e.activation(hs, hs, AF.Sin, scale=-2 * math.pi), reads=[b_mt], writes=[b_mt])
                pts = []
                for ch in range(MW // 512):
                    sl = slice(q * MW + ch * 512, q * MW + (ch + 1) * 512)
                    pt2, bpt2 = psum.get()
                    S.op("pe", lambda e: e.matmul(pt2[0:64, :], lhsT=par[:, 4:68], rhs=h1[:, sl], start=True, stop=True), reads=[b_par, b_mt], writes=[bpt2])
                    pts.append((pt2, bpt2, sl))
                for (pt2, bpt2, sl) in pts:
                    S.op("dve", lambda e: e.tensor_scalar(h1[:, sl], pt2[0:64, :], par[:, 2:3], fq[:, 1:2], ALU.add, ALU.mult), reads=[bpt2, b_par, b_fq], writes=[b_mt])
                S.op("dve", lambda e: e.tensor_scalar(hs, hs, 8.0, None, ALU.add), reads=[b_mt], writes=[b_mt])
                frac_neg(hs, 64, [b_mt])
                S.op("act", lambda e: e.activation(h2[d][:, q * MW:(q + 1) * MW], hs, AF.Sin, scale=-2 * math.pi), reads=[b_mt], writes=[b_h2[d]])
        b_mt_final = Buf()
        b_mt_final.w = b_mt.w
        b_mt_final.r = dict(b_mt.r)
        for k_, v_ in list(b_fr.r.items()) + ([b_fr.w] if b_fr.w else []):
            b_mt_final.r[k_] = max(b_mt_final.r.get(k_, 0), v_)

    S.op("dve", lambda e: e.tensor_scalar(f3st[:], dec[:], -1.0, None, ALU.mult), reads=[b_dec], writes=[b_f3st])
    S.op("dve", lambda e: e.tensor_tensor(dec[:], dec[:], f3st[:], ALU.max), reads=[b_dec, b_f3st], writes=[b_dec])
    Kf = [sb("Kf0", [128, 2, 2, CB, 128]), sb("Kf1", [128, 2, 2, CB, 128])]
    b_Kf = [[Buf(), Buf()], [Buf(), Buf()]]
    stag_p = Pool_(nc, "stag", 3, [128, CB, 256], F32)
    tmp_p = {e: Pool_(nc, "tmp" + e, n_, [128, CB * 128], F32) for e, n_ in (("dve", 3), ("pool", 2))}
    cb_p = Pool_(nc, "cb", 8, [128, CB, 128], BF16)
    xin_p = Pool_(nc, "xin", 12, [64, CB, 128], BF16)
    kt = [sb("ktf", [64, 2 * CB, 128]), sb("ktb", [64, 2 * CB, 128])]; b_kt = [Buf(), Buf()]
    ktb16 = [sb("ktf16", [64, 2 * CB, 128], BF16), sb("ktb16", [64, 2 * CB, 128], BF16)]; b_ktb = [Buf(), Buf()]
    win = sb("win", [64, 2 * CB, 128]); b_win = Buf()
    nrm = sb("nrm", [64, 4, 2 * CB]); b_nrm = Buf()
    yst = [sb("yst_r", [64, CB, 128]), sb("yst_i", [64, CB, 128])]; b_yst = [Buf(), Buf()]
    fo_p = Pool_(nc, "fo", 2, [128, CB, 64], BF16)
    for b in b_Kf[0] + b_Kf[1] + [b_kt[0], b_kt[1], b_ktb[0], b_ktb[1], b_win, b_nrm] + b_yst + \
            [t[1] for t in stag_p.t + tmp_p["dve"].t + tmp_p["pool"].t + cb_p.t + xin_p.t + fo_p.t]:
        b.w = b_mt_final.w
        b.r = dict(b_mt_final.r)

    def out_dma(dst, src, bsrc):
        b = Buf(); outs_b.append(b)
        S.dma("sp", dst, src, reads=[bsrc], writes=[b])

    def stage_data_stationary(chan_mms, nch, width, rows=128):
        st, bst = stag_p.get()
        per = 512 // width
        for c0 in range(0, nch, per):
            pt, bpt = psum.get()
            n = min(per, nch - c0)
            for i in range(n):
                mms = chan_mms(c0 + i)
                for q, (lhsT, rhs, rd) in enumerate(mms):
                    S.op("pe", lambda e, lhsT=lhsT, rhs=rhs, i=i, q=q: e.matmul(pt[0:rows, i * width:(i + 1) * width], lhsT=lhsT, rhs=rhs,
                                                                                 start=(q == 0), stop=(q == len(mms) - 1)),
                         reads=rd + [b_c], writes=[bpt])
            S.op("act", lambda e: e.copy(st[0:rows, c0:c0 + n, 0:width], pt[0:rows, 0:n * width].rearrange("p (c w) -> p c w", c=n)),
                 reads=[bpt], writes=[bst])
        return st, bst

    def cplx_mul(sr, si, tr, ti, rd, n1, nch, rows=128):
        dr, bdr = cb_p.get(); di, bdi = cb_p.get()
        drv = dr[0:rows, 0:nch, 0:n1]; div = di[0:rows, 0:nch, 0:n1]
        ta, bta = tmp_p["dve"].get(); tb, btb = tmp_p["dve"].get()
        tav = ta[0:rows, 0:nch * n1].rearrange("p (c k) -> p c k", c=nch); tbv = tb[0:rows, 0:nch * n1].rearrange("p (c k) -> p c k", c=nch)
        S.op("dve", lambda e: e.tensor_tensor(tav, sr, tr, ALU.mult), reads=rd, writes=[bta])
        S.op("dve", lambda e: e.tensor_tensor(tbv, si, ti, ALU.mult), reads=rd, writes=[btb])
        S.op("dve", lambda e: e.tensor_tensor(drv, tav, tbv, ALU.subtract), reads=[bta, btb], writes=[bdr])
        tc, btc = tmp_p["pool"].get(); td, btd = tmp_p["pool"].get()
        tcv = tc[0:rows, 0:nch * n1].rearrange("p (c k) -> p c k", c=nch); tdv = td[0:rows, 0:nch * n1].rearrange("p (c k) -> p c k", c=nch)
        S.op("pool", lambda e: e.tensor_tensor(tcv, sr, ti, ALU.mult), reads=rd, writes=[btc])
        S.op("pool", lambda e: e.tensor_tensor(tdv, si, tr, ALU.mult), reads=rd, writes=[btd])
        S.op("pool", lambda e: e.tensor_tensor(div, tcv, tdv, ALU.add), reads=[btc, btd], writes=[bdi])
        return (dr, bdr), (di, bdi)

    def bc(tab, nch, n1):
        return tab.unsqueeze(1).broadcast_to([128, nch, n1])

    def stageB_fwd(Ar, Ai, nch, n1, want_imag, evac):
        per = 512 // n1
        for g0 in range(0, nch, per):
            ng = min(per, nch - g0)
            rr = Ar[0][:, g0:g0 + ng, 0:n1]; ri = Ai[0][:, g0:g0 + ng, 0:n1]
            ptr, bptr = psum.get()
            o = ptr[:, 0:ng * n1].rearrange("p (c k) -> p c k", c=ng)
            S.op("pe", lambda e: e.matmul(o, lhsT=cm("C"), rhs=rr, start=True, stop=False), reads=[Ar[1], b_c], writes=[bptr])
            S.op("pe", lambda e: e.matmul(o, lhsT=cm("S"), rhs=ri, start=False, stop=True), reads=[Ai[1], b_c], writes=[bptr])
            pti = bpti = None
            if want_imag:
                pti, bpti = psum.get()
                o2 = pti[:, 0:ng * n1].rearrange("p (c k) -> p c k", c=ng)
                S.op("pe", lambda e: e.matmul(o2, lhsT=cm("C"), rhs=ri, start=True, stop=False), reads=[Ai[1], b_c], writes=[bpti])
                S.op("pe", lambda e: e.matmul(o2, lhsT=cm("nS"), rhs=rr, start=False, stop=True), reads=[Ar[1], b_c], writes=[bpti])
            evac(g0, ng, ptr, bptr, pti, bpti)

    def fwd_fft16k(chan_mms, nch, evac):
        st, bst = stage_data_stationary(chan_mms, nch, 256)
        yield
        Ar, Ai = cplx_mul(st[:, 0:nch, 0:128], st[:, 0:nch, 128:256], bc(cfm("T16r"), nch, 128), bc(cfm("T16i"), nch, 128), [bst, b_c], 128, nch)
        yield
        stageB_fwd(Ar, Ai, nch, 128, True, evac)
        yield

    def load_x(dram, row0):
        t, bt = xin_p.get()
        S.dma("sp", t[:], dram[row0:row0 + CB, :].rearrange("c (s1 s2) -> s1 c s2", s2=128), writes=[bt])
        return t, bt

    def long_conv(xa, xb, o, ga, gb, b):
        Kt = Kf[b % 2]; bK = b_Kf[b % 2][o]
        stB, bstB = stag_p.get()

        def evacB(g0, ng, ptr, bptr, pti, bpti):
            S.op("act", lambda e: e.copy(stB[:, g0:g0 + ng, 0:128], ptr[:, 0:ng * 128].rearrange("p (c k) -> p c k", c=ng)), reads=[bptr], writes=[bstB])
            S.op("act", lambda e: e.copy(stB[:, g0:g0 + ng, 128:256], pti[:, 0:ng * 128].rearrange("p (c k) -> p c k", c=ng)), reads=[bpti], writes=[bstB])
        yield from fwd_fft16k(lambda c: [(xa[0][:, c, :], cm("FA", 64), [xa[1]]), (xb[0][:, c, :], cm("FAi", 64), [xb[1]])], CB, evacB)
        Pr, Pi = cplx_mul(stB[:, :, 0:128], stB[:, :, 128:256], Kt[:, o, 0, :, :], Kt[:, o, 1, :, :], [bstB, bK], 128, CB)
        yield
        st, bst = stage_data_stationary(lambda c: [(Pr[0][:, c, :], cm("IA1"), [Pr[1]]), (Pi[0][:, c, :], cm("IA2"), [Pi[1]])], CB, 256)
        yield
        Br, Bi = cplx_mul(st[:, :, 0:128], st[:, :, 128:256], bc(cfm("T16r"), CB, 128), bc(cfm("T16ci"), CB, 128), [bst, b_c], 128, CB)
        yield
        for g0 in range(0, CB, 4):
            pt, bpt = psum.get()
            o4 = pt[0:64, :].rearrange("p (c k) -> p c k", c=4)
            S.op("pe", lambda e: e.matmul(o4, lhsT=cm("IBc"), rhs=Br[0][:, g0:g0 + 4, :], start=True, stop=False), reads=[Br[1], b_c], writes=[bpt])
            S.op("pe", lambda e: e.matmul(o4, lhsT=cm("IBs"), rhs=Bi[0][:, g0:g0 + 4, :], start=False, stop=True), reads=[Bi[1], b_c], writes=[bpt])
            S.op("act", lambda e: e.copy(yst[0][:, g0:g0 + 4, :], o4), reads=[bpt], writes=[b_yst[0]])
            pt2, bpt2 = psum.get()
            o5 = pt2[0:64, :].rearrange("p (c k) -> p c k", c=4)
            S.op("pe", lambda e: e.matmul(o5, lhsT=cm("IBc"), rhs=Bi[0][:, g0:g0 + 4, :], start=True, stop=False), reads=[Bi[1], b_c], writes=[bpt2])
            S.op("pe", lambda e: e.matmul(o5, lhsT=cm("IBsp"), rhs=Br[0][:, g0:g0 + 4, :], start=False, stop=True), reads=[Br[1], b_c], writes=[bpt2])
            S.op("act", lambda e: e.copy(yst[1][:, g0:g0 + 4, :], o5), reads=[bpt2], writes=[b_yst[1]])
        yield
        res = []
        skb = skp[:, o, b * CB:(b + 1) * CB].unsqueeze(2).broadcast_to([64, CB, 128])
        for h, (xin, gate) in enumerate(((xa, ga), (xb, gb))):
            z, bz = xin_p.get()
            tq, btq = tmp_p["dve"].get()
            tv = tq[0:64, :].rearrange("p (c k) -> p c k", c=CB)
            S.op("dve", lambda e: e.tensor_tensor(tv, xin[0][:], skb, ALU.mult), reads=[xin[1], b_skp], writes=[btq])
            S.op("dve", lambda e: e.tensor_tensor(tv, tv, yst[h][:], ALU.add), reads=[btq, b_yst[h]], writes=[btq])
            S.op("dve", lambda e: e.tensor_tensor(z[:], tv, gate[0][:], ALU.mult), reads=[btq, gate[1]], writes=[bz])
            res.append((z, bz))
        yield
        return res

    def filter_chain(b):
        Kt = Kf[b % 2]
        for d in range(2):
            S.op("dve", lambda e: e.tensor_tensor(win[:], tpos[:, d, :].unsqueeze(1).broadcast_to([64, 2 * CB, 128]),
                                                  dec[:, d, b, :].unsqueeze(2).broadcast_to([64, 2 * CB, 128]), ALU.mult),
                 reads=[b_tpos, b_dec], writes=[b_win])
            S.op("act", lambda e: e.activation(win[:], win[:], AF.Exp, scale=-1.0), reads=[b_win], writes=[b_win])
            for s20 in range(0, 128, 32):
                pt, bpt = psum.get()
                for q in range(32):
                    s2 = s20 + q
                    S.op("pe", lambda e, s2=s2, q=q: e.matmul(pt[0:64, q * 16:(q + 1) * 16], lhsT=h2[d][:, s2:L:128], rhs=fw3s[:, d, b, :],
                                                              start=True, stop=True), reads=[b_h2[d], b_fw3], writes=[bpt])
                S.op("dve", lambda e: e.tensor_tensor(kt[d][:, :, s20:s20 + 32].rearrange("p c s -> p s c"),
                                                      pt[0:64, :].rearrange("p (s c) -> p s c", c=16),
                                                      win[:, :, s20:s20 + 32].rearrange("p c s -> p s c"), ALU.mult),
                     reads=[bpt, b_win], writes=[b_kt[d]])
                yield
            if d == 1:
                S.op("dve", lambda e: e.memset(kt[1][0:1, :, 0:1], 0.0), reads=[], writes=[b_kt[1]])
            S.op("act", lambda e: e.activation(win[:], kt[d][:], AF.Square), reads=[b_kt[d]], writes=[b_win])
            S.op("dve", lambda e: e.tensor_reduce(nrm[:, d, :], win[:], axis=AX.X, op=ALU.add), reads=[b_win], writes=[b_nrm])
            yield
        S.op("dve", lambda e: e.tensor_tensor(nrm[:, 2, :], nrm[:, 0, :], nrm[:, 1, :], ALU.add), reads=[b_nrm], writes=[b_nrm])
        pt, bpt = psum.get()
        S.op("pe", lambda e: e.matmul(pt[0:64, 0:2 * CB], lhsT=ones_f[:], rhs=nrm[:, 2, :], start=True, stop=True), reads=[b_ones, b_nrm], writes=[bpt])
        S.op("dve", lambda e: e.tensor_scalar(nrm[:, 3, :], pt[0:64, 0:2 * CB], 1e-6, None, ALU.add), reads=[bpt], writes=[b_nrm])
        S.op("dve", lambda e: e.reciprocal(nrm[:, 3, :], nrm[:, 3, :]), reads=[b_nrm], writes=[b_nrm])
        S.op("act", lambda e: e.activation(nrm[:, 3, :], nrm[:, 3, :], AF.Sqrt), reads=[b_nrm], writes=[b_nrm])
        yield
        for d in range(2):
            S.op("dve", lambda e: e.tensor_tensor(ktb16[d][:], kt[d][:], nrm[:, 3, :].unsqueeze(2).broadcast_to([64, 2 * CB, 128]), ALU.mult),
                 reads=[b_kt[d], b_nrm], writes=[b_ktb[d]])
        yield
        for o in range(2):
            def evacK(g0, ng, ptr, bptr, pti, bpti, o=o):
                S.op("act", lambda e: e.copy(Kt[:, o, 0, g0:g0 + ng, :], ptr[:, 0:ng * 128].rearrange("p (c k) -> p c k", c=ng)), reads=[bptr], writes=[b_Kf[b % 2][o]])
                S.op("act", lambda e: e.copy(Kt[:, o, 1, g0:g0 + ng, :], pti[:, 0:ng * 128].rearrange("p (c k) -> p c k", c=ng)), reads=[bpti], writes=[b_Kf[b % 2][o]])
            yield from fwd_fft16k(lambda c, o=o: [(ktb16[0][:, o * CB + c, :], cm("FA", 64), [b_ktb[0]]),
                                                  (ktb16[1][:, o * CB + c, :], cm("FAhi", 64), [b_ktb[1]])], CB, evacK)

    def conv_chain(b):
        r0 = [b * CB, NCH + b * CB]
        v = [load_x(hv_d, r) for r in r0]
        x1 = [load_x(hx1_d, r) for r in r0]
        x2 = [load_x(hx2_d, r) for r in r0]
        yield
        z1 = yield from long_conv(v[0], v[1], 0, x1[0], x1[1], b)
        z2 = yield from long_conv(z1[0], z1[1], 1, x2[0], x2[1], b)
        for h in range(2):
            out_dma(o_yc[r0[h]:r0[h] + CB, :].rearrange("c (s1 s2) -> s1 c s2", s2=128), z2[h][0][:], z2[h][1])
        yield

    def fnet_chain(b):
        for h in range(2):
            row = h * NCH + b * CB
            zr, bzr = load_x(zr_d, row)
            zi, bzi = load_x(zi_d, row)
            yield
            st, bst = stage_data_stationary(lambda c: [(zr[:, c, :], cm("FN1", 64), [bzr]), (zi[:, c, :], cm("FN2", 64), [bzi])], CB, 128)
            yield
            Ar, Ai = cplx_mul(st[:, :, 0:64], st[:, :, 64:128], bc(cfm("T8r"), CB, 64), bc(cfm("T8i"), CB, 64), [bst, b_c], 64, CB)
            yield
            fo, bfo = fo_p.get()

            def evacF(g0, ng, ptr, bptr, pti, bpti):
                S.op("act", lambda e: e.copy(fo[:, g0:g0 + ng, :], ptr[:, 0:ng * 64].rearrange("p (c k) -> p c k", c=ng)), reads=[bptr], writes=[bfo])
            stageB_fwd(Ar, Ai, CB, 64, False, evacF)
            out_dma(o_ff[row:row + CB, :].rearrange("c (k2 k1) -> k2 c k1", k1=64), fo[:], bfo)
            yield

    def run_all(gens):
        gens = [g for g in gens if g is not None]
        while gens:
            for g in list(gens):
                try:
                    next(g)
                except StopIteration:
                    gens.remove(g)

    run_all([filter_chain(0)])
    for b in range(NBATCH):
        run_all([conv_chain(b), filter_chain(b + 1) if b + 1 < NBATCH else None, fnet_chain(b)])

    S.finish(outs_b, "sp")
    return S

D = 2048
G = 1024
DM = 4096
NT = 2048
TBK = 512
EPS = 1e-6


def build_L3(nc):
    S = Sched(nc)
    dt_in = lambda name, shape, dt=F32: nc.dram_tensor(name, shape, dt, kind="ExternalInput").ap()
    ffc = dt_in("ffc", [G, NT], BF16); ycc = dt_in("ycc", [G, NT], BF16)
    sgB = dt_in("sgB", [G, NT], BF16); sgC = dt_in("sgC", [G, NT], BF16)
    yAg = dt_in("yAg", [G, NT], BF16); yMg = dt_in("yMg", [G, NT], BF16)
    x_d = dt_in("x", [NT, D])
    fnet_w = dt_in("fnet_w", [G, G])
    vec = dt_in("vec3", [128, 3, 8])
    w_out = dt_in("w_out", [DM, D])
    post_g_bc = dt_in("post_g_bc", [128, D])
    xo = nc.dram_tensor("xo", [NT, D], F32, kind="ExternalOutput").ap()
    wsc = nc.dram_tensor("w_out_bf", [DM, D], BF16, kind="Internal").ap()
    outs_b = []

    sb = lambda name, shape, dt=F32: nc.alloc_sbuf_tensor(name, shape, dt)
    ones_f = sb("ones_f", [128, 128]); b_ones = Buf()
    S.op("pool", lambda e: e.memset(ones_f[:], 1.0), writes=[b_ones])
    vec_s = sb("vec_s", [128, 3, 8]); b_vec = Buf()
    S.dma("sp", vec_s[:], vec, writes=[b_vec])
    pg = sb("pg", [128, D]); b_pg = Buf()
    S.dma("sp", pg[:], post_g_bc, writes=[b_pg])
    psum = Pool_(nc, "ps", 8, [128, 512], F32, psum=True)
    fw_bf = sb("fw_bf", [128, 8, G], BF16); b_fw = Buf()

    b_wsc = [Buf() for _ in range(32)]
    for k in range(8):
        S.dma("pool", fw_bf[:, k, :], fnet_w[k * 128:(k + 1) * 128, :], writes=[b_fw])
    for q in range(8):
        S.dma("pool", wsc[q * 512:(q + 1) * 512, :], w_out[q * 512:(q + 1) * 512, :], writes=b_wsc[q * 4:(q + 1) * 4])
    last_scr = []

    ygall = sb("ygall", [128, 32, TBK], BF16); b_yg = [Buf() for _ in range(4)]
    wt_p = Pool_(nc, "wt", 4, [128, 8, 512], BF16)
    outraw = sb("outraw", [128, 4, D]); b_or = [Buf() for _ in range(4)]
    x_p = Pool_(nc, "xt", 2, [128, D], F32)
    in_p = Pool_(nc, "inb", 2, [128, 8, TBK], BF16)
    sg_p = Pool_(nc, "sgb", 2, [128, 8, TBK], BF16)
    yb = sb("yb", [128, 8, TBK]); b_yb = Buf()
    acc = sb("acc", [128, TBK]); b_acc = Buf()
    sq_p = Pool_(nc, "sq", 2, [128, TBK], F32)
    st_p = Pool_(nc, "st", 4, [128, 4], F32)
    junk = sb("junk", [128, D], BF16); b_junk = Buf()
    for b in b_yg + b_or + [b_yb, b_acc, b_junk] + [t[1] for t in wt_p.t + x_p.t + in_p.t + sg_p.t + sq_p.t + st_p.t]:
        for ls in last_scr:
            if ls.w is not None:
                b.r[ls.w[0]] = max(b.r.get(ls.w[0], 0), ls.w[1])
            for k_, v_ in ls.r.items():
                b.r[k_] = max(b.r.get(k_, 0), v_)

    def colsum(src_ap, bsrc, first):
        pt, bpt = psum.get()
        S.op("pe", lambda e: e.matmul(pt[:], lhsT=ones_f[:], rhs=src_ap, start=True, stop=True), reads=[b_ones] + bsrc, writes=[bpt])
        if first:
            S.op("dve", lambda e: e.tensor_copy(acc[:], pt[:]), reads=[bpt], writes=[b_acc])
        else:
            S.op("dve", lambda e: e.tensor_tensor(acc[:], acc[:], pt[:], ALU.add), reads=[bpt, b_acc], writes=[b_acc])

    def rstd_acc():
        S.op("dve", lambda e: e.tensor_scalar(acc[:], acc[:], 1.0 / G, EPS, ALU.mult, ALU.add), reads=[b_acc], writes=[b_acc])
        S.op("dve", lambda e: e.reciprocal(acc[:], acc[:]), reads=[b_acc], writes=[b_acc])
        S.op("act", lambda e: e.activation(acc[:], acc[:], AF.Sqrt), reads=[b_acc], writes=[b_acc])

    def post_norm(blk):
        for tt in range(4):
            r0 = blk * TBK + tt * 128
            xt, bxt = x_p.get()
            S.dma("sp", xt[:], x_d[r0:r0 + 128, :], writes=[bxt])
            stt, bstt = st_p.get()
            S.op("dve", lambda e: e.memset(stt[:], 0.0), writes=[bstt])
            S.op("act", lambda e: e.activation(junk[:], outraw[:, tt, :], AF.Square, accum_out=stt[:, 0:1]), reads=[b_or[tt], bstt], writes=[b_junk, bstt])
            S.op("dve", lambda e: e.tensor_scalar(stt[:, 1:2], stt[:, 0:1], 1.0 / D, EPS, ALU.mult, ALU.add), reads=[bstt], writes=[bstt])
            S.op("dve", lambda e: e.reciprocal(stt[:, 2:3], stt[:, 1:2]), reads=[bstt], writes=[bstt])
            S.op("act", lambda e: e.activation(stt[:, 2:3], stt[:, 2:3], AF.Sqrt), reads=[bstt], writes=[bstt])
            S.op("dve", lambda e: e.scalar_tensor_tensor(outraw[:, tt, :], outraw[:, tt, :], stt[:, 2:3], pg[:], ALU.mult, ALU.mult),
                 reads=[b_or[tt], bstt, b_pg], writes=[b_or[tt]])
            S.op("dve", lambda e: e.tensor_tensor(xt[:], xt[:], outraw[:, tt, :], ALU.add), reads=[bxt, b_or[tt]], writes=[bxt])
            b = Buf(); outs_b.append(b)
            S.dma("sp", xo[r0:r0 + 128, :], xt[:], reads=[bxt], writes=[b])


    for blk in range(NT // TBK):
        ts = slice(blk * TBK, (blk + 1) * TBK)
        ld = lambda dram: dram[:, ts].rearrange("(k p) t -> p k t", p=128)
        S.dma("sp", ygall[:, 0:8, :], ld(yAg), writes=[b_yg[0]])
        S.dma("sp", ygall[:, 24:32, :], ld(yMg), writes=[b_yg[3]])
        ff, bff = in_p.get()
        S.dma("sp", ff[:], ld(ffc), writes=[bff])
        sg, bsg = sg_p.get()
        S.dma("sp", sg[:], ld(sgB), writes=[bsg])
        yc, byc = in_p.get()
        S.dma("sp", yc[:], ld(ycc), writes=[byc])
        sg2, bsg2 = sg_p.get()
        S.dma("sp", sg2[:], ld(sgC), writes=[bsg2])
        if blk > 0:
            post_norm(blk - 1)
        pend = None
        for j in range(8):
            pt, bpt = psum.get()
            for k in range(8):
                S.op("pe", lambda e, k=k: e.matmul(pt[:], lhsT=fw_bf[:, k, j * 128:(j + 1) * 128], rhs=ff[:, k, :], start=(k == 0), stop=(k == 7)),
                     reads=[b_fw, bff], writes=[bpt])
            S.op("act", lambda e: e.activation(yb[:, j, :], pt[:], AF.Identity, bias=vec_s[:, 0, j:j + 1]), reads=[bpt, b_vec], writes=[b_yb])
            sq, bsq = sq_p.get()
            S.op("act", lambda e: e.activation(sq[:], yb[:, j, :], AF.Square), reads=[b_yb], writes=[bsq])
            if pend is not None:
                colsum(pend[0][:], [pend[1]], pend[2] == 0)
            pend = (sq, bsq, j)
        colsum(pend[0][:], [pend[1]], False)
        rstd_acc()
        for j in range(8):
            sq, bsq = sq_p.get()
            S.op("dve", lambda e: e.scalar_tensor_tensor(sq[:], yb[:, j, :], vec_s[:, 1, j:j + 1], acc[:], ALU.mult, ALU.mult),
                 reads=[b_yb, b_vec, b_acc], writes=[bsq])
            S.op("dve", lambda e: e.tensor_tensor(ygall[:, 8 + j, :], sq[:], sg[:, j, :], ALU.mult), reads=[bsq, bsg], writes=[b_yg[1]])
        sg, bsg = sg2, bsg2
        for j in range(8):
            sq, bsq = sq_p.get()
            S.op("act", lambda e: e.activation(sq[:], yc[:, j, :], AF.Square), reads=[byc], writes=[bsq])
            colsum(sq[:], [bsq], j == 0)
        rstd_acc()
        for j in range(8):
            sq, bsq = sq_p.get()
            S.op("dve", lambda e: e.scalar_tensor_tensor(sq[:], yc[:, j, :], vec_s[:, 2, j:j + 1], acc[:], ALU.mult, ALU.mult),
                 reads=[byc, b_vec, b_acc], writes=[bsq])
            S.op("dve", lambda e: e.tensor_tensor(ygall[:, 16 + j, :], sq[:], sg[:, j, :], ALU.mult), reads=[bsq, bsg], writes=[b_yg[2]])
        for ng in range(4):
            pts = [psum.get() for _ in range(4)]
            for kq in range(4):
                wt, bwt = wt_p.get()
                S.dma("sp", wt[:], wsc[kq * 1024:(kq + 1) * 1024, ng * 512:(ng + 1) * 512].rearrange("(k p) n -> p k n", p=128),
                      reads=b_wsc[kq * 8:(kq + 1) * 8], writes=[bwt])
                for tt in range(4):
                    pt, bpt = pts[tt]
                    for k in range(8):
                        kk = kq * 8 + k
                        S.op("pe", lambda e, k=k, kk=kk, tt=tt, pt=pt: e.matmul(pt[:], lhsT=ygall[:, kk, tt * 128:(tt + 1) * 128], rhs=wt[:, k, :],
                                                                                 start=(kk == 0), stop=(kk == 31)),
                             reads=[b_yg[kq], bwt], writes=[bpt])
            for tt in range(4):
                pt, bpt = pts[tt]
                S.op("act", lambda e, tt=tt, pt=pt: e.copy(outraw[:, tt, ng * 512:(ng + 1) * 512], pt[:]), reads=[bpt], writes=[b_or[tt]])
    post_norm(NT // TBK - 1)

    S.finish(outs_b, "sp")
    return S
import numpy as np
D=2048; G=1024; L=8192; TB=1024; H=16; NB=2

def chunked(v, nch):
    return np.ascontiguousarray(v.reshape(nch, 128).T)

def dftG_const():
    k = np.arange(G)
    ang = 2*np.pi*((k[:,None]*k[None,:]) % G)/G
    sc = 1.0/np.sqrt(float(L)*G)
    return np.concatenate([np.cos(ang)*sc, -np.sin(ang)*sc], axis=1).astype(np.float32)

def l1_inputs(inp, l, core, x_cur):
    b = core // 4; j = core % 4
    xb = x_cur[b]
    xpad = np.concatenate([np.zeros((H, D), np.float32), xb, np.zeros((H, D), np.float32)], 0)
    xp = np.stack([xpad[j*2048 + blk*TB : j*2048 + blk*TB + TB + 2*H] for blk in range(NB)], 0)
    cw = np.ascontiguousarray(inp['conv_dw_w'][l].T.reshape(8, 128, 31).transpose(1, 0, 2))
    gn = inp['group_norm_g'][l]
    cvec = np.stack([chunked(inp['conv_dw_b'][l], 8), chunked(inp['conv_ln_g'][l], 8), chunked(inp['conv_ln_b'][l], 8),
                     chunked(inp['conv_pw_b'][l], 8), chunked(gn[0:G], 8)], axis=1)
    hw = np.concatenate([inp['hy_short_w'][l], inp['hy_short_b'][l][None]], 0)
    hyw = np.ascontiguousarray(hw.T.reshape(24, 128, 4).transpose(1, 0, 2))
    return {
        "xp": np.ascontiguousarray(xp), "w_in": inp['w_in'][l],
        "pre_g_bc": np.ascontiguousarray(np.broadcast_to(inp['pre_norm_g'][l], (128, D))),
        "conv_w": cw, "conv_vec": np.ascontiguousarray(cvec), "conv_pw_w": inp['conv_pw_w'][l],
        "hy_w": hyw, "mem": inp['mem'][b],
        "mem_g_bc": np.ascontiguousarray(np.broadcast_to(inp['mem_norm_g'], (128, D))),
        "mem_wk": inp['mem_wk'][l], "mem_wv": inp['mem_wv'][l],
        "gD": chunked(gn[3*G:4*G], 8), "dftG": dftG_const(),
    }

BF = ml_dtypes.bfloat16
NCH=128; CB=8; NBATCH=NCH//CB

def l2_consts():
    j = np.arange(128)
    ang = 2*np.pi*((j[:,None]*j[None,:]) % 128)/128.0
    C = np.cos(ang); Sn = np.sin(ang)
    j64 = np.arange(64)
    ang64 = 2*np.pi*((j64[:,None]*j64[None,:]) % 64)/64.0
    C64 = np.cos(ang64); S64 = np.sin(ang64)
    cb = np.zeros((128, CBF_W), np.float64)
    def put(name, m):
        o, w = CO[name]; assert m.shape[1] == w, (name, m.shape); cb[:m.shape[0], o:o+w] = m
    FA = np.concatenate([C, -Sn], 1)
    put("FA", FA); put("FAhi", FA[64:128]); put("C", C); put("S", Sn); put("nS", -Sn)
    put("IA1", np.concatenate([C, Sn], 1)); put("IA2", np.concatenate([-Sn, C], 1))
    put("IBc", C[:, :64]/16384.0); put("IBs", -Sn[:, :64]/16384.0)
    put("FN1", np.concatenate([C64, -S64], 1)); put("FN2", np.concatenate([S64, C64], 1))
    put("FAi", np.concatenate([Sn, C], 1)); put("IBsp", Sn[:, :64]/16384.0)
    cf = np.zeros((128, CF_W), np.float64)
    a16 = 2*np.pi*(j[:,None]*j[None,:])/16384.0
    a8 = 2*np.pi*(j[:,None]*j64[None,:])/8192.0
    def putf(name, m):
        o, w = CF[name]; cf[:, o:o+w] = m
    putf("T16r", np.cos(a16)); putf("T16i", -np.sin(a16)); putf("T16ci", np.sin(a16))
    putf("T8r", np.cos(a8)); putf("T8i", -np.sin(a8))
    return cb.astype(BF), cf.astype(np.float32)

def l2_inputs(inp, l, core, Zb, hyb):
    cs = slice(core*NCH, (core+1)*NCH)
    def rows(arrs, off):
        return np.ascontiguousarray(np.concatenate([a[off + core*NCH: off + (core+1)*NCH] for a in arrs], 0))
    pos = inp['positions'].astype(np.int32)
    posb = pos[(L - np.arange(L)) % L]
    pos_rep = np.stack([np.broadcast_to(pos, (32, L)), np.broadcast_to(posb, (32, L))], 0)
    pos_t = np.stack([pos.reshape(64, 128), posb.reshape(64, 128)], 0)
    bands = np.linspace(1e-4, 15, 16, dtype=np.float32)
    bsc = np.zeros((32, 2), np.float32)
    bsc[:, 0] = np.concatenate([bands, bands]) / np.float32(L)
    bsc[:16, 1] = 0.25
    fw1 = inp['hy_fw1'][l]
    mvec = np.stack([inp['hy_fb1'][l], inp['hy_freq1'][l], inp['hy_fb2'][l], inp['hy_freq2'][l]], 1)
    fw3 = inp['hy_fw3'][l].reshape(64, 2, 2, 1024)[:, :, :, cs]
    fw3c = fw3.reshape(64, 2, 2, NBATCH, CB).transpose(0, 2, 3, 1, 4)
    dec = inp['hy_decay'][l][:, :, cs]
    decc = dec.reshape(2, 2, NBATCH, CB).transpose(1, 2, 0, 3)
    cbf, cf = l2_consts()
    return {
        "zr": rows(Zb, 0), "zi": rows(Zb, 1024),
        "hv": rows(hyb, 0), "hx1": rows(hyb, 1024), "hx2": rows(hyb, 2048),
        "pos_rep": np.ascontiguousarray(pos_rep), "pos_t": np.ascontiguousarray(pos_t), "bsc": bsc,
        "fw1a": np.ascontiguousarray(fw1[0:1]), "fw1b": np.ascontiguousarray(fw1[1:33]), "mvec": np.ascontiguousarray(mvec),
        "fw2": inp['hy_fw2'][l], "fw3c": np.ascontiguousarray(fw3c),
        "dec_rep": np.ascontiguousarray(np.broadcast_to(decc, (64,) + decc.shape)),
        "skip_rep": np.ascontiguousarray(np.broadcast_to(inp['hy_skip'][l][:, cs], (64, 2, NCH))),
        "cbf": cbf, "cf": cf,
    }


def l3_inputs(inp, l, core, x_cur, ffb, ycb, l1o):
    b = core // 4; j = core % 4
    ts = slice(j*2048, (j+1)*2048)
    gn = inp['group_norm_g'][l]
    vec3 = np.stack([chunked(inp['fnet_b'][l], 8), chunked(gn[1024:2048], 8), chunked(gn[2048:3072], 8)], axis=1)
    return {
        "ffc": np.ascontiguousarray(ffb[:, ts]), "ycc": np.ascontiguousarray(ycb[:, ts]),
        "sgB": l1o["sgB"], "sgC": l1o["sgC"], "yAg": l1o["yAg"], "yMg": l1o["yMg"],
        "x": np.ascontiguousarray(x_cur[b][ts]), "fnet_w": inp['fnet_w'][l], "vec3": np.ascontiguousarray(vec3),
        "w_out": inp['w_out'][l],
        "post_g_bc": np.ascontiguousarray(np.broadcast_to(inp['post_norm_g'][l], (128, 2048))),
    }

_PROGS = {}


def _prog(name, builder):
    if name not in _PROGS:
        nc = bass.Bass("TRN2", target_bir_lowering=False)
        builder(nc)
        _PROGS[name] = nc
    return _PROGS[name]


def kernel(**inputs):
    inp = {k: np.asarray(v) for k, v in inputs.items()}
    x_cur = np.ascontiguousarray(inp['x'], dtype=np.float32)
    cores = list(range(8))
    nc1 = _prog("L1", build_L1); nc2 = _prog("L2", build_L2); nc3 = _prog("L3", build_L3)
    for l in range(2):
        r1 = run_bass_kernel_spmd(nc1, [l1_inputs(inp, l, c, x_cur) for c in cores], core_ids=cores).results
        Zb = [np.concatenate([np.asarray(r1[4 * b + j]["Z"]) for j in range(4)], axis=1) for b in range(2)]
        hyb = [np.concatenate([np.asarray(r1[4 * b + j]["hyc"]) for j in range(4)], axis=1) for b in range(2)]
        r2 = run_bass_kernel_spmd(nc2, [l2_inputs(inp, l, c, Zb, hyb) for c in cores], core_ids=cores).results
        ffb = [np.concatenate([np.asarray(r2[c]["ff"])[b * NCH:(b + 1) * NCH] for c in cores], axis=0) for b in range(2)]
        ycb = [np.concatenate([np.asarray(r2[c]["yc"])[b * NCH:(b + 1) * NCH] for c in cores], axis=0) for b in range(2)]
        l1o = [{k: np.asarray(r1[c][k]) for k in ("sgB", "sgC", "yAg", "yMg")} for c in cores]
        r3 = run_bass_kernel_spmd(nc3, [l3_inputs(inp, l, c, x_cur, ffb[c // 4], ycb[c // 4], l1o[c]) for c in cores],
                                  core_ids=cores).results
        x_cur = np.stack([np.concatenate([np.asarray(r3[4 * b + j]["xo"]) for j in range(4)], axis=0) for b in range(2)], axis=0)
    return np.ascontiguousarray(x_cur, dtype=np.float32)
```
